# Optimizing a Trainium2 kernel written in Bass

```python
import math
import jax, jax.numpy as jnp
from jax import lax
import numpy as np

D_MODEL = 1024
BATCH = 8
SEQ = 2048
DEPTH = 4
DEC_BATCH = 128
DEC_SEQ = 4
PAST_LEN = 16384
PAGE_SIZE = 128

D_SSM = D_MODEL // 2
D_CONV = D_MODEL - D_SSM
SSM_GROUP = 16
N_SSM_GROUPS = D_SSM // SSM_GROUP
SSM_STATE = 64
CONV_WIDTH = 31
CONV_BUF = CONV_WIDTH - 1
D_FF = 4 * D_MODEL
D_IN = D_SSM + 2 * D_CONV
EPS = 1e-6
DT_MIN = 1e-3
DT_MAX = 1e-1

kernel_name = "hymba_s5_conformer_conv_step"


def rms_norm(x, g):
    xf = x.astype(jnp.float32)
    y = xf * lax.rsqrt(jnp.mean(xf * xf, axis=-1, keepdims=True) + EPS)
    return (y * g.astype(jnp.float32)).astype(x.dtype)


def layer_norm(x, g, b):
    xf = x.astype(jnp.float32)
    mu = jnp.mean(xf, axis=-1, keepdims=True)
    xc = xf - mu
    y = xc * lax.rsqrt(jnp.mean(xc * xc, axis=-1, keepdims=True) + EPS)
    return (y * g.astype(jnp.float32) + b.astype(jnp.float32)).astype(x.dtype)


def _lin_rec(e1, e2):
    a1, b1 = e1
    a2, b2 = e2
    return a2 * a1, a2 * b1 + b2


def s5_mixer(u, h0, a_re, a_im, log_dt, b_re, b_im, c_re, c_im, d_skip, w_glu, b_glu):
    n, l, _ = u.shape
    f32 = jnp.float32
    lam = lax.complex(a_re.astype(f32), a_im.astype(f32))
    dt = jnp.exp(log_dt.astype(f32))[:, None]
    lam_bar = jnp.exp(lam * dt)
    b = lax.complex(b_re.astype(f32), b_im.astype(f32))
    b_bar = ((lam_bar - 1.0) / lam)[..., None] * b
    uf = u.astype(f32)
    ug = uf.reshape(n, l, N_SSM_GROUPS, SSM_GROUP).astype(jnp.complex64)
    bu = jnp.einsum('gpc,nlgc->nlgp', b_bar, ug)
    a = jnp.broadcast_to(lam_bar, bu.shape)
    a_cum, h = lax.associative_scan(_lin_rec, (a, bu), axis=1)
    if h0 is not None:
        h = h + a_cum * h0[:, None]
    c = lax.complex(c_re.astype(f32), c_im.astype(f32))
    y = jnp.einsum('gcp,nlgp->nlgc', c, h).real.reshape(n, l, D_SSM)
    y = y + d_skip.astype(f32) * uf
    y = jax.nn.gelu(y)
    y = y * jax.nn.sigmoid(y @ w_glu.astype(f32) + b_glu.astype(f32))
    return y.astype(u.dtype), h[:, -1]


def conv_mixer(v, gate, buf, w_dw, b_dw, ln_g, ln_b):
    z = v * jax.nn.sigmoid(gate)
    zc = jnp.concatenate([buf.astype(z.dtype), z], axis=1)
    y = lax.conv_general_dilated(
        zc, w_dw[:, None, :].astype(z.dtype), window_strides=(1,), padding='VALID',
        dimension_numbers=('NWC', 'WIO', 'NWC'), feature_group_count=D_CONV)
    y = y + b_dw.astype(y.dtype)
    y = jax.nn.silu(layer_norm(y, ln_g, ln_b))
    return y, zc[:, -CONV_BUF:]


def setup_inputs(seed: int = 0) -> dict:
    key = jax.random.key(seed)
    ks = jax.random.split(key, 32)
    f32 = jnp.float32
    nrm = lambda k, s, sc: jax.random.normal(k, s, f32) * sc
    n_idx = jnp.arange(SSM_STATE, dtype=f32)
    a_re = -0.5 * jnp.exp(nrm(ks[5], (DEPTH, N_SSM_GROUPS, SSM_STATE), 0.01))
    a_im = math.pi * n_idx[None, None, :] + nrm(ks[6], (DEPTH, N_SSM_GROUPS, SSM_STATE), 0.01)
    log_dt = jax.random.uniform(ks[7], (DEPTH, N_SSM_GROUPS), f32,
                                math.log(DT_MIN), math.log(DT_MAX))
    return {
        "x_prompt": nrm(ks[0], (BATCH, SEQ, D_MODEL), 1.0),
        "x_sample": nrm(ks[1], (DEC_BATCH, DEC_SEQ, D_MODEL), 1.0),
        "state_ssm_re": nrm(ks[2], (DEPTH, DEC_BATCH, N_SSM_GROUPS, SSM_STATE), 0.1),
        "state_ssm_im": nrm(ks[3], (DEPTH, DEC_BATCH, N_SSM_GROUPS, SSM_STATE), 0.1),
        "state_conv": nrm(ks[4], (DEPTH, DEC_BATCH, CONV_BUF, D_CONV), 0.5),
        "norm_mix_g": 1.0 + nrm(ks[8], (DEPTH, D_MODEL), 0.01),
        "w_in": nrm(ks[9], (DEPTH, D_MODEL, D_IN), D_MODEL ** -0.5),
        "b_in": nrm(ks[10], (DEPTH, D_IN), 0.01),
        "ssm_a_re": a_re,
        "ssm_a_im": a_im,
        "ssm_log_dt": log_dt,
        "ssm_b_re": nrm(ks[11], (DEPTH, N_SSM_GROUPS, SSM_STATE, SSM_GROUP), (2 * SSM_GROUP) ** -0.5),
        "ssm_b_im": nrm(ks[12], (DEPTH, N_SSM_GROUPS, SSM_STATE, SSM_GROUP), (2 * SSM_GROUP) ** -0.5),
        "ssm_c_re": nrm(ks[13], (DEPTH, N_SSM_GROUPS, SSM_GROUP, SSM_STATE), SSM_STATE ** -0.5),
        "ssm_c_im": nrm(ks[14], (DEPTH, N_SSM_GROUPS, SSM_GROUP, SSM_STATE), SSM_STATE ** -0.5),
        "ssm_d": nrm(ks[15], (DEPTH, D_SSM), 1.0),
        "w_glu": nrm(ks[16], (DEPTH, D_SSM, D_SSM), D_SSM ** -0.5),
        "b_glu": nrm(ks[17], (DEPTH, D_SSM), 0.01),
        "conv_w": nrm(ks[18], (DEPTH, CONV_WIDTH, D_CONV), CONV_WIDTH ** -0.5),
        "conv_b": nrm(ks[19], (DEPTH, D_CONV), 0.01),
        "conv_ln_g": 1.0 + nrm(ks[20], (DEPTH, D_CONV), 0.01),
        "conv_ln_b": nrm(ks[21], (DEPTH, D_CONV), 0.01),
        "w_out": nrm(ks[22], (DEPTH, D_MODEL, D_MODEL), D_MODEL ** -0.5),
        "norm_mlp_g": 1.0 + nrm(ks[23], (DEPTH, D_MODEL), 0.01),
        "w_up": nrm(ks[24], (DEPTH, D_MODEL, D_FF), D_MODEL ** -0.5),
        "w_down": nrm(ks[25], (DEPTH, D_FF, D_MODEL), D_FF ** -0.5),
        "norm_f_g": 1.0 + nrm(ks[26], (D_MODEL,), 0.01),
    }


def reference(x_prompt, x_sample, state_ssm_re, state_ssm_im, state_conv,
              norm_mix_g, w_in, b_in, ssm_a_re, ssm_a_im, ssm_log_dt,
              ssm_b_re, ssm_b_im, ssm_c_re, ssm_c_im, ssm_d, w_glu, b_glu,
              conv_w, conv_b, conv_ln_g, conv_ln_b, w_out,
              norm_mlp_g, w_up, w_down, norm_f_g):

    def run_trunk(x, h0_re, h0_im, buf0):
        n = x.shape[0]
        h_re_out, h_im_out, buf_out = [], [], []
        for l in range(DEPTH):
            hn = rms_norm(x, norm_mix_g[l])
            proj = hn @ w_in[l] + b_in[l]
            u = proj[..., :D_SSM]
            v = proj[..., D_SSM:D_SSM + D_CONV]
            gate = proj[..., D_SSM + D_CONV:]
            if h0_re is None:
                h0 = None
                buf = jnp.zeros((n, CONV_BUF, D_CONV), x.dtype)
            else:
                h0 = lax.complex(h0_re[l].astype(jnp.float32), h0_im[l].astype(jnp.float32))
                buf = buf0[l]
            y_ssm, h_last = s5_mixer(u, h0, ssm_a_re[l], ssm_a_im[l], ssm_log_dt[l],
                                     ssm_b_re[l], ssm_b_im[l], ssm_c_re[l], ssm_c_im[l],
                                     ssm_d[l], w_glu[l], b_glu[l])
            y_conv, new_buf = conv_mixer(v, gate, buf, conv_w[l], conv_b[l],
                                         conv_ln_g[l], conv_ln_b[l])
            x = x + jnp.concatenate([y_ssm, y_conv.astype(y_ssm.dtype)], axis=-1) @ w_out[l]
            hn = rms_norm(x, norm_mlp_g[l])
            x = x + jnp.square(jax.nn.relu(hn @ w_up[l])) @ w_down[l]
            h_re_out.append(h_last.real)
            h_im_out.append(h_last.imag)
            buf_out.append(new_buf)
        y = rms_norm(x, norm_f_g)
        return y, jnp.stack(h_re_out), jnp.stack(h_im_out), jnp.stack(buf_out)

    y_prompt, ssm_re_p, ssm_im_p, conv_p = run_trunk(x_prompt, None, None, None)
    y_sample, ssm_re_s, ssm_im_s, conv_s = run_trunk(x_sample, state_ssm_re, state_ssm_im, state_conv)
    return (y_prompt, y_sample, ssm_re_p, ssm_im_p, conv_p, ssm_re_s, ssm_im_s, conv_s)
```

```python
import math
from contextlib import ExitStack
import numpy as np
import concourse.bass as bass
import concourse.mybir as mybir
from concourse.bass_utils import run_bass_kernel_spmd

F32 = mybir.dt.float32
BF16 = mybir.dt.bfloat16
AF = mybir.ActivationFunctionType
ALU = mybir.AluOpType

NCORES = 8
D = 1024
DEPTH = 4
NP_TOK = 2048
NSEQ = 16
NS_TOK = 64
NTOK = NP_TOK + NS_TOK
TT = [(0, 512), (512, 512), (1024, 512), (1536, 512), (2048, 64)]
ZW = 30 + NP_TOK + NSEQ * 34
ZS0 = 30 + NP_TOK
EPS = 1e-6
NVL = 172
NV = NVL * DEPTH + 8
MAGIC = 12582912.0
TWO_PI = 2.0 * math.pi
T0C = 8
import os
DBG = os.environ.get('DBG_SKIP', '')


class _Op:
    __slots__ = ("eng", "fn", "deps", "semkey", "inc", "signaled", "count", "idx")

    def __init__(self, eng, fn, semkey, inc, always):
        self.eng = eng
        self.fn = fn
        self.deps = set()
        self.semkey = semkey
        self.inc = inc
        self.signaled = always
        self.count = 0


class Prog:
    ENGS = ("pe", "act", "dve", "pool", "sp")

    def __init__(self):
        self.ops = []
        self.last_write = {}
        self.readers = {}
        self.dma_hist = {}
        self.capture = None

    def _add(self, op, reads, writes, after):
        idx = len(self.ops)
        op.idx = idx
        deps = set(after)
        for r in reads:
            lw = self.last_write.get(r)
            if lw is not None:
                deps.add(lw)
        for w in writes:
            lw = self.last_write.get(w)
            if lw is not None:
                deps.add(lw)
            for rd in self.readers.get(w, ()):
                deps.add(rd)
        deps.discard(idx)
        op.deps = deps
        self.ops.append(op)
        for r in reads:
            self.readers.setdefault(r, []).append(idx)
        for w in writes:
            self.last_write[w] = idx
            self.readers[w] = []
        return idx

    def op(self, eng, fn, reads=(), writes=(), after=()):
        if self.capture is not None:
            self.capture.append((eng, fn, tuple(reads), tuple(writes), tuple(after)))
            return None
        return self._add(_Op(eng, fn, eng, 1, False), reads, writes, after)

    def flush(self, queue, n):
        for _ in range(min(n, len(queue))):
            eng, fn, reads, writes, after = queue.pop(0)
            self._add(_Op(eng, fn, eng, 1, False), reads, writes, after)

    DMA_SLOTS = {"w": 2, "io": 8}

    def dma(self, stream, fn, reads=(), writes=(), after=()):
        hist = self.dma_hist.setdefault(stream, [])
        k = self.DMA_SLOTS[stream]
        n = len(hist)
        after = list(after)
        if n >= k:
            after.append(hist[n - k])
        idx = self._add(_Op("sp", fn, "dma:%s%d" % (stream, n % k), 16, True), reads, writes, after)
        hist.append(idx)
        return idx

    def _skip(self, p, eng):
        return p.eng == eng and (not p.semkey.startswith("dma:")) and eng == "pe"

    def emit(self, block, sems):
        ops = self.ops
        for o in ops:
            for d in o.deps:
                p = ops[d]
                if self._skip(p, o.eng):
                    continue
                p.signaled = True
        counts = {}
        for o in ops:
            if o.signaled:
                counts[o.semkey] = counts.get(o.semkey, 0) + o.inc
                o.count = counts[o.semkey]
        per_eng = {e: [] for e in self.ENGS}
        for o in ops:
            per_eng[o.eng].append(o)

        def run_engine(ename, eng):
            waited = {}
            for o in per_eng[ename]:
                need = {}
                for d in o.deps:
                    p = ops[d]
                    if not p.signaled or self._skip(p, ename):
                        continue
                    if p.count > need.get(p.semkey, 0):
                        need[p.semkey] = p.count
                for k, v in need.items():
                    if waited.get(k, 0) < v:
                        eng.wait_ge(sems[k], v)
                        waited[k] = v
                ins = o.fn(eng)
                if o.signaled:
                    ins.then_inc(sems[o.semkey], o.inc)
            return waited

        @block.tensor
        def _(e):
            run_engine("pe", e)

        @block.scalar
        def _(e):
            run_engine("act", e)

        @block.vector
        def _(e):
            run_engine("dve", e)

        @block.gpsimd
        def _(e):
            run_engine("pool", e)

        @block.sync
        def _(e):
            w = run_engine("sp", e)
            for k, v in counts.items():
                if k.startswith("dma:") and w.get(k, 0) < v:
                    e.wait_ge(sems[k], v)


def build(nl=DEPTH):
    nc = bass.Bass("TRN2", target_bir_lowering=False)

    def din(name, shape):
        return nc.dram_tensor(name, shape, F32, kind="ExternalInput").ap()

    def dout(name, shape):
        return nc.dram_tensor(name, shape, F32, kind="ExternalOutput").ap()

    xT = din("xT", [8, 128, NTOK])
    w_in = din("w_in", [DEPTH, D, 1536])
    w_glu = din("w_glu", [DEPTH, 512, 512])
    w_out = din("w_out", [DEPTH, D, D])
    w_up = din("w_up", [DEPTH, D, 4096])
    w_down = din("w_down", [DEPTH, 4096, D])
    s5w = din("s5w", [DEPTH, 4, 128, 2048])
    s5wT = din("s5wT", [DEPTH, 4, 128, 1024])
    vec_d = din("vec", [128, NV])
    s5p_d = din("s5p", [128, DEPTH * 48])
    h0_d = din("h0", [DEPTH, 128, 2, 16, 16])
    zbuf_d = din("zbuf", [DEPTH, 128, 4 * 16 * 30])
    sconv_d = din("sconv", [DEPTH, NSEQ, 30, 512])
    ident_d = din("ident", [128, 128])

    yT = dout("yT", [8, 128, NTOK])
    sso_d = dout("sso", [DEPTH, 128, 2, 16, 17])
    cvp_d = dout("cvp", [DEPTH, 128, 4 * 30])
    cvs_d = dout("cvs", [DEPTH, 128, 4 * 64])
    cvc_d = dout("cvc", [DEPTH, NSEQ, 26, 512])

    P = Prog()
    with ExitStack() as es:
        def sb(name, shape, dt):
            return es.enter_context(nc.sbuf_tensor(name, shape, dt))

        X = sb("X", [128, 8, NTOK], F32)
        A = sb("A", [128, 8, NTOK], BF16)
        U = sb("U", [128, 4, NTOK], BF16)
        Z = sb("Z", [128, 4, ZW], BF16)
        S1 = sb("S1", [128, NTOK], F32)
        S2 = sb("S2", [128, NTOK], F32)
        STG = sb("STG", [128, 2, 2048], F32)
        WBF = sb("WBF", [128, 3, 2048], BF16)
        TMPF = sb("TMPF", [128, 2, 512], F32)
        VEC2 = sb("VEC2", [128, 2, NVL], F32)
        GF = sb("GF", [128, 8], F32)
        S5P = sb("S5P", [128, DEPTH * 48], F32)
        TB = sb("TB", [128, 77, 16], F32)
        PW = sb("PW", [128, 2, 11, 16], F32)
        H0 = sb("H0", [128, 2, 4, 16], F32)
        SSO = sb("SSO", [128, 2, 4, 17], F32)
        CVP = sb("CVP", [128, 4, 30], F32)
        CVS = sb("CVS", [128, 4, 64], F32)
        SACC = sb("SACC", [128, 2, NP_TOK // T0C], F32)
        SBS = sb("SBS", [128, 2, NS_TOK], F32)
        TAP = sb("TAP", [128, T0C, 128], BF16)
        RT = sb("RT", [128, NP_TOK // T0C], F32)
        HP = sb("HP", [128, 2, NP_TOK // T0C + 1], BF16)
        H0B = sb("H0B", [128, 2, NSEQ], BF16)
        LC8 = sb("LC8", [128, 2, T0C, 128], BF16)
        CAR = sb("CAR", [128, 4], F32)
        IDB = sb("IDB", [128, 128], BF16)
        NIDB = sb("NIDB", [128, 128], BF16)
        ONB = sb("ONB", [128, 128], BF16)
        DG = sb("DG", [128, 4, 128], BF16)
        PS = [es.enter_context(nc.psum_tensor(f"ps{i}", [128, 512], F32)) for i in range(8)]

        sem_names = ["pe", "act", "dve", "pool"] + ["dma:w%d" % i for i in range(2)] + ["dma:io%d" % i for i in range(8)]
        sems = {k: es.enter_context(nc.semaphore("s_" + k.replace(":", "_"))) for k in sem_names}
        block = es.enter_context(nc.Block())

        def tsl(tt):
            t0, n = TT[tt]
            return slice(t0, t0 + n)

        def zsl(tt):
            t0, n = TT[tt]
            return slice(30 + t0, 30 + t0 + n)

        def v3(ap, inner):
            return ap.rearrange("p (s t) -> p s t", t=inner)

        def Zs(j):
            return Z[:, j, ZS0:ZW].rearrange("p (s c) -> p s c", c=34)

        def Zrow(j):
            return [f"Z{j}_{t}" for t in range(5)]

        bank_ctr = [0]

        def next_bank():
            b = bank_ctr[0] % 8
            bank_ctr[0] += 1
            return b

        def mm(ps_ap, lhsT, rhs, start, stop, reads, bank):
            wres = bank if isinstance(bank, str) else f"ps{bank}"
            P.op("pe", lambda e: e.matmul(ps_ap, lhsT=lhsT, rhs=rhs, start=start, stop=stop),
                 reads=reads, writes=[wres])

        def mmt(ps_ap, lhsT, rhs, start, stop, reads, bank, col0):
            P.op("pe", lambda e: e.matmul(ps_ap, lhsT=lhsT, rhs=rhs, start=start, stop=stop, tile_position=(0, col0)),
                 reads=reads, writes=[f"ps{bank}"])

        def act(out, in_, func, reads, writes, bias=None, scale=None):
            kw = {}
            if bias is not None:
                kw["bias"] = bias
            if scale is not None:
                kw["scale"] = scale
            P.op("act", lambda e: e.activation(out=out, in_=in_, func=func, **kw), reads=reads, writes=writes)

        def tt_op(eng, out, in0, in1, op, reads, writes):
            P.op(eng, lambda e: e.tensor_tensor(out=out, in0=in0, in1=in1, op=op), reads=reads, writes=writes)

        def ts_op(eng, out, in0, s1, op0, reads, writes, s2=None, op1=None):
            if op1 is None:
                P.op(eng, lambda e: e.tensor_scalar(out=out, in0=in0, scalar1=s1, scalar2=None, op0=op0),
                     reads=reads, writes=writes)
            else:
                P.op(eng, lambda e: e.tensor_scalar(out=out, in0=in0, scalar1=s1, scalar2=s2, op0=op0, op1=op1),
                     reads=reads, writes=writes)

        def stt(eng, out, in0, scalar, in1, op0, op1, reads, writes):
            P.op(eng, lambda e: e.scalar_tensor_tensor(out=out, in0=in0, scalar=scalar, in1=in1, op0=op0, op1=op1),
                 reads=reads, writes=writes)

        def copy(eng, out, in_, reads, writes):
            P.op(eng, lambda e: e.tensor_copy(out=out, in_=in_), reads=reads, writes=writes)

        def dma_io(out, in_, reads, writes):
            P.dma("io", lambda e: e.dma_start(out=out, in_=in_), reads=reads, writes=writes)

        slabs = []

        def plan_slabs():
            for l in range(nl):
                for s in range(6):
                    slabs.append(w_in[l, :, s * 256:(s + 1) * 256].rearrange("(k p) c -> p k c", p=128))
                for j in range(4):
                    slabs.append(s5w[l, j, :, :])
                    slabs.append(s5wT[l, j, :, :])
                slabs.append(w_glu[l, :, :].rearrange("(k p) c -> p k c", p=128))
                for s in range(4):
                    slabs.append(w_out[l, :, s * 256:(s + 1) * 256].rearrange("(k p) c -> p k c", p=128))
                for fb in range(4):
                    for s in range(4):
                        c0 = fb * 1024 + s * 256
                        slabs.append(w_up[l, :, c0:c0 + 256].rearrange("(k p) c -> p k c", p=128))
                    for s in range(4):
                        slabs.append(w_down[l, fb * 1024:(fb + 1) * 1024, s * 256:(s + 1) * 256]
                                     .rearrange("(k p) c -> p k c", p=128))

        plan_slabs()
        ws = {"issued": 0, "got": 0}

        def ws_issue():
            i = ws["issued"]
            s, b = i % 2, i % 3
            src = slabs[i]
            if len(src.shape) == 3:
                nel = src.shape[1] * src.shape[2]
                dst = STG[:, s, 0:nel].rearrange("p (k c) -> p k c", c=src.shape[2])
            else:
                nel = src.shape[1]
                dst = STG[:, s, 0:nel]
            P.dma("w", lambda e: e.dma_start(out=dst, in_=src), writes=[f"stg{s}"])
            P.op("pool", lambda e: e.tensor_copy(out=WBF[:, b, 0:nel], in_=STG[:, s, 0:nel]),
                 reads=[f"stg{s}"], writes=[f"wbf{b}"])
            ws["issued"] += 1

        def ws_get(keep=0):
            while ws["issued"] < min(len(slabs), ws["got"] - keep + 3):
                ws_issue()
            i = ws["got"]
            ws["got"] += 1
            return i % 3

        def wv(b, ncols):
            return WBF[:, b, :].rearrange("p (k c) -> p k c", c=ncols)

        for kt in range(8):
            dma_io(X[:, kt, :], xT[kt, :, :], [], [f"X{kt}_{t}" for t in range(5)])
        dma_io(GF[:, :], vec_d[:, NVL * DEPTH:NVL * DEPTH + 8], [], ["GF"])
        dma_io(S5P[:, :], s5p_d[:, :], [], ["S5P"])
        dma_io(TMPF[:, 0, 0:128], ident_d[:, :], [], ["TMPF0"])
        P.dma("io", lambda e: e.dma_start(out=cvc_d[0:nl, :, :, :], in_=sconv_d[0:nl, :, 4:30, :]))
        copy("pool", IDB[:, :], TMPF[:, 0, 0:128], ["TMPF0"], ["IDB"])
        ts_op("pool", NIDB[:, :], TMPF[:, 0, 0:128], -1.0, ALU.mult, ["TMPF0"], ["NIDB"])
        P.op("pool", lambda e: e.memset(ONB[:, :], 1.0), writes=["ONB"])
        P.op("pool", lambda e: e.memset(HP[:, :, 0:1], 0.0), writes=["HP"])
        P.op("pool", lambda e: e.memset(LC8[:, :, :, :], 0.0), writes=["LC8"])
        for j in range(4):
            P.op("pool", lambda e, j=j: e.memset(Z[:, j, 0:30], 0.0), writes=[f"Zpad{j}"])

        def rmsnorm(gfn, gres, to_x):
            for kt in range(8):
                for tt in range(5):
                    sq = U[:, kt % 2, tsl(tt)]
                    act(sq, X[:, kt, tsl(tt)], AF.Square, [f"X{kt}_{tt}"], [f"U{kt % 2}_{tt}"])
                    n = TT[tt][1]
                    mm(PS[tt][:, 0:n], ONB[:, :], sq, kt == 0, kt == 7, ["ONB", f"U{kt % 2}_{tt}"], tt)
            for tt in range(5):
                n = TT[tt][1]
                act(S1[:, tsl(tt)], PS[tt][:, 0:n], AF.Ln, [f"ps{tt}"], [f"S1_{tt}"], bias=EPS, scale=1.0 / D)
                act(S1[:, tsl(tt)], S1[:, tsl(tt)], AF.Exp, [f"S1_{tt}"], [f"S1_{tt}"], scale=-0.5)
            for kt in range(8):
                for tt in range(5):
                    g = gfn(kt)
                    if to_x:
                        stt("dve", X[:, kt, tsl(tt)], X[:, kt, tsl(tt)], g, S1[:, tsl(tt)], ALU.mult, ALU.mult,
                            [f"X{kt}_{tt}", f"S1_{tt}", gres], [f"X{kt}_{tt}"])
                    else:
                        stt("dve", A[:, kt, tsl(tt)], X[:, kt, tsl(tt)], g, S1[:, tsl(tt)], ALU.mult, ALU.mult,
                            [f"X{kt}_{tt}", f"S1_{tt}", gres], [f"A{kt}_{tt}"])

        def dense_group(lhs_fn, rhs_fn, nk, tt, reads):
            b = next_bank()
            n = TT[tt][1]
            for kt in range(nk):
                mm(PS[b][:, 0:n], lhs_fn(kt), rhs_fn(kt, tt), kt == 0, kt == nk - 1, reads(kt, tt), b)
            return b, PS[b][:, 0:n]

        def tb(i):
            return TB[:, i, :]

        (DT, RHO, TH, R1, Y, K, FR, SIN1, COS1, QRE, QIM, T0_, T1_, RT0) = range(14)
        YC, KC, FRC = Y, K, FR
        NRE, DEN, INV = Y, K, FR
        ER = lambda n: 14 + n
        EI = lambda n: 23 + n
        NEI = lambda n: 32 + n
        QR = lambda n: 41 + n
        QI = lambda n: 49 + n
        NQI = lambda n: 57 + n
        RQR, RQI, RNQI = 65, 69, 73
        E1R, E1I = ER(1), EI(1)

        def tbo(eng, kind, out_i, *a):
            r = ["S5P", "TB"]
            w = ["TB"]
            if kind == "tt":
                tt_op(eng, tb(out_i), a[0], a[1], a[2], r, w)
            elif kind == "ts":
                ts_op(eng, tb(out_i), a[0], a[1], a[2], r, w)

        def cmul(out_r, out_i, ar, ai, br, bi):
            tbo("dve", "tt", T0_, tb(ar), tb(br), ALU.mult)
            tbo("dve", "tt", T1_, tb(ai), tb(bi), ALU.mult)
            tbo("dve", "tt", out_r, tb(T0_), tb(T1_), ALU.subtract)
            tbo("dve", "tt", T0_, tb(ar), tb(bi), ALU.mult)
            tbo("dve", "tt", T1_, tb(ai), tb(br), ALU.mult)
            tbo("dve", "tt", out_i, tb(T0_), tb(T1_), ALU.add)

        def s5_tables(l):
            sp0 = l * 48
            ARE, AIM, LDT = S5P[:, sp0:sp0 + 16], S5P[:, sp0 + 16:sp0 + 32], S5P[:, sp0 + 32:sp0 + 48]
            act(tb(DT), LDT, AF.Exp, ["S5P"], ["TB"])
            tbo("dve", "tt", RHO, tb(DT), ARE, ALU.mult)
            tbo("dve", "tt", TH, tb(DT), AIM, ALU.mult)
            act(tb(R1), tb(RHO), AF.Exp, ["TB"], ["TB"])
            act(tb(RT0), tb(RHO), AF.Exp, ["TB"], ["TB"], scale=float(T0C))
            tbo("dve", "ts", Y, tb(TH), 1.0 / TWO_PI, ALU.mult)
            tbo("dve", "ts", K, tb(Y), MAGIC, ALU.add)
            tbo("dve", "ts", K, tb(K), MAGIC, ALU.subtract)
            tbo("dve", "tt", FR, tb(Y), tb(K), ALU.subtract)
            act(tb(SIN1), tb(FR), AF.Sin, ["TB"], ["TB"], scale=TWO_PI * (1.0 - 1e-6))
            tbo("dve", "ts", YC, tb(Y), 0.25, ALU.add)
            tbo("dve", "ts", KC, tb(YC), MAGIC, ALU.add)
            tbo("dve", "ts", KC, tb(KC), MAGIC, ALU.subtract)
            tbo("dve", "tt", FRC, tb(YC), tb(KC), ALU.subtract)
            act(tb(COS1), tb(FRC), AF.Sin, ["TB"], ["TB"], scale=TWO_PI * (1.0 - 1e-6))
            tbo("dve", "tt", E1R, tb(R1), tb(COS1), ALU.mult)
            tbo("dve", "tt", E1I, tb(R1), tb(SIN1), ALU.mult)
            tbo("dve", "ts", NRE, tb(E1R), -1.0, ALU.add)
            tbo("dve", "tt", T0_, ARE, ARE, ALU.mult)
            tbo("dve", "tt", T1_, AIM, AIM, ALU.mult)
            tbo("dve", "tt", DEN, tb(T0_), tb(T1_), ALU.add)
            P.op("dve", lambda e: e.reciprocal(out=tb(INV), in_=tb(DEN)), reads=["TB"], writes=["TB"])
            tbo("dve", "tt", T0_, tb(NRE), ARE, ALU.mult)
            tbo("dve", "tt", T1_, tb(E1I), AIM, ALU.mult)
            tbo("dve", "tt", T0_, tb(T0_), tb(T1_), ALU.add)
            tbo("dve", "tt", QRE, tb(T0_), tb(INV), ALU.mult)
            tbo("dve", "tt", T0_, tb(E1I), ARE, ALU.mult)
            tbo("dve", "tt", T1_, tb(NRE), AIM, ALU.mult)
            tbo("dve", "tt", T0_, tb(T0_), tb(T1_), ALU.subtract)
            tbo("dve", "tt", QIM, tb(T0_), tb(INV), ALU.mult)
            P.op("dve", lambda e: e.memset(tb(ER(0)), 1.0), reads=["TB"], writes=["TB"])
            P.op("dve", lambda e: e.memset(tb(EI(0)), 0.0), reads=["TB"], writes=["TB"])
            for n in range(1, T0C):
                cmul(ER(n + 1), EI(n + 1), ER(n), EI(n), E1R, E1I)
            for n in range(T0C):
                cmul(QR(n), QI(n), ER(n), EI(n), QRE, QIM)
                tbo("dve", "ts", NQI(n), tb(QI(n)), -1.0, ALU.mult)
            for n in range(T0C + 1):
                tbo("dve", "ts", NEI(n), tb(EI(n)), -1.0, ALU.mult)
            for t_ in range(4):
                copy("dve", tb(RQR + t_), tb(QR(3 - t_)), ["TB"], ["TB"])
                copy("dve", tb(RQI + t_), tb(QI(3 - t_)), ["TB"], ["TB"])
                copy("dve", tb(RNQI + t_), tb(NQI(3 - t_)), ["TB"], ["TB"])
            copy("dve", PW[:, 0, 0, :], tb(COS1), ["TB"], ["PW"])
            copy("dve", PW[:, 1, 0, :], tb(SIN1), ["TB"], ["PW"])
            for lv in range(10):
                pr, pi = PW[:, 0, lv, :], PW[:, 1, lv, :]
                tt_op("dve", tb(T0_), pr, pr, ALU.mult, ["PW", "TB"], ["TB"])
                tt_op("dve", tb(T1_), pi, pi, ALU.mult, ["PW", "TB"], ["TB"])
                tt_op("dve", PW[:, 0, lv + 1, :], tb(T0_), tb(T1_), ALU.subtract, ["TB", "PW"], ["PW"])
                tt_op("dve", tb(T0_), pr, pi, ALU.mult, ["PW", "TB"], ["TB"])
                ts_op("dve", PW[:, 1, lv + 1, :], tb(T0_), 2.0, ALU.mult, ["TB", "PW"], ["PW"])


        P.capture = []
        s5_tables(0)
        tblq = P.capture
        P.capture = None
        for l in range(nl):
            vb = 0
            VEC = VEC2[:, l % 2, :]
            VR = f"VEC{l % 2}"
            dma_io(VEC2[:, l % 2, :], vec_d[:, l * NVL:(l + 1) * NVL], [], [VR])

            rmsnorm(lambda kt: VEC[:, kt:kt + 1], VR, False)

            dma_io(S2[:, 0:1920], zbuf_d[l, :, :], [], [f"S2_{t}" for t in range(4)])
            for j in range(4):
                src = S2[:, j * 480:(j + 1) * 480].rearrange("p (s c) -> p s c", c=30)
                copy("pool", Zs(j)[:, :, 0:30], src, [f"S2_{t}" for t in range(4)], [f"Z{j}_4"])

            for m in range(12):
                P.flush(tblq, 30)
                if m % 2 == 0:
                    wb = ws_get()
                wvw = wv(wb, 256)
                mc = slice((m % 2) * 128, (m % 2) * 128 + 128)
                bias = VEC[:, vb + 16 + m:vb + 17 + m]
                for tt in range(5):
                    b, ps = dense_group(lambda kt: wvw[:, kt, mc], lambda kt, tt: A[:, kt, tsl(tt)], 8, tt,
                                        lambda kt, tt: [f"wbf{wb}", f"A{kt}_{tt}"])
                    n = TT[tt][1]
                    if m < 4:
                        ti = T0C if tt < 4 else 4
                        ts_op("dve", U[:, m, tsl(tt)].rearrange("p (s k) -> p s k", s=ti),
                              ps.rearrange("p (k s) -> p s k", s=ti), bias, ALU.add, [f"ps{b}", VR], [f"U{m}_{tt}"])
                    elif m % 2 == 0:
                        act(S1[:, tsl(tt)], ps, AF.Sigmoid, [f"ps{b}", VR], [f"S1_{tt}"], bias=bias)
                    else:
                        j = (m - 5) // 2
                        if tt < 4:
                            stt("dve", Z[:, j, zsl(tt)], ps, bias, S1[:, tsl(tt)], ALU.add, ALU.mult,
                                [f"ps{b}", f"S1_{tt}", VR], [f"Z{j}_{tt}"])
                            if tt == 3:
                                stt("dve", CVP[:, j, :], ps[:, 482:512], bias, S1[:, 2018:2048], ALU.add, ALU.mult,
                                    [f"ps{b}", f"S1_{tt}", VR], ["CVP"])
                        else:
                            stt("dve", Zs(j)[:, :, 30:34], v3(ps, 4), bias, v3(S1[:, tsl(4)], 4), ALU.add, ALU.mult,
                                [f"ps{b}", f"S1_{tt}", VR], [f"Z{j}_4"])
                            stt("dve", CVS[:, j, :], ps, bias, S1[:, tsl(4)], ALU.add, ALU.mult,
                                [f"ps{b}", f"S1_{tt}", VR], ["CVS"])
            P.flush(tblq, 10 ** 6)
            dma_io(cvp_d[l, :, :], CVP[:, :, :].rearrange("p a b -> p (a b)"), ["CVP"], [])
            dma_io(cvs_d[l, :, :], CVS[:, :, :].rearrange("p a b -> p (a b)"), ["CVS"], [])

            NCH = NP_TOK // T0C
            CPT = 512 // T0C
            LG = T0C.bit_length() - 1
            SRE, SIM_ = SACC[:, 0, :], SACC[:, 1, :]
            DRE, DIM_, TMP, GRE, GIM = (S1[:, i * NCH:(i + 1) * NCH] for i in range(5))
            rS1 = ["S1_0", "S1_1", "S1_2"]
            T8A = S1[:, 1280:1536].rearrange("p (n w) -> p n w", w=32)
            T8B = S1[:, 1536:1792].rearrange("p (n w) -> p n w", w=32)
            rT8 = ["S1_2", "S1_3"]

            def coef8(dst, dres, cre_w, cim_w, wres, i_re, i_im, i_nim, pi_):
                def tab(i):
                    return TB[:, i:i + T0C, pi_].unsqueeze(2).broadcast_to([128, T0C, 32])
                crb = cre_w.unsqueeze(1).broadcast_to([128, T0C, 32])
                cib = cim_w.unsqueeze(1).broadcast_to([128, T0C, 32])
                rd = [wres, "TB"]
                tt_op("dve", T8A, crb, tab(i_re), ALU.mult, rd, rT8)
                tt_op("dve", T8B, cib, tab(i_im), ALU.mult, rd, rT8)
                tt_op("dve", dst[:, 0, :, :], T8A, T8B, ALU.subtract, rT8, [dres])
                tt_op("dve", T8A, crb, tab(i_nim), ALU.mult, rd, rT8)
                tt_op("dve", T8B, cib, tab(i_re), ALU.mult, rd, rT8)
                tt_op("dve", dst[:, 1, :, :], T8A, T8B, ALU.subtract, rT8, [dres])

            def tile_gen(j, keep):
                wa = ws_get(keep=keep)
                rwa = f"wbf{wa}"

                def blk(pp, i):
                    return WBF[:, wa, pp * 512 + i * 128:pp * 512 + (i + 1) * 128]

                def blkT(pp, i):
                    return WBF[:, wb2, pp * 256 + i * 128:pp * 256 + (i + 1) * 128]

                WCv = S2[:, 0:2048].rearrange("p (a b k) -> p a b k", a=4, b=2)
                rWC = ["S2_0", "S2_1", "S2_2", "S2_3"]
                def pair_stage(pp, stage):
                    pi_ = 4 * j + pp
                    sc = lambda i: TB[:, i, pi_:pi_ + 1]
                    bre, bim, cre, cim = blk(pp, 0), blk(pp, 1), blk(pp, 2), blk(pp, 3)
                    last = (pp == 3)
                    WCR, WCI = WCv[:, pp, 0, :], WCv[:, pp, 1, :]
                    h0r, h0i = H0[:, 0, pp, :], H0[:, 1, pp, :]
                    if stage in ("B0", "B1"):
                        ubv = U[:, j, 0:NP_TOK].rearrange("p (t s k) -> p t s k", t=4, s=T0C)
                        usv = U[:, j, tsl(4)].rearrange("p (t q) -> p t q", t=4)
                        nbu = [0]
                        BW = TMPF[:, :, :].rearrange("p a b -> p (a b)").bitcast(BF16).rearrange("p (a b c) -> p a b c", a=2, b=2)

                        def bu_mm(s_):
                            b = 5 + s_ % 2
                            rhs = ubv[:, :, s_, :]
                            ures = [f"U{j}_{t}" for t in range(4)]
                            mm(PS[b][:, 0:NCH], bre, rhs, True, True, [rwa] + ures, b)
                            mm(PS[b][:, 256:256 + NCH], bim, rhs, True, True, [rwa] + ures, b)

                        def bu_evac(s_):
                            b = 5 + s_ % 2
                            f = s_ % 2
                            n_e = T0C - 1 - s_
                            o1, o2 = BW[:, f, 0, :], BW[:, f, 1, :]
                            act(o1, PS[b][:, :], AF.Copy, [f"ps{b}", "TB"], [f"TMPF{f}"], scale=sc(QR(n_e)))
                            act(o2, PS[b][:, :], AF.Copy, [f"ps{b}", "TB"], [f"TMPF{f}"], scale=sc(QI(n_e)))

                        def bu_sacc(s_):
                            f = s_ % 2
                            o1, o2 = BW[:, f, 0, :], BW[:, f, 1, :]
                            rd = [f"TMPF{f}", "IDB", "NIDB"]
                            mm(PS[7][:, 0:NCH], IDB[:, :], o1[:, 0:NCH], s_ == 0, False, rd, 7)
                            mm(PS[7][:, 0:NCH], NIDB[:, :], o2[:, 256:256 + NCH], False, False, rd, 7)
                            mm(PS[7][:, 256:256 + NCH], IDB[:, :], o2[:, 0:NCH], False, False, rd, 7)
                            mm(PS[7][:, 256:256 + NCH], IDB[:, :], o1[:, 256:256 + NCH], False, s_ == T0C - 1, rd, 7)

                        def bu_step(rhs, ncol, dst_re, dst_im, n_e, first, ures):
                            b = 6
                            p_re, p_im = PS[b][:, 0:ncol], PS[b][:, 256:256 + ncol]
                            mm(p_re, bre, rhs, True, True, [rwa] + ures, b)
                            mm(p_im, bim, rhs, True, True, [rwa] + ures, b)
                            rb = [f"ps{b}", "TB", "SACC"]
                            if first:
                                ts_op("dve", dst_re, p_re, sc(QR(n_e)), ALU.mult, rb, ["SACC"])
                                ts_op("dve", dst_im, p_re, sc(QI(n_e)), ALU.mult, rb, ["SACC"])
                            else:
                                stt("dve", dst_re, p_re, sc(QR(n_e)), dst_re, ALU.mult, ALU.add, rb, ["SACC"])
                                stt("dve", dst_im, p_re, sc(QI(n_e)), dst_im, ALU.mult, ALU.add, rb, ["SACC"])
                            stt("dve", dst_re, p_im, sc(NQI(n_e)), dst_re, ALU.mult, ALU.add, rb, ["SACC"])
                            stt("dve", dst_im, p_im, sc(QR(n_e)), dst_im, ALU.mult, ALU.add, rb, ["SACC"])

                        if stage == "B0":
                            bu_mm(0)
                            bu_mm(1)
                            bu_evac(0)
                            bu_evac(1)
                        else:
                            for s_ in range(T0C):
                                bu_sacc(s_)
                                if s_ + 2 < T0C:
                                    bu_mm(s_ + 2)
                                    bu_evac(s_ + 2)
                    elif stage == "SMP":
                        mm(PS[6][:, 0:NS_TOK], bre, U[:, j, tsl(4)], True, True, [rwa, f"U{j}_4"], 6)
                        mm(PS[6][:, 256:256 + NS_TOK], bim, U[:, j, tsl(4)], True, True, [rwa, f"U{j}_4"], 6)
                        copy("dve", SBS[:, :, :], PS[6][:, :].rearrange("p (a k) -> p a k", a=2)[:, :, 0:NS_TOK], ["ps6"], ["SBS"])
                        ss_re, ss_im = SSO[:, 0, pp, 1:17], SSO[:, 1, pp, 1:17]
                        b_re = SBS[:, 0, :].rearrange("p (t q) -> p t q", t=4)
                        b_im = SBS[:, 1, :].rearrange("p (t q) -> p t q", t=4)
                        P1, P2 = S1[:, 1792:1856], S1[:, 1856:1920]
                        p1v = P1.rearrange("p (t q) -> p t q", t=4)
                        p2v = P2.rearrange("p (t q) -> p t q", t=4)

                        def tabr(i0_):
                            return TB[:, i0_:i0_ + 4, pi_].unsqueeze(2).broadcast_to([128, 4, NSEQ])

                        rb = ["SBS", "TB", "S1_3"]
                        for dst, ta, tb2 in ((ss_re, RQR, RNQI), (ss_im, RQI, RQR)):
                            tt_op("dve", p1v, b_re, tabr(ta), ALU.mult, rb, ["S1_3"])
                            tt_op("dve", p2v, b_im, tabr(tb2), ALU.mult, rb, ["S1_3"])
                            tt_op("dve", P1, P1, P2, ALU.add, ["S1_3"], ["S1_3"])
                            P.op("dve", lambda e, dst=dst: e.tensor_reduce(
                                out=dst, in_=P1.rearrange("p (t q) -> p q t", t=4), axis=mybir.AxisListType.X, op=ALU.add),
                                reads=["S1_3"], writes=["SSO"])
                        stt("dve", ss_re, h0r, sc(ER(4)), ss_re, ALU.mult, ALU.add, ["H0", "TB", "SSO"], ["SSO"])
                        stt("dve", ss_re, h0i, sc(NEI(4)), ss_re, ALU.mult, ALU.add, ["H0", "TB", "SSO"], ["SSO"])
                        stt("dve", ss_im, h0r, sc(EI(4)), ss_im, ALU.mult, ALU.add, ["H0", "TB", "SSO"], ["SSO"])
                        stt("dve", ss_im, h0i, sc(ER(4)), ss_im, ALU.mult, ALU.add, ["H0", "TB", "SSO"], ["SSO"])
                    elif stage == "ROTC":
                        copy("dve", SACC[:, :, 0:NCH], PS[7][:, :].rearrange("p (a k) -> p a k", a=2)[:, :, 0:NCH], ["ps7"], ["SACC"])
                    elif stage == "ROT":
                        s_re, s_im = SACC[:, 0, 0:NCH], SACC[:, 1, 0:NCH]
                        tt_op("dve", DRE, WCR, s_re, ALU.mult, rWC + ["SACC"], ["S1_0"])
                        tt_op("dve", TMP, WCI, s_im, ALU.mult, rWC + ["SACC"], ["S1_1"])
                        tt_op("dve", DRE, DRE, TMP, ALU.add, ["S1_0", "S1_1"], ["S1_0"])
                        tt_op("dve", DIM_, WCR, s_im, ALU.mult, rWC + ["SACC"], ["S1_0"])
                        tt_op("dve", TMP, WCI, s_re, ALU.mult, rWC + ["SACC"], ["S1_1"])
                        tt_op("dve", DIM_, DIM_, TMP, ALU.subtract, ["S1_0", "S1_1"], ["S1_0"])
                    elif stage == "SCAN":
                        act(RT[:, :], WCR, AF.Identity, rWC + ["TB"], ["RT"], bias=sc(RT0), scale=0.0)
                        P.op("dve", lambda e: e.tensor_tensor_scan(out=GRE, data0=RT[:, :], data1=DRE, initial=0.0,
                                                                   op0=ALU.mult, op1=ALU.add),
                             reads=["RT", "S1_0"], writes=["S1_1"])
                        P.op("dve", lambda e: e.tensor_tensor_scan(out=GIM, data0=RT[:, :], data1=DIM_, initial=0.0,
                                                                   op0=ALU.mult, op1=ALU.add),
                             reads=["RT", "S1_0"], writes=["S1_2"])
                        e_ = NCH - 1
                        tt_op("dve", CAR[:, 2:3], WCI[:, e_:e_ + 1], GIM[:, e_:e_ + 1], ALU.mult, rWC + ["S1_2", "CAR"], ["CAR"])
                        stt("dve", SSO[:, 0, pp, 0:1], WCR[:, e_:e_ + 1], GRE[:, e_:e_ + 1], CAR[:, 2:3], ALU.mult, ALU.subtract,
                            rWC + ["S1_1", "CAR"], ["SSO"])
                        tt_op("dve", CAR[:, 3:4], WCR[:, e_:e_ + 1], GIM[:, e_:e_ + 1], ALU.mult, rWC + ["S1_2", "CAR"], ["CAR"])
                        stt("dve", SSO[:, 1, pp, 0:1], WCI[:, e_:e_ + 1], GRE[:, e_:e_ + 1], CAR[:, 3:4], ALU.mult, ALU.add,
                            rWC + ["S1_1", "CAR"], ["SSO"])
                        tt_op("dve", DRE, WCR, GRE, ALU.mult, rWC + ["S1_1"], ["S1_0"])
                        tt_op("dve", TMP, WCI, GIM, ALU.mult, rWC + ["S1_2"], ["S1_1"])
                        tt_op("dve", HP[:, 0, 1:NCH + 1], DRE, TMP, ALU.subtract, ["S1_0", "S1_1"], ["HP"])
                        tt_op("dve", DIM_, WCI, GRE, ALU.mult, rWC + ["S1_1"], ["S1_0"])
                        tt_op("dve", TMP, WCR, GIM, ALU.mult, rWC + ["S1_2"], ["S1_1"])
                        tt_op("dve", HP[:, 1, 1:NCH + 1], DIM_, TMP, ALU.add, ["S1_0", "S1_1"], ["HP"])
                    elif stage == "C":
                        copy("dve", H0B[:, 0, :], h0r, ["H0"], ["H0B"])
                        copy("dve", H0B[:, 1, :], h0i, ["H0"], ["H0B"])
                        if 'c' in DBG:
                            return
                        w0 = 32 * pp
                        wz = 32 * ((pp - 1) % 4)
                        P.op("pool", lambda e, wz=wz: e.memset(LC8[:, :, :, wz:wz + 32], 0.0), writes=["LC8"])
                        coef8(LC8[:, :, :, w0:w0 + 32], "LC8", cre[:, w0:w0 + 32], cim[:, w0:w0 + 32], rwa,
                              ER(1), EI(1), NEI(1), pi_)
                        for jj in range(T0C):
                            fin = last and jj == T0C - 1
                            for tt in range(4):
                                pv = PS[tt][:, jj * CPT:(jj + 1) * CPT]
                                mm(pv, LC8[:, 0, jj, :], HP[:, 0, tt * CPT:(tt + 1) * CPT], False, False, ["LC8", "HP"], tt)
                                mm(pv, LC8[:, 1, jj, :], HP[:, 1, tt * CPT:(tt + 1) * CPT], False, fin, ["LC8", "HP"], tt)
                            if jj < 4:
                                pv = PS[4][:, jj * NSEQ:(jj + 1) * NSEQ]
                                mm(pv, LC8[:, 0, jj, :], H0B[:, 0, :], False, False, ["LC8", "H0B"], 4)
                                mm(pv, LC8[:, 1, jj, :], H0B[:, 1, :], False, last and jj == 3, ["LC8", "H0B"], 4)

                pair_stage(0, "B0")
                pair_stage(0, "B1")
                pair_stage(1, "B0")
                yield "early"
                wb2 = ws_get(keep=1)
                rwb = f"wbf{wb2}"
                for k_ in range(2):
                    dma_io(H0[:, k_, :, :], h0_d[l, :, k_, 4 * j:4 * j + 4, :], [], ["H0"])
                SCR = S1[:, 1280:1792].rearrange("p (a k) -> p a k", a=4)
                copy("dve", WCv[:, :, 0, 0:1], PW[:, 0, LG, 4 * j:4 * j + 4].unsqueeze(2), ["PW"], rWC)
                copy("dve", WCv[:, :, 1, 0:1], PW[:, 1, LG, 4 * j:4 * j + 4].unsqueeze(2), ["PW"], rWC)
                lv = 0
                while (1 << lv) < NCH and 'd' not in DBG:
                    m_ = 1 << lv
                    prb = PW[:, 0, LG + lv, 4 * j:4 * j + 4].unsqueeze(2).broadcast_to([128, 4, m_])
                    pib = PW[:, 1, LG + lv, 4 * j:4 * j + 4].unsqueeze(2).broadcast_to([128, 4, m_])
                    sR, sI = WCv[:, :, 0, 0:m_], WCv[:, :, 1, 0:m_]
                    dR, dI = WCv[:, :, 0, m_:2 * m_], WCv[:, :, 1, m_:2 * m_]
                    tt_op("dve", SCR[:, :, 0:m_], sI, pib, ALU.mult, rWC + ["PW"], rT8)
                    tt_op("dve", dR, sR, prb, ALU.mult, rWC + ["PW"], rWC)
                    tt_op("dve", dR, dR, SCR[:, :, 0:m_], ALU.subtract, rWC + rT8, rWC)
                    tt_op("dve", SCR[:, :, 0:m_], sR, pib, ALU.mult, rWC + ["PW"], rT8)
                    tt_op("dve", dI, sI, prb, ALU.mult, rWC + ["PW"], rWC)
                    tt_op("dve", dI, dI, SCR[:, :, 0:m_], ALU.add, rWC + rT8, rWC)
                    lv += 1
                for pp in range(4 if 't' not in DBG else 0):
                    pi_ = 4 * j + pp
                    w0 = 32 * pp
                    coef8(LC8[:, :, :, 0:32], "LC8", blk(pp, 2)[:, w0:w0 + 32], blk(pp, 3)[:, w0:w0 + 32], rwa, QR(0), QI(0), NQI(0), pi_)
                    for tau in range(T0C):
                        pst = PS[tau // 4][:, (tau % 4) * 128 + w0:(tau % 4) * 128 + w0 + 32]
                        mm(pst, blkT(pp, 0), LC8[:, 0, tau, 0:32], True, False, [rwb, "LC8"], tau // 4)
                        mm(pst, blkT(pp, 1), LC8[:, 1, tau, 0:32], False, True, [rwb, "LC8"], tau // 4)
                for tau in range(T0C if 't' not in DBG else 0):
                    pst = PS[tau // 4][:, (tau % 4) * 128:(tau % 4 + 1) * 128]
                    act(TAP[:, tau, :], pst, AF.Copy, [f"ps{tau // 4}"], [f"TAP{tau}"])
                    if tau == 0:
                        stt("dve", TAP[:, 0, :], IDB[:, :], VEC[:, vb + 28 + j:vb + 29 + j], TAP[:, 0, :], ALU.mult, ALU.add,
                            ["IDB", VR, "TAP0"], ["TAP0"])
                for tt in range(5):
                    t0, n = TT[tt]
                    ti = T0C if tt < 4 else 4
                    cw = n // ti
                    for tau in range(ti):
                        mm(PS[tt][:, tau * cw:n], TAP[:, tau, :], U[:, j, t0:t0 + (ti - tau) * cw], tau == 0, False,
                           [f"TAP{tau}", f"U{j}_{tt}"], tt)

                pair_stage(0, "ROTC")
                pair_stage(0, "SMP")
                pair_stage(0, "ROT")
                for pp in range(4):
                    if pp < 3:
                        pair_stage(pp + 1, "B1")
                    else:
                        yield "last"
                    pair_stage(pp, "SCAN")
                    if pp < 2:
                        pair_stage(pp + 2, "B0")
                    pair_stage(pp, "C")
                    if pp < 3:
                        pair_stage(pp + 1, "ROTC")
                        pair_stage(pp + 1, "SMP")
                        pair_stage(pp + 1, "ROT")
                for k_ in range(2):
                    dma_io(sso_d[l, :, k_, 4 * j:4 * j + 4, :], SSO[:, k_, :, :], ["SSO"], [])
                for tt in range(5):
                    n = TT[tt][1]
                    ti = T0C if tt < 4 else 4
                    act(A[:, j, tsl(tt)].rearrange("p (k s) -> p k s", s=ti),
                        PS[tt][:, 0:n].rearrange("p (s k) -> p k s", s=ti), AF.Gelu_apprx_tanh,
                        [f"ps{tt}"], [f"A{j}_{tt}"])
                yield "done"

            tg = tile_gen(0, 0)
            next(tg)
            for j in range(4):
                next(tg)
                tg_next = None
                if j < 3:
                    tg_next = tile_gen(j + 1, 2)
                    next(tg_next)
                next(tg)
                tg = tg_next
            bank_ctr[0] = 0

            wb = ws_get()
            wvw = wv(wb, 512)
            for m in range(4):
                bias = VEC[:, vb + 32 + m:vb + 33 + m]
                for tt in range(5):
                    b, ps = dense_group(lambda kt: wvw[:, kt, m * 128:(m + 1) * 128],
                                        lambda kt, tt: A[:, kt, tsl(tt)], 4, tt,
                                        lambda kt, tt: [f"wbf{wb}", f"A{kt}_{tt}"])
                    act(S1[:, tsl(tt)], ps, AF.Sigmoid, [f"ps{b}", VR], [f"S1_{tt}"], bias=bias)
                    tt_op("dve", U[:, m, tsl(tt)], A[:, m, tsl(tt)], S1[:, tsl(tt)], ALU.mult,
                          [f"A{m}_{tt}", f"S1_{tt}"], [f"U{m}_{tt}"])

            for j in range(4):
                banks = [next_bank() for _ in range(5)]
                for k in range(31):
                    d = k % 4
                    ts_op("dve", DG[:, d, :], IDB[:, :], VEC[:, vb + 48 + j * 31 + k:vb + 49 + j * 31 + k], ALU.mult,
                          ["IDB", VR], [f"DG{d}"])
                    for tt in range(4):
                        t0 = TT[tt][0]
                        rd = [f"DG{d}", f"Z{j}_{tt}", f"Zpad{j}"] + ([f"Z{j}_{tt - 1}"] if tt > 0 else [])
                        mm(PS[banks[tt]][:, :], DG[:, d, :], Z[:, j, t0 + k:t0 + k + 512], k == 0, k == 30, rd, banks[tt])
                    mm(v3(PS[banks[4]][:, 0:64], 4), DG[:, d, :], Zs(j)[:, :, k:k + 4], k == 0, k == 30,
                       [f"DG{d}", f"Z{j}_4"], banks[4])
                bias = VEC[:, vb + 36 + j:vb + 37 + j]
                for tt in range(5):
                    n = TT[tt][1]
                    act(A[:, 4 + j, tsl(tt)], PS[banks[tt]][:, 0:n], AF.Identity, [f"ps{banks[tt]}", VR],
                        [f"A{4 + j}_{tt}"], bias=bias)
            bank_ctr[0] = 0
            for tt in range(5):
                n = TT[tt][1]
                b, ps = dense_group(lambda kt: ONB[:, :], lambda kt, tt: A[:, 4 + kt, tsl(tt)], 4, tt,
                                    lambda kt, tt: ["ONB", f"A{4 + kt}_{tt}"])
                act(S1[:, tsl(tt)], ps, AF.Copy, [f"ps{b}"], [f"S1_{tt}"], scale=1.0 / 512)
            for j in range(4):
                for tt in range(5):
                    act(Z[:, j, tsl(tt)], A[:, 4 + j, tsl(tt)], AF.Square, [f"A{4 + j}_{tt}"], Zrow(j) + [f"Zpad{j}"])
            for tt in range(5):
                b, ps = dense_group(lambda kt: ONB[:, :], lambda kt, tt: Z[:, kt, tsl(tt)], 4, tt,
                                    lambda kt, tt: ["ONB"] + Zrow(kt))
                tt_op("dve", S2[:, tsl(tt)], S1[:, tsl(tt)], S1[:, tsl(tt)], ALU.mult, [f"S1_{tt}"], [f"S2_{tt}"])
                stt("dve", S2[:, tsl(tt)], ps, 1.0 / 512, S2[:, tsl(tt)], ALU.mult, ALU.subtract,
                    [f"ps{b}", f"S2_{tt}"], [f"S2_{tt}"])
                act(S2[:, tsl(tt)], S2[:, tsl(tt)], AF.Ln, [f"S2_{tt}"], [f"S2_{tt}"], bias=EPS)
                act(S2[:, tsl(tt)], S2[:, tsl(tt)], AF.Exp, [f"S2_{tt}"], [f"S2_{tt}"], scale=-0.5)
            tf = [0]
            for j in range(4):
                lg, lb = VEC[:, vb + 40 + j:vb + 41 + j], VEC[:, vb + 44 + j:vb + 45 + j]
                for tt in range(5):
                    n = TT[tt][1]
                    f = tf[0] % 2
                    tf[0] += 1
                    tmp = TMPF[:, f, 0:n]
                    tt_op("dve", tmp, A[:, 4 + j, tsl(tt)], S1[:, tsl(tt)], ALU.subtract,
                          [f"A{4 + j}_{tt}", f"S1_{tt}"], [f"TMPF{f}"])
                    tt_op("dve", tmp, tmp, S2[:, tsl(tt)], ALU.mult, [f"TMPF{f}", f"S2_{tt}"], [f"TMPF{f}"])
                    act(A[:, 4 + j, tsl(tt)], tmp, AF.Silu, [f"TMPF{f}", VR], [f"A{4 + j}_{tt}"], bias=lb, scale=lg)

            for m in range(8):
                if m % 2 == 0:
                    wb = ws_get()
                wvw = wv(wb, 256)
                mc = slice((m % 2) * 128, (m % 2) * 128 + 128)
                for tt in range(5):
                    b, ps = dense_group(lambda kt: wvw[:, kt, mc],
                                        lambda kt, tt: (U[:, kt, tsl(tt)] if kt < 4 else A[:, kt, tsl(tt)]), 8, tt,
                                        lambda kt, tt: [f"wbf{wb}", (f"U{kt}_{tt}" if kt < 4 else f"A{kt}_{tt}")])
                    tt_op("dve", X[:, m, tsl(tt)], X[:, m, tsl(tt)], ps, ALU.add, [f"X{m}_{tt}", f"ps{b}"], [f"X{m}_{tt}"])

            rmsnorm(lambda kt: VEC[:, 8 + kt:9 + kt], VR, False)
            bank_ctr[0] = 0

            def Hh(m, tt):
                return U[:, m, tsl(tt)] if m < 4 else Z[:, m - 4, tsl(tt)]

            def Hr(m, tt):
                return [f"U{m}_{tt}"] if m < 4 else Zrow(m - 4) + [f"Zpad{m - 4}"]

            for fb in range(4):
                if fb == 0 and l + 1 < nl:
                    P.capture = []
                    s5_tables(l + 1)
                    tblq = P.capture
                    P.capture = None
                for m in range(8):
                    P.flush(tblq, 6)
                    if m % 2 == 0:
                        wb = ws_get()
                    wvw = wv(wb, 256)
                    mc = slice((m % 2) * 128, (m % 2) * 128 + 128)
                    for tt in range(5):
                        n = TT[tt][1]
                        b, ps = dense_group(lambda kt: wvw[:, kt, mc], lambda kt, tt: A[:, kt, tsl(tt)], 8, tt,
                                            lambda kt, tt: [f"wbf{wb}", f"A{kt}_{tt}"])
                        f = tf[0] % 2
                        tf[0] += 1
                        tmp = TMPF[:, f, 0:n]
                        act(tmp, ps, AF.Relu, [f"ps{b}"], [f"TMPF{f}"])
                        act(Hh(m, tt), tmp, AF.Square, [f"TMPF{f}"], Hr(m, tt))
                for m in range(8):
                    P.flush(tblq, 6)
                    if m % 2 == 0:
                        wb = ws_get()
                    wvw = wv(wb, 256)
                    mc = slice((m % 2) * 128, (m % 2) * 128 + 128)
                    for tt in range(5):
                        b, ps = dense_group(lambda kt: wvw[:, kt, mc], lambda kt, tt: Hh(kt, tt), 8, tt,
                                            lambda kt, tt: [f"wbf{wb}"] + Hr(kt, tt))
                        tt_op("dve", X[:, m, tsl(tt)], X[:, m, tsl(tt)], ps, ALU.add,
                              [f"X{m}_{tt}", f"ps{b}"], [f"X{m}_{tt}"])
            P.flush(tblq, 10 ** 6)
            for j in range(4):
                P.op("pool", lambda e, j=j: e.memset(Z[:, j, 0:30], 0.0), writes=Zrow(j) + [f"Zpad{j}"])
            bank_ctr[0] = 0

        rmsnorm(lambda kt: GF[:, kt:kt + 1], "GF", True)
        for kt in range(8):
            dma_io(yT[kt, :, :], X[:, kt, :], [f"X{kt}_{t}" for t in range(5)], [])

        P.emit(block, sems)
    return nc


def _prep_shared(inp):
    f = np.float32
    nl = DEPTH
    order = list(range(0, 512))
    for j in range(4):
        order += list(range(1024 + 128 * j, 1024 + 128 * (j + 1)))
        order += list(range(512 + 128 * j, 512 + 128 * (j + 1)))
    order = np.array(order)
    w_in = np.ascontiguousarray(inp["w_in"][:, :, order])
    b_in = inp["b_in"][:, order]
    vec = np.zeros((128, NV), f)

    def cols(v, n):
        return np.asarray(v, f).reshape(n, 128).T

    for l in range(nl):
        b = l * NVL
        vec[:, b + 0:b + 8] = cols(inp["norm_mix_g"][l], 8)
        vec[:, b + 8:b + 16] = cols(inp["norm_mlp_g"][l], 8)
        vec[:, b + 16:b + 28] = cols(b_in[l], 12)
        vec[:, b + 28:b + 32] = cols(inp["ssm_d"][l], 4)
        vec[:, b + 32:b + 36] = cols(inp["b_glu"][l], 4)
        vec[:, b + 36:b + 40] = cols(inp["conv_b"][l], 4)
        vec[:, b + 40:b + 44] = cols(inp["conv_ln_g"][l], 4)
        vec[:, b + 44:b + 48] = cols(inp["conv_ln_b"][l], 4)
        cw = np.asarray(inp["conv_w"][l], f)
        vec[:, b + 48:b + 172] = cw.reshape(31, 4, 128).transpose(2, 1, 0).reshape(128, 124)
    vec[:, NVL * nl:NVL * nl + 8] = cols(inp["norm_f_g"], 8)

    s5p = np.zeros((128, nl * 48), f)
    for l in range(nl):
        for nm, off in (("ssm_a_re", 0), ("ssm_a_im", 16)):
            a = np.asarray(inp[nm][l], f).reshape(16, 2, 64)
            s5p[:, l * 48 + off:l * 48 + off + 16] = a.transpose(1, 2, 0).reshape(128, 16)
        ld = np.asarray(inp["ssm_log_dt"][l], f).reshape(16, 2)
        s5p[:, l * 48 + 32:l * 48 + 48] = np.repeat(ld.T[:, None, :], 64, axis=1).reshape(128, 16)

    s5w = np.zeros((nl, 4, 128, 2048), f)
    for l in range(nl):
        for pi in range(16):
            j, pp = pi // 4, pi % 4
            for x in range(2):
                g = 2 * pi + x
                gl = 2 * pp + x
                rows_gc = slice(gl * 16, gl * 16 + 16)
                cols_xp = slice(x * 64, x * 64 + 64)
                base = pp * 512
                s5w[l, j, rows_gc, base + 0 + x * 64:base + 0 + x * 64 + 64] = inp["ssm_b_re"][l, g].T
                s5w[l, j, rows_gc, base + 128 + x * 64:base + 128 + x * 64 + 64] = inp["ssm_b_im"][l, g].T
                s5w[l, j, cols_xp, base + 256 + gl * 16:base + 256 + gl * 16 + 16] = inp["ssm_c_re"][l, g].T
                s5w[l, j, cols_xp, base + 384 + gl * 16:base + 384 + gl * 16 + 16] = inp["ssm_c_im"][l, g].T
    s5wT = np.zeros((nl, 4, 128, 1024), f)
    for l in range(nl):
        for pi in range(16):
            j, pp = pi // 4, pi % 4
            for x in range(2):
                g = 2 * pi + x
                gl = 2 * pp + x
                s5wT[l, j, x * 64:x * 64 + 64, pp * 256 + gl * 16:pp * 256 + gl * 16 + 16] = inp["ssm_b_re"][l, g]
                s5wT[l, j, x * 64:x * 64 + 64, pp * 256 + 128 + gl * 16:pp * 256 + 128 + gl * 16 + 16] = inp["ssm_b_im"][l, g]
    return dict(s5wT=s5wT, w_in=w_in, w_glu=np.ascontiguousarray(inp["w_glu"], f), w_out=np.ascontiguousarray(inp["w_out"], f),
                w_up=np.ascontiguousarray(inp["w_up"], f), w_down=np.ascontiguousarray(inp["w_down"], f),
                s5w=s5w, vec=vec, s5p=s5p, ident=np.eye(128, dtype=f))


def _prep_core(inp, c):
    f = np.float32
    xa = np.concatenate([inp["x_prompt"][c], inp["x_sample"][NSEQ * c:NSEQ * (c + 1)].reshape(NS_TOK, D)], axis=0)
    xT = np.ascontiguousarray(xa.T.reshape(8, 128, NTOK), f)
    sl = slice(NSEQ * c, NSEQ * (c + 1))
    h0 = np.zeros((DEPTH, 128, 2, 16, 16), f)
    for k, nm in enumerate(("state_ssm_re", "state_ssm_im")):
        s = np.asarray(inp[nm][:, sl], f).reshape(DEPTH, NSEQ, 16, 2, 64)
        h0[:, :, k] = s.transpose(0, 3, 4, 2, 1).reshape(DEPTH, 128, 16, NSEQ)
    sc = np.asarray(inp["state_conv"][:, sl], f)
    zbuf = sc.reshape(DEPTH, NSEQ, 30, 4, 128).transpose(0, 4, 3, 1, 2)
    return dict(xT=xT, h0=np.ascontiguousarray(h0),
                zbuf=np.ascontiguousarray(zbuf.reshape(DEPTH, 128, 1920)), sconv=np.ascontiguousarray(sc))


_NC_CACHE = {}


def run_device(inp, nl=DEPTH):
    if nl not in _NC_CACHE:
        _NC_CACHE[nl] = build(nl)
    nc = _NC_CACHE[nl]
    shared = _prep_shared(inp)
    in_maps = []
    for c in range(NCORES):
        m = dict(shared)
        m.update(_prep_core(inp, c))
        in_maps.append(m)
    res = run_bass_kernel_spmd(nc, in_maps, core_ids=list(range(NCORES)))
    return res.results


def assemble(results, nl=DEPTH):
    f = np.float32
    y_p = np.zeros((NCORES, NP_TOK, D), f)
    y_s = np.zeros((NCORES * NSEQ, 4, D), f)
    re_p = np.zeros((nl, NCORES, 32, 64), f)
    im_p = np.zeros((nl, NCORES, 32, 64), f)
    cv_p = np.zeros((nl, NCORES, 30, 512), f)
    re_s = np.zeros((nl, NCORES * NSEQ, 32, 64), f)
    im_s = np.zeros((nl, NCORES * NSEQ, 32, 64), f)
    cv_s = np.zeros((nl, NCORES * NSEQ, 30, 512), f)
    for c, r in enumerate(results):
        y = np.asarray(r["yT"]).reshape(D, NTOK).T
        y_p[c] = y[:NP_TOK]
        y_s[NSEQ * c:NSEQ * (c + 1)] = y[NP_TOK:].reshape(NSEQ, 4, D)
        sso = np.asarray(r["sso"])[:nl].reshape(nl, 2, 64, 2, 16, 17)
        st = sso.transpose(0, 3, 5, 4, 1, 2).reshape(nl, 2, 17, 32, 64)
        re_p[:, c], im_p[:, c] = st[:, 0, 0], st[:, 1, 0]
        re_s[:, NSEQ * c:NSEQ * (c + 1)] = st[:, 0, 1:]
        im_s[:, NSEQ * c:NSEQ * (c + 1)] = st[:, 1, 1:]
        cvp = np.asarray(r["cvp"])[:nl].reshape(nl, 128, 4, 30)
        cv_p[:, c] = cvp.transpose(0, 3, 2, 1).reshape(nl, 30, 512)
        cvs = np.asarray(r["cvs"])[:nl].reshape(nl, 128, 4, NSEQ, 4)
        zs = cvs.transpose(0, 3, 4, 2, 1).reshape(nl, NSEQ, 4, 512)
        cvc = np.asarray(r["cvc"])[:nl]
        cv_s[:, NSEQ * c:NSEQ * (c + 1)] = np.concatenate([cvc, zs], axis=2)
    return (y_p, y_s, re_p, im_p, cv_p, re_s, im_s, cv_s)


def kernel(**inputs):
    inp = {k: np.asarray(v) for k, v in inputs.items()}
    return assemble(run_device(inp, DEPTH), DEPTH)
```

```python
import math
from contextlib import ExitStack
import numpy as np
import concourse.bass as bass
import concourse.mybir as mybir
from concourse.bass_utils import run_bass_kernel_spmd

F32 = mybir.dt.float32
BF16 = mybir.dt.bfloat16
AF = mybir.ActivationFunctionType
ALU = mybir.AluOpType

NCORES = 8
D = 1024
DEPTH = 4
NP_TOK = 2048
NSEQ = 16
NS_TOK = 64
NTOK = NP_TOK + NS_TOK
TT = [(0, 512), (512, 512), (1024, 512), (1536, 512), (2048, 64)]
ZW = 30 + NP_TOK + NSEQ * 34
ZS0 = 30 + NP_TOK
EPS = 1e-6
NVL = 172
NV = NVL * DEPTH + 8
MAGIC = 12582912.0
TWO_PI = 2.0 * math.pi
T0C = 8
import os
DBG = os.environ.get('DBG_SKIP', '')


class _Op:
    __slots__ = ("eng", "fn", "deps", "semkey", "inc", "signaled", "count", "idx")

    def __init__(self, eng, fn, semkey, inc, always):
        self.eng = eng
        self.fn = fn
        self.deps = set()
        self.semkey = semkey
        self.inc = inc
        self.signaled = always
        self.count = 0


class Prog:
    ENGS = ("pe", "act", "dve", "pool", "sp")

    def __init__(self):
        self.ops = []
        self.last_write = {}
        self.readers = {}
        self.dma_hist = {}
        self.capture = None

    def _add(self, op, reads, writes, after):
        idx = len(self.ops)
        op.idx = idx
        deps = set(after)
        for r in reads:
            lw = self.last_write.get(r)
            if lw is not None:
                deps.add(lw)
        for w in writes:
            lw = self.last_write.get(w)
            if lw is not None:
                deps.add(lw)
            for rd in self.readers.get(w, ()):
                deps.add(rd)
        deps.discard(idx)
        op.deps = deps
        self.ops.append(op)
        for r in reads:
            self.readers.setdefault(r, []).append(idx)
        for w in writes:
            self.last_write[w] = idx
            self.readers[w] = []
        return idx

    def op(self, eng, fn, reads=(), writes=(), after=()):
        if self.capture is not None:
            self.capture.append((eng, fn, tuple(reads), tuple(writes), tuple(after)))
            return None
        return self._add(_Op(eng, fn, eng, 1, False), reads, writes, after)

    def flush(self, queue, n):
        for _ in range(min(n, len(queue))):
            eng, fn, reads, writes, after = queue.pop(0)
            self._add(_Op(eng, fn, eng, 1, False), reads, writes, after)

    DMA_SLOTS = {"w": 2, "io": 8}

    def dma(self, stream, fn, reads=(), writes=(), after=()):
        hist = self.dma_hist.setdefault(stream, [])
        k = self.DMA_SLOTS[stream]
        n = len(hist)
        after = list(after)
        if n >= k:
            after.append(hist[n - k])
        idx = self._add(_Op("sp", fn, "dma:%s%d" % (stream, n % k), 16, True), reads, writes, after)
        hist.append(idx)
        return idx

    def _skip(self, p, eng):
        return p.eng == eng and (not p.semkey.startswith("dma:")) and eng == "pe"

    def emit(self, block, sems):
        ops = self.ops
        for o in ops:
            for d in o.deps:
                p = ops[d]
                if self._skip(p, o.eng):
                    continue
                p.signaled = True
        counts = {}
        for o in ops:
            if o.signaled:
                counts[o.semkey] = counts.get(o.semkey, 0) + o.inc
                o.count = counts[o.semkey]
        per_eng = {e: [] for e in self.ENGS}
        for o in ops:
            per_eng[o.eng].append(o)

        def run_engine(ename, eng):
            waited = {}
            for o in per_eng[ename]:
                need = {}
                for d in o.deps:
                    p = ops[d]
                    if not p.signaled or self._skip(p, ename):
                        continue
                    if p.count > need.get(p.semkey, 0):
                        need[p.semkey] = p.count
                for k, v in need.items():
                    if waited.get(k, 0) < v:
                        eng.wait_ge(sems[k], v)
                        waited[k] = v
                ins = o.fn(eng)
                if o.signaled:
                    ins.then_inc(sems[o.semkey], o.inc)
            return waited

        @block.tensor
        def _(e):
            run_engine("pe", e)

        @block.scalar
        def _(e):
            run_engine("act", e)

        @block.vector
        def _(e):
            run_engine("dve", e)

        @block.gpsimd
        def _(e):
            run_engine("pool", e)

        @block.sync
        def _(e):
            w = run_engine("sp", e)
            for k, v in counts.items():
                if k.startswith("dma:") and w.get(k, 0) < v:
                    e.wait_ge(sems[k], v)


def build(nl=DEPTH):
    nc = bass.Bass("TRN2", target_bir_lowering=False)

    def din(name, shape):
        return nc.dram_tensor(name, shape, F32, kind="ExternalInput").ap()

    def dout(name, shape):
        return nc.dram_tensor(name, shape, F32, kind="ExternalOutput").ap()

    xT = din("xT", [8, 128, NTOK])
    w_in = din("w_in", [DEPTH, D, 1536])
    w_glu = din("w_glu", [DEPTH, 512, 512])
    w_out = din("w_out", [DEPTH, D, D])
    w_up = din("w_up", [DEPTH, D, 4096])
    w_down = din("w_down", [DEPTH, 4096, D])
    s5w = din("s5w", [DEPTH, 4, 128, 2048])
    s5wT = din("s5wT", [DEPTH, 4, 128, 1024])
    vec_d = din("vec", [128, NV])
    s5p_d = din("s5p", [128, DEPTH * 48])
    h0_d = din("h0", [DEPTH, 128, 2, 16, 16])
    zbuf_d = din("zbuf", [DEPTH, 128, 4 * 16 * 30])
    sconv_d = din("sconv", [DEPTH, NSEQ, 30, 512])
    ident_d = din("ident", [128, 128])

    yT = dout("yT", [8, 128, NTOK])
    sso_d = dout("sso", [DEPTH, 128, 2, 16, 17])
    cvp_d = dout("cvp", [DEPTH, 128, 4 * 30])
    cvs_d = dout("cvs", [DEPTH, 128, 4 * 64])
    cvc_d = dout("cvc", [DEPTH, NSEQ, 26, 512])

    P = Prog()
    with ExitStack() as es:
        def sb(name, shape, dt):
            return es.enter_context(nc.sbuf_tensor(name, shape, dt))

        X = sb("X", [128, 8, NTOK], F32)
        A = sb("A", [128, 8, NTOK], BF16)
        U = sb("U", [128, 4, NTOK], BF16)
        Z = sb("Z", [128, 4, ZW], BF16)
        S1 = sb("S1", [128, NTOK], F32)
        S2 = sb("S2", [128, NTOK], F32)
        STG = sb("STG", [128, 2, 2048], F32)
        WBF = sb("WBF", [128, 3, 2048], BF16)
        TMPF = sb("TMPF", [128, 2, 512], F32)
        VEC2 = sb("VEC2", [128, 2, NVL], F32)
        GF = sb("GF", [128, 8], F32)
        S5P = sb("S5P", [128, DEPTH * 48], F32)
        TB = sb("TB", [128, 77, 16], F32)
        PW = sb("PW", [128, 2, 11, 16], F32)
        H0 = sb("H0", [128, 2, 4, 16], F32)
        SSO = sb("SSO", [128, 2, 4, 17], F32)
        CVP = sb("CVP", [128, 4, 30], F32)
        CVS = sb("CVS", [128, 4, 64], F32)
        SACC = sb("SACC", [128, 2, NP_TOK // T0C], F32)
        SBS = sb("SBS", [128, 2, NS_TOK], F32)
        TAP = sb("TAP", [128, T0C, 128], BF16)
        RT = sb("RT", [128, NP_TOK // T0C], F32)
        HP = sb("HP", [128, 2, NP_TOK // T0C + 1], BF16)
        H0B = sb("H0B", [128, 2, NSEQ], BF16)
        LC8 = sb("LC8", [128, 2, T0C, 128], BF16)
        CAR = sb("CAR", [128, 4], F32)
        IDB = sb("IDB", [128, 128], BF16)
        NIDB = sb("NIDB", [128, 128], BF16)
        ONB = sb("ONB", [128, 128], BF16)
        DG = sb("DG", [128, 4, 128], BF16)
        PS = [es.enter_context(nc.psum_tensor(f"ps{i}", [128, 512], F32)) for i in range(8)]

        sem_names = ["pe", "act", "dve", "pool"] + ["dma:w%d" % i for i in range(2)] + ["dma:io%d" % i for i in range(8)]
        sems = {k: es.enter_context(nc.semaphore("s_" + k.replace(":", "_"))) for k in sem_names}
        block = es.enter_context(nc.Block())

        def tsl(tt):
            t0, n = TT[tt]
            return slice(t0, t0 + n)

        def zsl(tt):
            t0, n = TT[tt]
            return slice(30 + t0, 30 + t0 + n)

        def v3(ap, inner):
            return ap.rearrange("p (s t) -> p s t", t=inner)

        def Zs(j):
            return Z[:, j, ZS0:ZW].rearrange("p (s c) -> p s c", c=34)

        def Zrow(j):
            return [f"Z{j}_{t}" for t in range(5)]

        bank_ctr = [0]

        def next_bank():
            b = bank_ctr[0] % 8
            bank_ctr[0] += 1
            return b

        def mm(ps_ap, lhsT, rhs, start, stop, reads, bank):
            wres = bank if isinstance(bank, str) else f"ps{bank}"
            P.op("pe", lambda e: e.matmul(ps_ap, lhsT=lhsT, rhs=rhs, start=start, stop=stop),
                 reads=reads, writes=[wres])

        def mmt(ps_ap, lhsT, rhs, start, stop, reads, bank, col0):
            P.op("pe", lambda e: e.matmul(ps_ap, lhsT=lhsT, rhs=rhs, start=start, stop=stop, tile_position=(0, col0)),
                 reads=reads, writes=[f"ps{bank}"])

        def act(out, in_, func, reads, writes, bias=None, scale=None):
            kw = {}
            if bias is not None:
                kw["bias"] = bias
            if scale is not None:
                kw["scale"] = scale
            P.op("act", lambda e: e.activation(out=out, in_=in_, func=func, **kw), reads=reads, writes=writes)

        def tt_op(eng, out, in0, in1, op, reads, writes):
            P.op(eng, lambda e: e.tensor_tensor(out=out, in0=in0, in1=in1, op=op), reads=reads, writes=writes)

        def ts_op(eng, out, in0, s1, op0, reads, writes, s2=None, op1=None):
            if op1 is None:
                P.op(eng, lambda e: e.tensor_scalar(out=out, in0=in0, scalar1=s1, scalar2=None, op0=op0),
                     reads=reads, writes=writes)
            else:
                P.op(eng, lambda e: e.tensor_scalar(out=out, in0=in0, scalar1=s1, scalar2=s2, op0=op0, op1=op1),
                     reads=reads, writes=writes)

        def stt(eng, out, in0, scalar, in1, op0, op1, reads, writes):
            P.op(eng, lambda e: e.scalar_tensor_tensor(out=out, in0=in0, scalar=scalar, in1=in1, op0=op0, op1=op1),
                 reads=reads, writes=writes)

        def copy(eng, out, in_, reads, writes):
            P.op(eng, lambda e: e.tensor_copy(out=out, in_=in_), reads=reads, writes=writes)

        def dma_io(out, in_, reads, writes):
            P.dma("io", lambda e: e.dma_start(out=out, in_=in_), reads=reads, writes=writes)

        slabs = []

        def plan_slabs():
            for l in range(nl):
                for s in range(6):
                    slabs.append(w_in[l, :, s * 256:(s + 1) * 256].rearrange("(k p) c -> p k c", p=128))
                for j in range(4):
                    slabs.append(s5w[l, j, :, :])
                    slabs.append(s5wT[l, j, :, :])
                slabs.append(w_glu[l, :, :].rearrange("(k p) c -> p k c", p=128))
                for s in range(4):
                    slabs.append(w_out[l, :, s * 256:(s + 1) * 256].rearrange("(k p) c -> p k c", p=128))
                for fb in range(4):
                    for s in range(4):
                        c0 = fb * 1024 + s * 256
                        slabs.append(w_up[l, :, c0:c0 + 256].rearrange("(k p) c -> p k c", p=128))
                    for s in range(4):
                        slabs.append(w_down[l, fb * 1024:(fb + 1) * 1024, s * 256:(s + 1) * 256]
                                     .rearrange("(k p) c -> p k c", p=128))

        plan_slabs()
        ws = {"issued": 0, "got": 0}

        def ws_issue():
            i = ws["issued"]
            s, b = i % 2, i % 3
            src = slabs[i]
            if len(src.shape) == 3:
                nel = src.shape[1] * src.shape[2]
                dst = STG[:, s, 0:nel].rearrange("p (k c) -> p k c", c=src.shape[2])
            else:
                nel = src.shape[1]
                dst = STG[:, s, 0:nel]
            P.dma("w", lambda e: e.dma_start(out=dst, in_=src), writes=[f"stg{s}"])
            P.op("pool", lambda e: e.tensor_copy(out=WBF[:, b, 0:nel], in_=STG[:, s, 0:nel]),
                 reads=[f"stg{s}"], writes=[f"wbf{b}"])
            ws["issued"] += 1

        def ws_get(keep=0):
            while ws["issued"] < min(len(slabs), ws["got"] - keep + 3):
                ws_issue()
            i = ws["got"]
            ws["got"] += 1
            return i % 3

        def wv(b, ncols):
            return WBF[:, b, :].rearrange("p (k c) -> p k c", c=ncols)

        for kt in range(8):
            dma_io(X[:, kt, :], xT[kt, :, :], [], [f"X{kt}_{t}" for t in range(5)])
        dma_io(GF[:, :], vec_d[:, NVL * DEPTH:NVL * DEPTH + 8], [], ["GF"])
        dma_io(S5P[:, :], s5p_d[:, :], [], ["S5P"])
        dma_io(TMPF[:, 0, 0:128], ident_d[:, :], [], ["TMPF0"])
        P.dma("io", lambda e: e.dma_start(out=cvc_d[0:nl, :, :, :], in_=sconv_d[0:nl, :, 4:30, :]))
        copy("pool", IDB[:, :], TMPF[:, 0, 0:128], ["TMPF0"], ["IDB"])
        ts_op("pool", NIDB[:, :], TMPF[:, 0, 0:128], -1.0, ALU.mult, ["TMPF0"], ["NIDB"])
        P.op("pool", lambda e: e.memset(ONB[:, :], 1.0), writes=["ONB"])
        P.op("pool", lambda e: e.memset(HP[:, :, 0:1], 0.0), writes=["HP"])
        P.op("pool", lambda e: e.memset(LC8[:, :, :, :], 0.0), writes=["LC8"])
        for j in range(4):
            P.op("pool", lambda e, j=j: e.memset(Z[:, j, 0:30], 0.0), writes=[f"Zpad{j}"])

        def rmsnorm(gfn, gres, to_x):
            for kt in range(8):
                for tt in range(5):
                    sq = U[:, kt % 2, tsl(tt)]
                    act(sq, X[:, kt, tsl(tt)], AF.Square, [f"X{kt}_{tt}"], [f"U{kt % 2}_{tt}"])
                    n = TT[tt][1]
                    mm(PS[tt][:, 0:n], ONB[:, :], sq, kt == 0, kt == 7, ["ONB", f"U{kt % 2}_{tt}"], tt)
            for tt in range(5):
                n = TT[tt][1]
                act(S1[:, tsl(tt)], PS[tt][:, 0:n], AF.Ln, [f"ps{tt}"], [f"S1_{tt}"], bias=EPS, scale=1.0 / D)
                act(S1[:, tsl(tt)], S1[:, tsl(tt)], AF.Exp, [f"S1_{tt}"], [f"S1_{tt}"], scale=-0.5)
            for kt in range(8):
                for tt in range(5):
                    g = gfn(kt)
                    if to_x:
                        stt("dve", X[:, kt, tsl(tt)], X[:, kt, tsl(tt)], g, S1[:, tsl(tt)], ALU.mult, ALU.mult,
                            [f"X{kt}_{tt}", f"S1_{tt}", gres], [f"X{kt}_{tt}"])
                    else:
                        stt("dve", A[:, kt, tsl(tt)], X[:, kt, tsl(tt)], g, S1[:, tsl(tt)], ALU.mult, ALU.mult,
                            [f"X{kt}_{tt}", f"S1_{tt}", gres], [f"A{kt}_{tt}"])

        def dense_group(lhs_fn, rhs_fn, nk, tt, reads):
            b = next_bank()
            n = TT[tt][1]
            for kt in range(nk):
                mm(PS[b][:, 0:n], lhs_fn(kt), rhs_fn(kt, tt), kt == 0, kt == nk - 1, reads(kt, tt), b)
            return b, PS[b][:, 0:n]

        def tb(i):
            return TB[:, i, :]

        (DT, RHO, TH, R1, Y, K, FR, SIN1, COS1, QRE, QIM, T0_, T1_, RT0) = range(14)
        YC, KC, FRC = Y, K, FR
        NRE, DEN, INV = Y, K, FR
        ER = lambda n: 14 + n
        EI = lambda n: 23 + n
        NEI = lambda n: 32 + n
        QR = lambda n: 41 + n
        QI = lambda n: 49 + n
        NQI = lambda n: 57 + n
        RQR, RQI, RNQI = 65, 69, 73
        E1R, E1I = ER(1), EI(1)

        def tbo(eng, kind, out_i, *a):
            r = ["S5P", "TB"]
            w = ["TB"]
            if kind == "tt":
                tt_op(eng, tb(out_i), a[0], a[1], a[2], r, w)
            elif kind == "ts":
                ts_op(eng, tb(out_i), a[0], a[1], a[2], r, w)

        def cmul(out_r, out_i, ar, ai, br, bi):
            tbo("dve", "tt", T0_, tb(ar), tb(br), ALU.mult)
            tbo("dve", "tt", T1_, tb(ai), tb(bi), ALU.mult)
            tbo("dve", "tt", out_r, tb(T0_), tb(T1_), ALU.subtract)
            tbo("dve", "tt", T0_, tb(ar), tb(bi), ALU.mult)
            tbo("dve", "tt", T1_, tb(ai), tb(br), ALU.mult)
            tbo("dve", "tt", out_i, tb(T0_), tb(T1_), ALU.add)

        def s5_tables(l):
            sp0 = l * 48
            ARE, AIM, LDT = S5P[:, sp0:sp0 + 16], S5P[:, sp0 + 16:sp0 + 32], S5P[:, sp0 + 32:sp0 + 48]
            act(tb(DT), LDT, AF.Exp, ["S5P"], ["TB"])
            tbo("dve", "tt", RHO, tb(DT), ARE, ALU.mult)
            tbo("dve", "tt", TH, tb(DT), AIM, ALU.mult)
            act(tb(R1), tb(RHO), AF.Exp, ["TB"], ["TB"])
            act(tb(RT0), tb(RHO), AF.Exp, ["TB"], ["TB"], scale=float(T0C))
            tbo("dve", "ts", Y, tb(TH), 1.0 / TWO_PI, ALU.mult)
            tbo("dve", "ts", K, tb(Y), MAGIC, ALU.add)
            tbo("dve", "ts", K, tb(K), MAGIC, ALU.subtract)
            tbo("dve", "tt", FR, tb(Y), tb(K), ALU.subtract)
            act(tb(SIN1), tb(FR), AF.Sin, ["TB"], ["TB"], scale=TWO_PI * (1.0 - 1e-6))
            tbo("dve", "ts", YC, tb(Y), 0.25, ALU.add)
            tbo("dve", "ts", KC, tb(YC), MAGIC, ALU.add)
            tbo("dve", "ts", KC, tb(KC), MAGIC, ALU.subtract)
            tbo("dve", "tt", FRC, tb(YC), tb(KC), ALU.subtract)
            act(tb(COS1), tb(FRC), AF.Sin, ["TB"], ["TB"], scale=TWO_PI * (1.0 - 1e-6))
            tbo("dve", "tt", E1R, tb(R1), tb(COS1), ALU.mult)
            tbo("dve", "tt", E1I, tb(R1), tb(SIN1), ALU.mult)
            tbo("dve", "ts", NRE, tb(E1R), -1.0, ALU.add)
            tbo("dve", "tt", T0_, ARE, ARE, ALU.mult)
            tbo("dve", "tt", T1_, AIM, AIM, ALU.mult)
            tbo("dve", "tt", DEN, tb(T0_), tb(T1_), ALU.add)
            P.op("dve", lambda e: e.reciprocal(out=tb(INV), in_=tb(DEN)), reads=["TB"], writes=["TB"])
            tbo("dve", "tt", T0_, tb(NRE), ARE, ALU.mult)
            tbo("dve", "tt", T1_, tb(E1I), AIM, ALU.mult)
            tbo("dve", "tt", T0_, tb(T0_), tb(T1_), ALU.add)
            tbo("dve", "tt", QRE, tb(T0_), tb(INV), ALU.mult)
            tbo("dve", "tt", T0_, tb(E1I), ARE, ALU.mult)
            tbo("dve", "tt", T1_, tb(NRE), AIM, ALU.mult)
            tbo("dve", "tt", T0_, tb(T0_), tb(T1_), ALU.subtract)
            tbo("dve", "tt", QIM, tb(T0_), tb(INV), ALU.mult)
            P.op("dve", lambda e: e.memset(tb(ER(0)), 1.0), reads=["TB"], writes=["TB"])
            P.op("dve", lambda e: e.memset(tb(EI(0)), 0.0), reads=["TB"], writes=["TB"])
            for n in range(1, T0C):
                cmul(ER(n + 1), EI(n + 1), ER(n), EI(n), E1R, E1I)
            for n in range(T0C):
                cmul(QR(n), QI(n), ER(n), EI(n), QRE, QIM)
                tbo("dve", "ts", NQI(n), tb(QI(n)), -1.0, ALU.mult)
            for n in range(T0C + 1):
                tbo("dve", "ts", NEI(n), tb(EI(n)), -1.0, ALU.mult)
            for t_ in range(4):
                copy("dve", tb(RQR + t_), tb(QR(3 - t_)), ["TB"], ["TB"])
                copy("dve", tb(RQI + t_), tb(QI(3 - t_)), ["TB"], ["TB"])
                copy("dve", tb(RNQI + t_), tb(NQI(3 - t_)), ["TB"], ["TB"])
            copy("dve", PW[:, 0, 0, :], tb(COS1), ["TB"], ["PW"])
            copy("dve", PW[:, 1, 0, :], tb(SIN1), ["TB"], ["PW"])
            for lv in range(10):
                pr, pi = PW[:, 0, lv, :], PW[:, 1, lv, :]
                tt_op("dve", tb(T0_), pr, pr, ALU.mult, ["PW", "TB"], ["TB"])
                tt_op("dve", tb(T1_), pi, pi, ALU.mult, ["PW", "TB"], ["TB"])
                tt_op("dve", PW[:, 0, lv + 1, :], tb(T0_), tb(T1_), ALU.subtract, ["TB", "PW"], ["PW"])
                tt_op("dve", tb(T0_), pr, pi, ALU.mult, ["PW", "TB"], ["TB"])
                ts_op("dve", PW[:, 1, lv + 1, :], tb(T0_), 2.0, ALU.mult, ["TB", "PW"], ["PW"])


        P.capture = []
        s5_tables(0)
        tblq = P.capture
        P.capture = None
        for l in range(nl):
            vb = 0
            VEC = VEC2[:, l % 2, :]
            VR = f"VEC{l % 2}"
            dma_io(VEC2[:, l % 2, :], vec_d[:, l * NVL:(l + 1) * NVL], [], [VR])

            rmsnorm(lambda kt: VEC[:, kt:kt + 1], VR, False)

            dma_io(S2[:, 0:1920], zbuf_d[l, :, :], [], [f"S2_{t}" for t in range(4)])
            for j in range(4):
                src = S2[:, j * 480:(j + 1) * 480].rearrange("p (s c) -> p s c", c=30)
                copy("pool", Zs(j)[:, :, 0:30], src, [f"S2_{t}" for t in range(4)], [f"Z{j}_4"])

            for m in range(12):
                P.flush(tblq, 30)
                if m % 2 == 0:
                    wb = ws_get()
                wvw = wv(wb, 256)
                mc = slice((m % 2) * 128, (m % 2) * 128 + 128)
                bias = VEC[:, vb + 16 + m:vb + 17 + m]
                for tt in range(5):
                    b, ps = dense_group(lambda kt: wvw[:, kt, mc], lambda kt, tt: A[:, kt, tsl(tt)], 8, tt,
                                        lambda kt, tt: [f"wbf{wb}", f"A{kt}_{tt}"])
                    n = TT[tt][1]
                    if m < 4:
                        ti = T0C if tt < 4 else 4
                        ts_op("dve", U[:, m, tsl(tt)].rearrange("p (s k) -> p s k", s=ti),
                              ps.rearrange("p (k s) -> p s k", s=ti), bias, ALU.add, [f"ps{b}", VR], [f"U{m}_{tt}"])
                    elif m % 2 == 0:
                        act(S1[:, tsl(tt)], ps, AF.Sigmoid, [f"ps{b}", VR], [f"S1_{tt}"], bias=bias)
                    else:
                        j = (m - 5) // 2
                        if tt < 4:
                            stt("dve", Z[:, j, zsl(tt)], ps, bias, S1[:, tsl(tt)], ALU.add, ALU.mult,
                                [f"ps{b}", f"S1_{tt}", VR], [f"Z{j}_{tt}"])
                            if tt == 3:
                                stt("dve", CVP[:, j, :], ps[:, 482:512], bias, S1[:, 2018:2048], ALU.add, ALU.mult,
                                    [f"ps{b}", f"S1_{tt}", VR], ["CVP"])
                        else:
                            stt("dve", Zs(j)[:, :, 30:34], v3(ps, 4), bias, v3(S1[:, tsl(4)], 4), ALU.add, ALU.mult,
                                [f"ps{b}", f"S1_{tt}", VR], [f"Z{j}_4"])
                            stt("dve", CVS[:, j, :], ps, bias, S1[:, tsl(4)], ALU.add, ALU.mult,
                                [f"ps{b}", f"S1_{tt}", VR], ["CVS"])
            P.flush(tblq, 10 ** 6)
            dma_io(cvp_d[l, :, :], CVP[:, :, :].rearrange("p a b -> p (a b)"), ["CVP"], [])
            dma_io(cvs_d[l, :, :], CVS[:, :, :].rearrange("p a b -> p (a b)"), ["CVS"], [])

            NCH = NP_TOK // T0C
            CPT = 512 // T0C
            LG = T0C.bit_length() - 1
            SRE, SIM_ = SACC[:, 0, :], SACC[:, 1, :]
            DRE, DIM_, TMP, GRE, GIM = (S1[:, i * NCH:(i + 1) * NCH] for i in range(5))
            rS1 = ["S1_0", "S1_1", "S1_2"]
            T8A = S1[:, 1280:1536].rearrange("p (n w) -> p n w", w=32)
            T8B = S1[:, 1536:1792].rearrange("p (n w) -> p n w", w=32)
            rT8 = ["S1_2", "S1_3"]

            def coef8(dst, dres, cre_w, cim_w, wres, i_re, i_im, i_nim, pi_):
                def tab(i):
                    return TB[:, i:i + T0C, pi_].unsqueeze(2).broadcast_to([128, T0C, 32])
                crb = cre_w.unsqueeze(1).broadcast_to([128, T0C, 32])
                cib = cim_w.unsqueeze(1).broadcast_to([128, T0C, 32])
                rd = [wres, "TB"]
                tt_op("dve", T8A, crb, tab(i_re), ALU.mult, rd, rT8)
                tt_op("dve", T8B, cib, tab(i_im), ALU.mult, rd, rT8)
                tt_op("dve", dst[:, 0, :, :], T8A, T8B, ALU.subtract, rT8, [dres])
                tt_op("dve", T8A, crb, tab(i_nim), ALU.mult, rd, rT8)
                tt_op("dve", T8B, cib, tab(i_re), ALU.mult, rd, rT8)
                tt_op("dve", dst[:, 1, :, :], T8A, T8B, ALU.subtract, rT8, [dres])

            for j in range(4):
                wa = ws_get()
                wb2 = ws_get(keep=1)
                for k_ in range(2):
                    dma_io(H0[:, k_, :, :], h0_d[l, :, k_, 4 * j:4 * j + 4, :], [], ["H0"])
                rwa, rwb = f"wbf{wa}", f"wbf{wb2}"

                def blk(pp, i):
                    return WBF[:, wa, pp * 512 + i * 128:pp * 512 + (i + 1) * 128]

                def blkT(pp, i):
                    return WBF[:, wb2, pp * 256 + i * 128:pp * 256 + (i + 1) * 128]

                WCv = S2[:, 0:2048].rearrange("p (a b k) -> p a b k", a=4, b=2)
                rWC = ["S2_0", "S2_1", "S2_2", "S2_3"]
                def pair_stage(pp, stage):
                    pi_ = 4 * j + pp
                    sc = lambda i: TB[:, i, pi_:pi_ + 1]
                    bre, bim, cre, cim = blk(pp, 0), blk(pp, 1), blk(pp, 2), blk(pp, 3)
                    last = (pp == 3)
                    WCR, WCI = WCv[:, pp, 0, :], WCv[:, pp, 1, :]
                    h0r, h0i = H0[:, 0, pp, :], H0[:, 1, pp, :]
                    if stage in ("B0", "B1"):
                        ubv = U[:, j, 0:NP_TOK].rearrange("p (t s k) -> p t s k", t=4, s=T0C)
                        usv = U[:, j, tsl(4)].rearrange("p (t q) -> p t q", t=4)
                        nbu = [0]
                        BW = TMPF[:, :, :].rearrange("p a b -> p (a b)").bitcast(BF16).rearrange("p (a b c) -> p a b c", a=2, b=2)

                        def bu_mm(s_):
                            b = 5 + s_ % 2
                            rhs = ubv[:, :, s_, :]
                            ures = [f"U{j}_{t}" for t in range(4)]
                            mm(PS[b][:, 0:NCH], bre, rhs, True, True, [rwa] + ures, b)
                            mm(PS[b][:, 256:256 + NCH], bim, rhs, True, True, [rwa] + ures, b)

                        def bu_evac(s_):
                            b = 5 + s_ % 2
                            f = s_ % 2
                            n_e = T0C - 1 - s_
                            o1, o2 = BW[:, f, 0, :], BW[:, f, 1, :]
                            act(o1, PS[b][:, :], AF.Copy, [f"ps{b}", "TB"], [f"TMPF{f}"], scale=sc(QR(n_e)))
                            act(o2, PS[b][:, :], AF.Copy, [f"ps{b}", "TB"], [f"TMPF{f}"], scale=sc(QI(n_e)))

                        def bu_sacc(s_):
                            f = s_ % 2
                            o1, o2 = BW[:, f, 0, :], BW[:, f, 1, :]
                            rd = [f"TMPF{f}", "IDB", "NIDB"]
                            mm(PS[7][:, 0:NCH], IDB[:, :], o1[:, 0:NCH], s_ == 0, False, rd, 7)
                            mm(PS[7][:, 0:NCH], NIDB[:, :], o2[:, 256:256 + NCH], False, False, rd, 7)
                            mm(PS[7][:, 256:256 + NCH], IDB[:, :], o2[:, 0:NCH], False, False, rd, 7)
                            mm(PS[7][:, 256:256 + NCH], IDB[:, :], o1[:, 256:256 + NCH], False, s_ == T0C - 1, rd, 7)

                        def bu_step(rhs, ncol, dst_re, dst_im, n_e, first, ures):
                            b = 6
                            p_re, p_im = PS[b][:, 0:ncol], PS[b][:, 256:256 + ncol]
                            mm(p_re, bre, rhs, True, True, [rwa] + ures, b)
                            mm(p_im, bim, rhs, True, True, [rwa] + ures, b)
                            rb = [f"ps{b}", "TB", "SACC"]
                            if first:
                                ts_op("dve", dst_re, p_re, sc(QR(n_e)), ALU.mult, rb, ["SACC"])
                                ts_op("dve", dst_im, p_re, sc(QI(n_e)), ALU.mult, rb, ["SACC"])
                            else:
                                stt("dve", dst_re, p_re, sc(QR(n_e)), dst_re, ALU.mult, ALU.add, rb, ["SACC"])
                                stt("dve", dst_im, p_re, sc(QI(n_e)), dst_im, ALU.mult, ALU.add, rb, ["SACC"])
                            stt("dve", dst_re, p_im, sc(NQI(n_e)), dst_re, ALU.mult, ALU.add, rb, ["SACC"])
                            stt("dve", dst_im, p_im, sc(QR(n_e)), dst_im, ALU.mult, ALU.add, rb, ["SACC"])

                        if stage == "B0":
                            bu_mm(0)
                            bu_mm(1)
                            bu_evac(0)
                            bu_evac(1)
                        else:
                            for s_ in range(T0C):
                                bu_sacc(s_)
                                if s_ + 2 < T0C:
                                    bu_mm(s_ + 2)
                                    bu_evac(s_ + 2)
                    elif stage == "SMP":
                        mm(PS[6][:, 0:NS_TOK], bre, U[:, j, tsl(4)], True, True, [rwa, f"U{j}_4"], 6)
                        mm(PS[6][:, 256:256 + NS_TOK], bim, U[:, j, tsl(4)], True, True, [rwa, f"U{j}_4"], 6)
                        copy("dve", SBS[:, :, :], PS[6][:, :].rearrange("p (a k) -> p a k", a=2)[:, :, 0:NS_TOK], ["ps6"], ["SBS"])
                        ss_re, ss_im = SSO[:, 0, pp, 1:17], SSO[:, 1, pp, 1:17]
                        b_re = SBS[:, 0, :].rearrange("p (t q) -> p t q", t=4)
                        b_im = SBS[:, 1, :].rearrange("p (t q) -> p t q", t=4)
                        P1, P2 = S1[:, 1792:1856], S1[:, 1856:1920]
                        p1v = P1.rearrange("p (t q) -> p t q", t=4)
                        p2v = P2.rearrange("p (t q) -> p t q", t=4)

                        def tabr(i0_):
                            return TB[:, i0_:i0_ + 4, pi_].unsqueeze(2).broadcast_to([128, 4, NSEQ])

                        rb = ["SBS", "TB", "S1_3"]
                        for dst, ta, tb2 in ((ss_re, RQR, RNQI), (ss_im, RQI, RQR)):
                            tt_op("dve", p1v, b_re, tabr(ta), ALU.mult, rb, ["S1_3"])
                            tt_op("dve", p2v, b_im, tabr(tb2), ALU.mult, rb, ["S1_3"])
                            tt_op("dve", P1, P1, P2, ALU.add, ["S1_3"], ["S1_3"])
                            P.op("dve", lambda e, dst=dst: e.tensor_reduce(
                                out=dst, in_=P1.rearrange("p (t q) -> p q t", t=4), axis=mybir.AxisListType.X, op=ALU.add),
                                reads=["S1_3"], writes=["SSO"])
                        stt("dve", ss_re, h0r, sc(ER(4)), ss_re, ALU.mult, ALU.add, ["H0", "TB", "SSO"], ["SSO"])
                        stt("dve", ss_re, h0i, sc(NEI(4)), ss_re, ALU.mult, ALU.add, ["H0", "TB", "SSO"], ["SSO"])
                        stt("dve", ss_im, h0r, sc(EI(4)), ss_im, ALU.mult, ALU.add, ["H0", "TB", "SSO"], ["SSO"])
                        stt("dve", ss_im, h0i, sc(ER(4)), ss_im, ALU.mult, ALU.add, ["H0", "TB", "SSO"], ["SSO"])
                    elif stage == "ROTC":
                        copy("dve", SACC[:, :, 0:NCH], PS[7][:, :].rearrange("p (a k) -> p a k", a=2)[:, :, 0:NCH], ["ps7"], ["SACC"])
                    elif stage == "ROT":
                        s_re, s_im = SACC[:, 0, 0:NCH], SACC[:, 1, 0:NCH]
                        tt_op("dve", DRE, WCR, s_re, ALU.mult, rWC + ["SACC"], ["S1_0"])
                        tt_op("dve", TMP, WCI, s_im, ALU.mult, rWC + ["SACC"], ["S1_1"])
                        tt_op("dve", DRE, DRE, TMP, ALU.add, ["S1_0", "S1_1"], ["S1_0"])
                        tt_op("dve", DIM_, WCR, s_im, ALU.mult, rWC + ["SACC"], ["S1_0"])
                        tt_op("dve", TMP, WCI, s_re, ALU.mult, rWC + ["SACC"], ["S1_1"])
                        tt_op("dve", DIM_, DIM_, TMP, ALU.subtract, ["S1_0", "S1_1"], ["S1_0"])
                    elif stage == "SCAN":
                        act(RT[:, :], WCR, AF.Identity, rWC + ["TB"], ["RT"], bias=sc(RT0), scale=0.0)
                        P.op("dve", lambda e: e.tensor_tensor_scan(out=GRE, data0=RT[:, :], data1=DRE, initial=0.0,
                                                                   op0=ALU.mult, op1=ALU.add),
                             reads=["RT", "S1_0"], writes=["S1_1"])
                        P.op("dve", lambda e: e.tensor_tensor_scan(out=GIM, data0=RT[:, :], data1=DIM_, initial=0.0,
                                                                   op0=ALU.mult, op1=ALU.add),
                             reads=["RT", "S1_0"], writes=["S1_2"])
                        e_ = NCH - 1
                        tt_op("dve", CAR[:, 2:3], WCI[:, e_:e_ + 1], GIM[:, e_:e_ + 1], ALU.mult, rWC + ["S1_2", "CAR"], ["CAR"])
                        stt("dve", SSO[:, 0, pp, 0:1], WCR[:, e_:e_ + 1], GRE[:, e_:e_ + 1], CAR[:, 2:3], ALU.mult, ALU.subtract,
                            rWC + ["S1_1", "CAR"], ["SSO"])
                        tt_op("dve", CAR[:, 3:4], WCR[:, e_:e_ + 1], GIM[:, e_:e_ + 1], ALU.mult, rWC + ["S1_2", "CAR"], ["CAR"])
                        stt("dve", SSO[:, 1, pp, 0:1], WCI[:, e_:e_ + 1], GRE[:, e_:e_ + 1], CAR[:, 3:4], ALU.mult, ALU.add,
                            rWC + ["S1_1", "CAR"], ["SSO"])
                        tt_op("dve", DRE, WCR, GRE, ALU.mult, rWC + ["S1_1"], ["S1_0"])
                        tt_op("dve", TMP, WCI, GIM, ALU.mult, rWC + ["S1_2"], ["S1_1"])
                        tt_op("dve", HP[:, 0, 1:NCH + 1], DRE, TMP, ALU.subtract, ["S1_0", "S1_1"], ["HP"])
                        tt_op("dve", DIM_, WCI, GRE, ALU.mult, rWC + ["S1_1"], ["S1_0"])
                        tt_op("dve", TMP, WCR, GIM, ALU.mult, rWC + ["S1_2"], ["S1_1"])
                        tt_op("dve", HP[:, 1, 1:NCH + 1], DIM_, TMP, ALU.add, ["S1_0", "S1_1"], ["HP"])
                    elif stage == "C":
                        copy("dve", H0B[:, 0, :], h0r, ["H0"], ["H0B"])
                        copy("dve", H0B[:, 1, :], h0i, ["H0"], ["H0B"])
                        if 'c' in DBG:
                            return
                        w0 = 32 * pp
                        wz = 32 * ((pp - 1) % 4)
                        P.op("dve", lambda e, wz=wz: e.memset(LC8[:, :, :, wz:wz + 32], 0.0), writes=["LC8"])
                        coef8(LC8[:, :, :, w0:w0 + 32], "LC8", cre[:, w0:w0 + 32], cim[:, w0:w0 + 32], rwa,
                              ER(1), EI(1), NEI(1), pi_)
                        for jj in range(T0C):
                            fin = last and jj == T0C - 1
                            for tt in range(4):
                                pv = PS[tt][:, jj * CPT:(jj + 1) * CPT]
                                mm(pv, LC8[:, 0, jj, :], HP[:, 0, tt * CPT:(tt + 1) * CPT], False, False, ["LC8", "HP"], tt)
                                mm(pv, LC8[:, 1, jj, :], HP[:, 1, tt * CPT:(tt + 1) * CPT], False, fin, ["LC8", "HP"], tt)
                            if jj < 4:
                                pv = PS[4][:, jj * NSEQ:(jj + 1) * NSEQ]
                                mm(pv, LC8[:, 0, jj, :], H0B[:, 0, :], False, False, ["LC8", "H0B"], 4)
                                mm(pv, LC8[:, 1, jj, :], H0B[:, 1, :], False, last and jj == 3, ["LC8", "H0B"], 4)

                pair_stage(0, "B0")
                pair_stage(0, "B1")
                pair_stage(1, "B0")
                SCR = S1[:, 1280:1792].rearrange("p (a k) -> p a k", a=4)
                copy("dve", WCv[:, :, 0, 0:1], PW[:, 0, LG, 4 * j:4 * j + 4].unsqueeze(2), ["PW"], rWC)
                copy("dve", WCv[:, :, 1, 0:1], PW[:, 1, LG, 4 * j:4 * j + 4].unsqueeze(2), ["PW"], rWC)
                lv = 0
                while (1 << lv) < NCH and 'd' not in DBG:
                    m_ = 1 << lv
                    prb = PW[:, 0, LG + lv, 4 * j:4 * j + 4].unsqueeze(2).broadcast_to([128, 4, m_])
                    pib = PW[:, 1, LG + lv, 4 * j:4 * j + 4].unsqueeze(2).broadcast_to([128, 4, m_])
                    sR, sI = WCv[:, :, 0, 0:m_], WCv[:, :, 1, 0:m_]
                    dR, dI = WCv[:, :, 0, m_:2 * m_], WCv[:, :, 1, m_:2 * m_]
                    tt_op("dve", SCR[:, :, 0:m_], sI, pib, ALU.mult, rWC + ["PW"], rT8)
                    tt_op("dve", dR, sR, prb, ALU.mult, rWC + ["PW"], rWC)
                    tt_op("dve", dR, dR, SCR[:, :, 0:m_], ALU.subtract, rWC + rT8, rWC)
                    tt_op("dve", SCR[:, :, 0:m_], sR, pib, ALU.mult, rWC + ["PW"], rT8)
                    tt_op("dve", dI, sI, prb, ALU.mult, rWC + ["PW"], rWC)
                    tt_op("dve", dI, dI, SCR[:, :, 0:m_], ALU.add, rWC + rT8, rWC)
                    lv += 1
                for pp in range(4 if 't' not in DBG else 0):
                    pi_ = 4 * j + pp
                    w0 = 32 * pp
                    coef8(LC8[:, :, :, 0:32], "LC8", blk(pp, 2)[:, w0:w0 + 32], blk(pp, 3)[:, w0:w0 + 32], rwa, QR(0), QI(0), NQI(0), pi_)
                    for tau in range(T0C):
                        pst = PS[tau // 4][:, (tau % 4) * 128 + w0:(tau % 4) * 128 + w0 + 32]
                        mm(pst, blkT(pp, 0), LC8[:, 0, tau, 0:32], True, False, [rwb, "LC8"], tau // 4)
                        mm(pst, blkT(pp, 1), LC8[:, 1, tau, 0:32], False, True, [rwb, "LC8"], tau // 4)
                for tau in range(T0C if 't' not in DBG else 0):
                    pst = PS[tau // 4][:, (tau % 4) * 128:(tau % 4 + 1) * 128]
                    act(TAP[:, tau, :], pst, AF.Copy, [f"ps{tau // 4}"], [f"TAP{tau}"])
                    if tau == 0:
                        stt("dve", TAP[:, 0, :], IDB[:, :], VEC[:, vb + 28 + j:vb + 29 + j], TAP[:, 0, :], ALU.mult, ALU.add,
                            ["IDB", VR, "TAP0"], ["TAP0"])
                for tt in range(5):
                    t0, n = TT[tt]
                    ti = T0C if tt < 4 else 4
                    cw = n // ti
                    for tau in range(ti):
                        mm(PS[tt][:, tau * cw:n], TAP[:, tau, :], U[:, j, t0:t0 + (ti - tau) * cw], tau == 0, False,
                           [f"TAP{tau}", f"U{j}_{tt}"], tt)

                pair_stage(0, "ROTC")
                pair_stage(0, "SMP")
                pair_stage(0, "ROT")
                for pp in range(4):
                    if pp < 3:
                        pair_stage(pp + 1, "B1")
                    pair_stage(pp, "SCAN")
                    if pp < 2:
                        pair_stage(pp + 2, "B0")
                    pair_stage(pp, "C")
                    if pp < 3:
                        pair_stage(pp + 1, "ROTC")
                        pair_stage(pp + 1, "SMP")
                        pair_stage(pp + 1, "ROT")
                for k_ in range(2):
                    dma_io(sso_d[l, :, k_, 4 * j:4 * j + 4, :], SSO[:, k_, :, :], ["SSO"], [])
                for tt in range(5):
                    n = TT[tt][1]
                    ti = T0C if tt < 4 else 4
                    act(A[:, j, tsl(tt)].rearrange("p (k s) -> p k s", s=ti),
                        PS[tt][:, 0:n].rearrange("p (s k) -> p k s", s=ti), AF.Gelu_apprx_tanh,
                        [f"ps{tt}"], [f"A{j}_{tt}"])
            bank_ctr[0] = 0

            wb = ws_get()
            wvw = wv(wb, 512)
            for m in range(4):
                bias = VEC[:, vb + 32 + m:vb + 33 + m]
                for tt in range(5):
                    b, ps = dense_group(lambda kt: wvw[:, kt, m * 128:(m + 1) * 128],
                                        lambda kt, tt: A[:, kt, tsl(tt)], 4, tt,
                                        lambda kt, tt: [f"wbf{wb}", f"A{kt}_{tt}"])
                    act(S1[:, tsl(tt)], ps, AF.Sigmoid, [f"ps{b}", VR], [f"S1_{tt}"], bias=bias)
                    tt_op("dve", U[:, m, tsl(tt)], A[:, m, tsl(tt)], S1[:, tsl(tt)], ALU.mult,
                          [f"A{m}_{tt}", f"S1_{tt}"], [f"U{m}_{tt}"])

            for j in range(4):
                banks = [next_bank() for _ in range(5)]
                for k in range(31):
                    d = k % 4
                    ts_op("dve", DG[:, d, :], IDB[:, :], VEC[:, vb + 48 + j * 31 + k:vb + 49 + j * 31 + k], ALU.mult,
                          ["IDB", VR], [f"DG{d}"])
                    for tt in range(4):
                        t0 = TT[tt][0]
                        rd = [f"DG{d}", f"Z{j}_{tt}", f"Zpad{j}"] + ([f"Z{j}_{tt - 1}"] if tt > 0 else [])
                        mm(PS[banks[tt]][:, :], DG[:, d, :], Z[:, j, t0 + k:t0 + k + 512], k == 0, k == 30, rd, banks[tt])
                    mm(v3(PS[banks[4]][:, 0:64], 4), DG[:, d, :], Zs(j)[:, :, k:k + 4], k == 0, k == 30,
                       [f"DG{d}", f"Z{j}_4"], banks[4])
                bias = VEC[:, vb + 36 + j:vb + 37 + j]
                for tt in range(5):
                    n = TT[tt][1]
                    act(A[:, 4 + j, tsl(tt)], PS[banks[tt]][:, 0:n], AF.Identity, [f"ps{banks[tt]}", VR],
                        [f"A{4 + j}_{tt}"], bias=bias)
            bank_ctr[0] = 0
            for tt in range(5):
                n = TT[tt][1]
                b, ps = dense_group(lambda kt: ONB[:, :], lambda kt, tt: A[:, 4 + kt, tsl(tt)], 4, tt,
                                    lambda kt, tt: ["ONB", f"A{4 + kt}_{tt}"])
                act(S1[:, tsl(tt)], ps, AF.Copy, [f"ps{b}"], [f"S1_{tt}"], scale=1.0 / 512)
            for j in range(4):
                for tt in range(5):
                    act(Z[:, j, tsl(tt)], A[:, 4 + j, tsl(tt)], AF.Square, [f"A{4 + j}_{tt}"], Zrow(j) + [f"Zpad{j}"])
            for tt in range(5):
                b, ps = dense_group(lambda kt: ONB[:, :], lambda kt, tt: Z[:, kt, tsl(tt)], 4, tt,
                                    lambda kt, tt: ["ONB"] + Zrow(kt))
                tt_op("dve", S2[:, tsl(tt)], S1[:, tsl(tt)], S1[:, tsl(tt)], ALU.mult, [f"S1_{tt}"], [f"S2_{tt}"])
                stt("dve", S2[:, tsl(tt)], ps, 1.0 / 512, S2[:, tsl(tt)], ALU.mult, ALU.subtract,
                    [f"ps{b}", f"S2_{tt}"], [f"S2_{tt}"])
                act(S2[:, tsl(tt)], S2[:, tsl(tt)], AF.Ln, [f"S2_{tt}"], [f"S2_{tt}"], bias=EPS)
                act(S2[:, tsl(tt)], S2[:, tsl(tt)], AF.Exp, [f"S2_{tt}"], [f"S2_{tt}"], scale=-0.5)
            tf = [0]
            for j in range(4):
                lg, lb = VEC[:, vb + 40 + j:vb + 41 + j], VEC[:, vb + 44 + j:vb + 45 + j]
                for tt in range(5):
                    n = TT[tt][1]
                    f = tf[0] % 2
                    tf[0] += 1
                    tmp = TMPF[:, f, 0:n]
                    tt_op("dve", tmp, A[:, 4 + j, tsl(tt)], S1[:, tsl(tt)], ALU.subtract,
                          [f"A{4 + j}_{tt}", f"S1_{tt}"], [f"TMPF{f}"])
                    tt_op("dve", tmp, tmp, S2[:, tsl(tt)], ALU.mult, [f"TMPF{f}", f"S2_{tt}"], [f"TMPF{f}"])
                    act(A[:, 4 + j, tsl(tt)], tmp, AF.Silu, [f"TMPF{f}", VR], [f"A{4 + j}_{tt}"], bias=lb, scale=lg)

            for m in range(8):
                if m % 2 == 0:
                    wb = ws_get()
                wvw = wv(wb, 256)
                mc = slice((m % 2) * 128, (m % 2) * 128 + 128)
                for tt in range(5):
                    b, ps = dense_group(lambda kt: wvw[:, kt, mc],
                                        lambda kt, tt: (U[:, kt, tsl(tt)] if kt < 4 else A[:, kt, tsl(tt)]), 8, tt,
                                        lambda kt, tt: [f"wbf{wb}", (f"U{kt}_{tt}" if kt < 4 else f"A{kt}_{tt}")])
                    tt_op("dve", X[:, m, tsl(tt)], X[:, m, tsl(tt)], ps, ALU.add, [f"X{m}_{tt}", f"ps{b}"], [f"X{m}_{tt}"])

            rmsnorm(lambda kt: VEC[:, 8 + kt:9 + kt], VR, False)
            bank_ctr[0] = 0

            def Hh(m, tt):
                return U[:, m, tsl(tt)] if m < 4 else Z[:, m - 4, tsl(tt)]

            def Hr(m, tt):
                return [f"U{m}_{tt}"] if m < 4 else Zrow(m - 4) + [f"Zpad{m - 4}"]

            for fb in range(4):
                if fb == 0 and l + 1 < nl:
                    P.capture = []
                    s5_tables(l + 1)
                    tblq = P.capture
                    P.capture = None
                for m in range(8):
                    P.flush(tblq, 6)
                    if m % 2 == 0:
                        wb = ws_get()
                    wvw = wv(wb, 256)
                    mc = slice((m % 2) * 128, (m % 2) * 128 + 128)
                    for tt in range(5):
                        n = TT[tt][1]
                        b, ps = dense_group(lambda kt: wvw[:, kt, mc], lambda kt, tt: A[:, kt, tsl(tt)], 8, tt,
                                            lambda kt, tt: [f"wbf{wb}", f"A{kt}_{tt}"])
                        f = tf[0] % 2
                        tf[0] += 1
                        tmp = TMPF[:, f, 0:n]
                        act(tmp, ps, AF.Relu, [f"ps{b}"], [f"TMPF{f}"])
                        act(Hh(m, tt), tmp, AF.Square, [f"TMPF{f}"], Hr(m, tt))
                for m in range(8):
                    P.flush(tblq, 6)
                    if m % 2 == 0:
                        wb = ws_get()
                    wvw = wv(wb, 256)
                    mc = slice((m % 2) * 128, (m % 2) * 128 + 128)
                    for tt in range(5):
                        b, ps = dense_group(lambda kt: wvw[:, kt, mc], lambda kt, tt: Hh(kt, tt), 8, tt,
                                            lambda kt, tt: [f"wbf{wb}"] + Hr(kt, tt))
                        tt_op("dve", X[:, m, tsl(tt)], X[:, m, tsl(tt)], ps, ALU.add,
                              [f"X{m}_{tt}", f"ps{b}"], [f"X{m}_{tt}"])
            P.flush(tblq, 10 ** 6)
            for j in range(4):
                P.op("pool", lambda e, j=j: e.memset(Z[:, j, 0:30], 0.0), writes=Zrow(j) + [f"Zpad{j}"])
            bank_ctr[0] = 0

        rmsnorm(lambda kt: GF[:, kt:kt + 1], "GF", True)
        for kt in range(8):
            dma_io(yT[kt, :, :], X[:, kt, :], [f"X{kt}_{t}" for t in range(5)], [])

        P.emit(block, sems)
    return nc


def _prep_shared(inp):
    f = np.float32
    nl = DEPTH
    order = list(range(0, 512))
    for j in range(4):
        order += list(range(1024 + 128 * j, 1024 + 128 * (j + 1)))
        order += list(range(512 + 128 * j, 512 + 128 * (j + 1)))
    order = np.array(order)
    w_in = np.ascontiguousarray(inp["w_in"][:, :, order])
    b_in = inp["b_in"][:, order]
    vec = np.zeros((128, NV), f)

    def cols(v, n):
        return np.asarray(v, f).reshape(n, 128).T

    for l in range(nl):
        b = l * NVL
        vec[:, b + 0:b + 8] = cols(inp["norm_mix_g"][l], 8)
        vec[:, b + 8:b + 16] = cols(inp["norm_mlp_g"][l], 8)
        vec[:, b + 16:b + 28] = cols(b_in[l], 12)
        vec[:, b + 28:b + 32] = cols(inp["ssm_d"][l], 4)
        vec[:, b + 32:b + 36] = cols(inp["b_glu"][l], 4)
        vec[:, b + 36:b + 40] = cols(inp["conv_b"][l], 4)
        vec[:, b + 40:b + 44] = cols(inp["conv_ln_g"][l], 4)
        vec[:, b + 44:b + 48] = cols(inp["conv_ln_b"][l], 4)
        cw = np.asarray(inp["conv_w"][l], f)
        vec[:, b + 48:b + 172] = cw.reshape(31, 4, 128).transpose(2, 1, 0).reshape(128, 124)
    vec[:, NVL * nl:NVL * nl + 8] = cols(inp["norm_f_g"], 8)

    s5p = np.zeros((128, nl * 48), f)
    for l in range(nl):
        for nm, off in (("ssm_a_re", 0), ("ssm_a_im", 16)):
            a = np.asarray(inp[nm][l], f).reshape(16, 2, 64)
            s5p[:, l * 48 + off:l * 48 + off + 16] = a.transpose(1, 2, 0).reshape(128, 16)
        ld = np.asarray(inp["ssm_log_dt"][l], f).reshape(16, 2)
        s5p[:, l * 48 + 32:l * 48 + 48] = np.repeat(ld.T[:, None, :], 64, axis=1).reshape(128, 16)

    s5w = np.zeros((nl, 4, 128, 2048), f)
    for l in range(nl):
        for pi in range(16):
            j, pp = pi // 4, pi % 4
            for x in range(2):
                g = 2 * pi + x
                gl = 2 * pp + x
                rows_gc = slice(gl * 16, gl * 16 + 16)
                cols_xp = slice(x * 64, x * 64 + 64)
                base = pp * 512
                s5w[l, j, rows_gc, base + 0 + x * 64:base + 0 + x * 64 + 64] = inp["ssm_b_re"][l, g].T
                s5w[l, j, rows_gc, base + 128 + x * 64:base + 128 + x * 64 + 64] = inp["ssm_b_im"][l, g].T
                s5w[l, j, cols_xp, base + 256 + gl * 16:base + 256 + gl * 16 + 16] = inp["ssm_c_re"][l, g].T
                s5w[l, j, cols_xp, base + 384 + gl * 16:base + 384 + gl * 16 + 16] = inp["ssm_c_im"][l, g].T
    s5wT = np.zeros((nl, 4, 128, 1024), f)
    for l in range(nl):
        for pi in range(16):
            j, pp = pi // 4, pi % 4
            for x in range(2):
                g = 2 * pi + x
                gl = 2 * pp + x
                s5wT[l, j, x * 64:x * 64 + 64, pp * 256 + gl * 16:pp * 256 + gl * 16 + 16] = inp["ssm_b_re"][l, g]
                s5wT[l, j, x * 64:x * 64 + 64, pp * 256 + 128 + gl * 16:pp * 256 + 128 + gl * 16 + 16] = inp["ssm_b_im"][l, g]
    return dict(s5wT=s5wT, w_in=w_in, w_glu=np.ascontiguousarray(inp["w_glu"], f), w_out=np.ascontiguousarray(inp["w_out"], f),
                w_up=np.ascontiguousarray(inp["w_up"], f), w_down=np.ascontiguousarray(inp["w_down"], f),
                s5w=s5w, vec=vec, s5p=s5p, ident=np.eye(128, dtype=f))


def _prep_core(inp, c):
    f = np.float32
    xa = np.concatenate([inp["x_prompt"][c], inp["x_sample"][NSEQ * c:NSEQ * (c + 1)].reshape(NS_TOK, D)], axis=0)
    xT = np.ascontiguousarray(xa.T.reshape(8, 128, NTOK), f)
    sl = slice(NSEQ * c, NSEQ * (c + 1))
    h0 = np.zeros((DEPTH, 128, 2, 16, 16), f)
    for k, nm in enumerate(("state_ssm_re", "state_ssm_im")):
        s = np.asarray(inp[nm][:, sl], f).reshape(DEPTH, NSEQ, 16, 2, 64)
        h0[:, :, k] = s.transpose(0, 3, 4, 2, 1).reshape(DEPTH, 128, 16, NSEQ)
    sc = np.asarray(inp["state_conv"][:, sl], f)
    zbuf = sc.reshape(DEPTH, NSEQ, 30, 4, 128).transpose(0, 4, 3, 1, 2)
    return dict(xT=xT, h0=np.ascontiguousarray(h0),
                zbuf=np.ascontiguousarray(zbuf.reshape(DEPTH, 128, 1920)), sconv=np.ascontiguousarray(sc))


_NC_CACHE = {}


def run_device(inp, nl=DEPTH):
    if nl not in _NC_CACHE:
        _NC_CACHE[nl] = build(nl)
    nc = _NC_CACHE[nl]
    shared = _prep_shared(inp)
    in_maps = []
    for c in range(NCORES):
        m = dict(shared)
        m.update(_prep_core(inp, c))
        in_maps.append(m)
    res = run_bass_kernel_spmd(nc, in_maps, core_ids=list(range(NCORES)))
    return res.results


def assemble(results, nl=DEPTH):
    f = np.float32
    y_p = np.zeros((NCORES, NP_TOK, D), f)
    y_s = np.zeros((NCORES * NSEQ, 4, D), f)
    re_p = np.zeros((nl, NCORES, 32, 64), f)
    im_p = np.zeros((nl, NCORES, 32, 64), f)
    cv_p = np.zeros((nl, NCORES, 30, 512), f)
    re_s = np.zeros((nl, NCORES * NSEQ, 32, 64), f)
    im_s = np.zeros((nl, NCORES * NSEQ, 32, 64), f)
    cv_s = np.zeros((nl, NCORES * NSEQ, 30, 512), f)
    for c, r in enumerate(results):
        y = np.asarray(r["yT"]).reshape(D, NTOK).T
        y_p[c] = y[:NP_TOK]
        y_s[NSEQ * c:NSEQ * (c + 1)] = y[NP_TOK:].reshape(NSEQ, 4, D)
        sso = np.asarray(r["sso"])[:nl].reshape(nl, 2, 64, 2, 16, 17)
        st = sso.transpose(0, 3, 5, 4, 1, 2).reshape(nl, 2, 17, 32, 64)
        re_p[:, c], im_p[:, c] = st[:, 0, 0], st[:, 1, 0]
        re_s[:, NSEQ * c:NSEQ * (c + 1)] = st[:, 0, 1:]
        im_s[:, NSEQ * c:NSEQ * (c + 1)] = st[:, 1, 1:]
        cvp = np.asarray(r["cvp"])[:nl].reshape(nl, 128, 4, 30)
        cv_p[:, c] = cvp.transpose(0, 3, 2, 1).reshape(nl, 30, 512)
        cvs = np.asarray(r["cvs"])[:nl].reshape(nl, 128, 4, NSEQ, 4)
        zs = cvs.transpose(0, 3, 4, 2, 1).reshape(nl, NSEQ, 4, 512)
        cvc = np.asarray(r["cvc"])[:nl]
        cv_s[:, NSEQ * c:NSEQ * (c + 1)] = np.concatenate([cvc, zs], axis=2)
    return (y_p, y_s, re_p, im_p, cv_p, re_s, im_s, cv_s)


def kernel(**inputs):
    inp = {k: np.asarray(v) for k, v in inputs.items()}
    return assemble(run_device(inp, DEPTH), DEPTH)
```

```python
import math
from contextlib import ExitStack
import numpy as np
import concourse.bass as bass
import concourse.mybir as mybir
from concourse.bass_utils import run_bass_kernel_spmd

F32 = mybir.dt.float32
BF16 = mybir.dt.bfloat16
AF = mybir.ActivationFunctionType
ALU = mybir.AluOpType

NCORES = 8
D = 1024
DEPTH = 4
NP_TOK = 2048
NSEQ = 16
NS_TOK = 64
NTOK = NP_TOK + NS_TOK
TT = [(0, 512), (512, 512), (1024, 512), (1536, 512), (2048, 64)]
ZW = 30 + NP_TOK + NSEQ * 34
ZS0 = 30 + NP_TOK
EPS = 1e-6
NVL = 172
NV = NVL * DEPTH + 8
MAGIC = 12582912.0
TWO_PI = 2.0 * math.pi
T0C = 8
import os
DBG = os.environ.get('DBG_SKIP', '')


class _Op:
    __slots__ = ("eng", "fn", "deps", "semkey", "inc", "signaled", "count", "idx")

    def __init__(self, eng, fn, semkey, inc, always):
        self.eng = eng
        self.fn = fn
        self.deps = set()
        self.semkey = semkey
        self.inc = inc
        self.signaled = always
        self.count = 0


class Prog:
    ENGS = ("pe", "act", "dve", "pool", "sp")

    def __init__(self):
        self.ops = []
        self.last_write = {}
        self.readers = {}
        self.dma_hist = {}
        self.capture = None

    def _add(self, op, reads, writes, after):
        idx = len(self.ops)
        op.idx = idx
        deps = set(after)
        for r in reads:
            lw = self.last_write.get(r)
            if lw is not None:
                deps.add(lw)
        for w in writes:
            lw = self.last_write.get(w)
            if lw is not None:
                deps.add(lw)
            for rd in self.readers.get(w, ()):
                deps.add(rd)
        deps.discard(idx)
        op.deps = deps
        self.ops.append(op)
        for r in reads:
            self.readers.setdefault(r, []).append(idx)
        for w in writes:
            self.last_write[w] = idx
            self.readers[w] = []
        return idx

    def op(self, eng, fn, reads=(), writes=(), after=()):
        if self.capture is not None:
            self.capture.append((eng, fn, tuple(reads), tuple(writes), tuple(after)))
            return None
        return self._add(_Op(eng, fn, eng, 1, False), reads, writes, after)

    def flush(self, queue, n):
        for _ in range(min(n, len(queue))):
            eng, fn, reads, writes, after = queue.pop(0)
            self._add(_Op(eng, fn, eng, 1, False), reads, writes, after)

    DMA_SLOTS = {"w": 2, "io": 8}

    def dma(self, stream, fn, reads=(), writes=(), after=()):
        hist = self.dma_hist.setdefault(stream, [])
        k = self.DMA_SLOTS[stream]
        n = len(hist)
        after = list(after)
        if n >= k:
            after.append(hist[n - k])
        idx = self._add(_Op("sp", fn, "dma:%s%d" % (stream, n % k), 16, True), reads, writes, after)
        hist.append(idx)
        return idx

    def _skip(self, p, eng):
        return p.eng == eng and (not p.semkey.startswith("dma:")) and eng == "pe"

    def emit(self, block, sems):
        ops = self.ops
        for o in ops:
            for d in o.deps:
                p = ops[d]
                if self._skip(p, o.eng):
                    continue
                p.signaled = True
        counts = {}
        for o in ops:
            if o.signaled:
                counts[o.semkey] = counts.get(o.semkey, 0) + o.inc
                o.count = counts[o.semkey]
        per_eng = {e: [] for e in self.ENGS}
        for o in ops:
            per_eng[o.eng].append(o)

        def run_engine(ename, eng):
            waited = {}
            for o in per_eng[ename]:
                need = {}
                for d in o.deps:
                    p = ops[d]
                    if not p.signaled or self._skip(p, ename):
                        continue
                    if p.count > need.get(p.semkey, 0):
                        need[p.semkey] = p.count
                for k, v in need.items():
                    if waited.get(k, 0) < v:
                        eng.wait_ge(sems[k], v)
                        waited[k] = v
                ins = o.fn(eng)
                if o.signaled:
                    ins.then_inc(sems[o.semkey], o.inc)
            return waited

        @block.tensor
        def _(e):
            run_engine("pe", e)

        @block.scalar
        def _(e):
            run_engine("act", e)

        @block.vector
        def _(e):
            run_engine("dve", e)

        @block.gpsimd
        def _(e):
            run_engine("pool", e)

        @block.sync
        def _(e):
            w = run_engine("sp", e)
            for k, v in counts.items():
                if k.startswith("dma:") and w.get(k, 0) < v:
                    e.wait_ge(sems[k], v)


def build(nl=DEPTH):
    nc = bass.Bass("TRN2", target_bir_lowering=False)

    def din(name, shape):
        return nc.dram_tensor(name, shape, F32, kind="ExternalInput").ap()

    def dout(name, shape):
        return nc.dram_tensor(name, shape, F32, kind="ExternalOutput").ap()

    xT = din("xT", [8, 128, NTOK])
    w_in = din("w_in", [DEPTH, D, 1536])
    w_glu = din("w_glu", [DEPTH, 512, 512])
    w_out = din("w_out", [DEPTH, D, D])
    w_up = din("w_up", [DEPTH, D, 4096])
    w_down = din("w_down", [DEPTH, 4096, D])
    s5w = din("s5w", [DEPTH, 4, 128, 2048])
    s5wT = din("s5wT", [DEPTH, 4, 128, 1024])
    vec_d = din("vec", [128, NV])
    s5p_d = din("s5p", [128, DEPTH * 48])
    h0_d = din("h0", [DEPTH, 128, 2, 16, 16])
    zbuf_d = din("zbuf", [DEPTH, 128, 4 * 16 * 30])
    sconv_d = din("sconv", [DEPTH, NSEQ, 30, 512])
    ident_d = din("ident", [128, 128])

    yT = dout("yT", [8, 128, NTOK])
    sso_d = dout("sso", [DEPTH, 128, 2, 16, 17])
    cvp_d = dout("cvp", [DEPTH, 128, 4 * 30])
    cvs_d = dout("cvs", [DEPTH, 128, 4 * 64])
    cvc_d = dout("cvc", [DEPTH, NSEQ, 26, 512])

    P = Prog()
    with ExitStack() as es:
        def sb(name, shape, dt):
            return es.enter_context(nc.sbuf_tensor(name, shape, dt))

        X = sb("X", [128, 8, NTOK], F32)
        A = sb("A", [128, 8, NTOK], BF16)
        U = sb("U", [128, 4, NTOK], BF16)
        Z = sb("Z", [128, 4, ZW], BF16)
        S1 = sb("S1", [128, NTOK], F32)
        S2 = sb("S2", [128, NTOK], F32)
        STG = sb("STG", [128, 2, 2048], F32)
        WBF = sb("WBF", [128, 3, 2048], BF16)
        TMPF = sb("TMPF", [128, 2, 512], F32)
        VEC2 = sb("VEC2", [128, 2, NVL], F32)
        GF = sb("GF", [128, 8], F32)
        S5P = sb("S5P", [128, DEPTH * 48], F32)
        TB = sb("TB", [128, 77, 16], F32)
        PW = sb("PW", [128, 2, 11, 16], F32)
        H0 = sb("H0", [128, 2, 4, 16], F32)
        SSO = sb("SSO", [128, 2, 4, 17], F32)
        CVP = sb("CVP", [128, 4, 30], F32)
        CVS = sb("CVS", [128, 4, 64], F32)
        SACC = sb("SACC", [128, 2, NP_TOK // T0C], F32)
        SBS = sb("SBS", [128, 2, NS_TOK], F32)
        TAP = sb("TAP", [128, T0C, 128], BF16)
        RT = sb("RT", [128, NP_TOK // T0C], F32)
        HP = sb("HP", [128, 2, NP_TOK // T0C + 1], BF16)
        H0B = sb("H0B", [128, 2, NSEQ], BF16)
        LC8 = sb("LC8", [128, 2, T0C, 128], BF16)
        CAR = sb("CAR", [128, 4], F32)
        IDB = sb("IDB", [128, 128], BF16)
        NIDB = sb("NIDB", [128, 128], BF16)
        ONB = sb("ONB", [128, 128], BF16)
        DG = sb("DG", [128, 4, 128], BF16)
        PS = [es.enter_context(nc.psum_tensor(f"ps{i}", [128, 512], F32)) for i in range(8)]

        sem_names = ["pe", "act", "dve", "pool"] + ["dma:w%d" % i for i in range(2)] + ["dma:io%d" % i for i in range(8)]
        sems = {k: es.enter_context(nc.semaphore("s_" + k.replace(":", "_"))) for k in sem_names}
        block = es.enter_context(nc.Block())

        def tsl(tt):
            t0, n = TT[tt]
            return slice(t0, t0 + n)

        def zsl(tt):
            t0, n = TT[tt]
            return slice(30 + t0, 30 + t0 + n)

        def v3(ap, inner):
            return ap.rearrange("p (s t) -> p s t", t=inner)

        def Zs(j):
            return Z[:, j, ZS0:ZW].rearrange("p (s c) -> p s c", c=34)

        def Zrow(j):
            return [f"Z{j}_{t}" for t in range(5)]

        bank_ctr = [0]

        def next_bank():
            b = bank_ctr[0] % 8
            bank_ctr[0] += 1
            return b

        def mm(ps_ap, lhsT, rhs, start, stop, reads, bank):
            wres = bank if isinstance(bank, str) else f"ps{bank}"
            P.op("pe", lambda e: e.matmul(ps_ap, lhsT=lhsT, rhs=rhs, start=start, stop=stop),
                 reads=reads, writes=[wres])

        def mmt(ps_ap, lhsT, rhs, start, stop, reads, bank, col0):
            P.op("pe", lambda e: e.matmul(ps_ap, lhsT=lhsT, rhs=rhs, start=start, stop=stop, tile_position=(0, col0)),
                 reads=reads, writes=[f"ps{bank}"])

        def act(out, in_, func, reads, writes, bias=None, scale=None):
            kw = {}
            if bias is not None:
                kw["bias"] = bias
            if scale is not None:
                kw["scale"] = scale
            P.op("act", lambda e: e.activation(out=out, in_=in_, func=func, **kw), reads=reads, writes=writes)

        def tt_op(eng, out, in0, in1, op, reads, writes):
            P.op(eng, lambda e: e.tensor_tensor(out=out, in0=in0, in1=in1, op=op), reads=reads, writes=writes)

        def ts_op(eng, out, in0, s1, op0, reads, writes, s2=None, op1=None):
            if op1 is None:
                P.op(eng, lambda e: e.tensor_scalar(out=out, in0=in0, scalar1=s1, scalar2=None, op0=op0),
                     reads=reads, writes=writes)
            else:
                P.op(eng, lambda e: e.tensor_scalar(out=out, in0=in0, scalar1=s1, scalar2=s2, op0=op0, op1=op1),
                     reads=reads, writes=writes)

        def stt(eng, out, in0, scalar, in1, op0, op1, reads, writes):
            P.op(eng, lambda e: e.scalar_tensor_tensor(out=out, in0=in0, scalar=scalar, in1=in1, op0=op0, op1=op1),
                 reads=reads, writes=writes)

        def copy(eng, out, in_, reads, writes):
            P.op(eng, lambda e: e.tensor_copy(out=out, in_=in_), reads=reads, writes=writes)

        def dma_io(out, in_, reads, writes):
            P.dma("io", lambda e: e.dma_start(out=out, in_=in_), reads=reads, writes=writes)

        slabs = []

        def plan_slabs():
            for l in range(nl):
                for s in range(6):
                    slabs.append(w_in[l, :, s * 256:(s + 1) * 256].rearrange("(k p) c -> p k c", p=128))
                for j in range(4):
                    s5_idx.add(len(slabs))
                    slabs.append(s5w[l, j, :, :])
                    s5_idx.add(len(slabs))
                    slabs.append(s5wT[l, j, :, :])
                slabs.append(w_glu[l, :, :].rearrange("(k p) c -> p k c", p=128))
                for s in range(4):
                    slabs.append(w_out[l, :, s * 256:(s + 1) * 256].rearrange("(k p) c -> p k c", p=128))
                for fb in range(4):
                    for s in range(4):
                        c0 = fb * 1024 + s * 256
                        slabs.append(w_up[l, :, c0:c0 + 256].rearrange("(k p) c -> p k c", p=128))
                    for s in range(4):
                        slabs.append(w_down[l, fb * 1024:(fb + 1) * 1024, s * 256:(s + 1) * 256]
                                     .rearrange("(k p) c -> p k c", p=128))

        s5_idx = set()
        plan_slabs()
        ws = {"issued": 0, "got": 0}
        pend = {}

        def ws_emit_pending(upto=None):
            for i in sorted(pend):
                if upto is None or i <= upto:
                    out_ap, in_ap, rd, wr = pend.pop(i)
                    P.op("act", lambda e, out_ap=out_ap, in_ap=in_ap: e.activation(out=out_ap, in_=in_ap, func=AF.Copy),
                         reads=rd, writes=wr)

        def ws_issue():
            i = ws["issued"]
            s, b = i % 2, i % 3
            src = slabs[i]
            if len(src.shape) == 3:
                nel = src.shape[1] * src.shape[2]
                dst = STG[:, s, 0:nel].rearrange("p (k c) -> p k c", c=src.shape[2])
            else:
                nel = src.shape[1]
                dst = STG[:, s, 0:nel]
            ws_emit_pending(i - 2)
            P.dma("w", lambda e: e.dma_start(out=dst, in_=src), writes=[f"stg{s}"])
            if i in s5_idx:
                pend[i] = (WBF[:, b, 0:nel], STG[:, s, 0:nel], [f"stg{s}"], [f"wbf{b}"])
            else:
                P.op("pool", lambda e: e.tensor_copy(out=WBF[:, b, 0:nel], in_=STG[:, s, 0:nel]),
                     reads=[f"stg{s}"], writes=[f"wbf{b}"])
            ws["issued"] += 1

        def ws_get(keep=0):
            while ws["issued"] < min(len(slabs), ws["got"] - keep + 3):
                ws_issue()
            i = ws["got"]
            ws["got"] += 1
            ws_emit_pending(i)
            return i % 3

        def wv(b, ncols):
            return WBF[:, b, :].rearrange("p (k c) -> p k c", c=ncols)

        for kt in range(8):
            dma_io(X[:, kt, :], xT[kt, :, :], [], [f"X{kt}_{t}" for t in range(5)])
        dma_io(GF[:, :], vec_d[:, NVL * DEPTH:NVL * DEPTH + 8], [], ["GF"])
        dma_io(S5P[:, :], s5p_d[:, :], [], ["S5P"])
        dma_io(TMPF[:, 0, 0:128], ident_d[:, :], [], ["TMPF0"])
        P.dma("io", lambda e: e.dma_start(out=cvc_d[0:nl, :, :, :], in_=sconv_d[0:nl, :, 4:30, :]))
        copy("pool", IDB[:, :], TMPF[:, 0, 0:128], ["TMPF0"], ["IDB"])
        ts_op("pool", NIDB[:, :], TMPF[:, 0, 0:128], -1.0, ALU.mult, ["TMPF0"], ["NIDB"])
        P.op("pool", lambda e: e.memset(ONB[:, :], 1.0), writes=["ONB"])
        P.op("pool", lambda e: e.memset(HP[:, :, 0:1], 0.0), writes=["HP"])
        P.op("pool", lambda e: e.memset(LC8[:, :, :, :], 0.0), writes=["LC8"])
        for j in range(4):
            P.op("pool", lambda e, j=j: e.memset(Z[:, j, 0:30], 0.0), writes=[f"Zpad{j}"])

        def rmsnorm(gfn, gres, to_x):
            for kt in range(8):
                for tt in range(5):
                    sq = U[:, kt % 2, tsl(tt)]
                    act(sq, X[:, kt, tsl(tt)], AF.Square, [f"X{kt}_{tt}"], [f"U{kt % 2}_{tt}"])
                    n = TT[tt][1]
                    mm(PS[tt][:, 0:n], ONB[:, :], sq, kt == 0, kt == 7, ["ONB", f"U{kt % 2}_{tt}"], tt)
            for tt in range(5):
                n = TT[tt][1]
                act(S1[:, tsl(tt)], PS[tt][:, 0:n], AF.Ln, [f"ps{tt}"], [f"S1_{tt}"], bias=EPS, scale=1.0 / D)
                act(S1[:, tsl(tt)], S1[:, tsl(tt)], AF.Exp, [f"S1_{tt}"], [f"S1_{tt}"], scale=-0.5)
            for kt in range(8):
                for tt in range(5):
                    g = gfn(kt)
                    if to_x:
                        stt("dve", X[:, kt, tsl(tt)], X[:, kt, tsl(tt)], g, S1[:, tsl(tt)], ALU.mult, ALU.mult,
                            [f"X{kt}_{tt}", f"S1_{tt}", gres], [f"X{kt}_{tt}"])
                    else:
                        stt("dve", A[:, kt, tsl(tt)], X[:, kt, tsl(tt)], g, S1[:, tsl(tt)], ALU.mult, ALU.mult,
                            [f"X{kt}_{tt}", f"S1_{tt}", gres], [f"A{kt}_{tt}"])

        def dense_group(lhs_fn, rhs_fn, nk, tt, reads):
            b = next_bank()
            n = TT[tt][1]
            for kt in range(nk):
                mm(PS[b][:, 0:n], lhs_fn(kt), rhs_fn(kt, tt), kt == 0, kt == nk - 1, reads(kt, tt), b)
            return b, PS[b][:, 0:n]

        def tb(i):
            return TB[:, i, :]

        (DT, RHO, TH, R1, Y, K, FR, SIN1, COS1, QRE, QIM, T0_, T1_, RT0) = range(14)
        YC, KC, FRC = Y, K, FR
        NRE, DEN, INV = Y, K, FR
        ER = lambda n: 14 + n
        EI = lambda n: 23 + n
        NEI = lambda n: 32 + n
        QR = lambda n: 41 + n
        QI = lambda n: 49 + n
        NQI = lambda n: 57 + n
        RQR, RQI, RNQI = 65, 69, 73
        E1R, E1I = ER(1), EI(1)

        def tbo(eng, kind, out_i, *a):
            r = ["S5P", "TB"]
            w = ["TB"]
            if kind == "tt":
                tt_op(eng, tb(out_i), a[0], a[1], a[2], r, w)
            elif kind == "ts":
                ts_op(eng, tb(out_i), a[0], a[1], a[2], r, w)

        def cmul(out_r, out_i, ar, ai, br, bi):
            tbo("dve", "tt", T0_, tb(ar), tb(br), ALU.mult)
            tbo("dve", "tt", T1_, tb(ai), tb(bi), ALU.mult)
            tbo("dve", "tt", out_r, tb(T0_), tb(T1_), ALU.subtract)
            tbo("dve", "tt", T0_, tb(ar), tb(bi), ALU.mult)
            tbo("dve", "tt", T1_, tb(ai), tb(br), ALU.mult)
            tbo("dve", "tt", out_i, tb(T0_), tb(T1_), ALU.add)

        def s5_tables(l):
            sp0 = l * 48
            ARE, AIM, LDT = S5P[:, sp0:sp0 + 16], S5P[:, sp0 + 16:sp0 + 32], S5P[:, sp0 + 32:sp0 + 48]
            act(tb(DT), LDT, AF.Exp, ["S5P"], ["TB"])
            tbo("dve", "tt", RHO, tb(DT), ARE, ALU.mult)
            tbo("dve", "tt", TH, tb(DT), AIM, ALU.mult)
            act(tb(R1), tb(RHO), AF.Exp, ["TB"], ["TB"])
            act(tb(RT0), tb(RHO), AF.Exp, ["TB"], ["TB"], scale=float(T0C))
            tbo("dve", "ts", Y, tb(TH), 1.0 / TWO_PI, ALU.mult)
            tbo("dve", "ts", K, tb(Y), MAGIC, ALU.add)
            tbo("dve", "ts", K, tb(K), MAGIC, ALU.subtract)
            tbo("dve", "tt", FR, tb(Y), tb(K), ALU.subtract)
            act(tb(SIN1), tb(FR), AF.Sin, ["TB"], ["TB"], scale=TWO_PI * (1.0 - 1e-6))
            tbo("dve", "ts", YC, tb(Y), 0.25, ALU.add)
            tbo("dve", "ts", KC, tb(YC), MAGIC, ALU.add)
            tbo("dve", "ts", KC, tb(KC), MAGIC, ALU.subtract)
            tbo("dve", "tt", FRC, tb(YC), tb(KC), ALU.subtract)
            act(tb(COS1), tb(FRC), AF.Sin, ["TB"], ["TB"], scale=TWO_PI * (1.0 - 1e-6))
            tbo("dve", "tt", E1R, tb(R1), tb(COS1), ALU.mult)
            tbo("dve", "tt", E1I, tb(R1), tb(SIN1), ALU.mult)
            tbo("dve", "ts", NRE, tb(E1R), -1.0, ALU.add)
            tbo("dve", "tt", T0_, ARE, ARE, ALU.mult)
            tbo("dve", "tt", T1_, AIM, AIM, ALU.mult)
            tbo("dve", "tt", DEN, tb(T0_), tb(T1_), ALU.add)
            P.op("dve", lambda e: e.reciprocal(out=tb(INV), in_=tb(DEN)), reads=["TB"], writes=["TB"])
            tbo("dve", "tt", T0_, tb(NRE), ARE, ALU.mult)
            tbo("dve", "tt", T1_, tb(E1I), AIM, ALU.mult)
            tbo("dve", "tt", T0_, tb(T0_), tb(T1_), ALU.add)
            tbo("dve", "tt", QRE, tb(T0_), tb(INV), ALU.mult)
            tbo("dve", "tt", T0_, tb(E1I), ARE, ALU.mult)
            tbo("dve", "tt", T1_, tb(NRE), AIM, ALU.mult)
            tbo("dve", "tt", T0_, tb(T0_), tb(T1_), ALU.subtract)
            tbo("dve", "tt", QIM, tb(T0_), tb(INV), ALU.mult)
            P.op("dve", lambda e: e.memset(tb(ER(0)), 1.0), reads=["TB"], writes=["TB"])
            P.op("dve", lambda e: e.memset(tb(EI(0)), 0.0), reads=["TB"], writes=["TB"])
            for n in range(1, T0C):
                cmul(ER(n + 1), EI(n + 1), ER(n), EI(n), E1R, E1I)
            for n in range(T0C):
                cmul(QR(n), QI(n), ER(n), EI(n), QRE, QIM)
                tbo("dve", "ts", NQI(n), tb(QI(n)), -1.0, ALU.mult)
            for n in range(T0C + 1):
                tbo("dve", "ts", NEI(n), tb(EI(n)), -1.0, ALU.mult)
            for t_ in range(4):
                copy("dve", tb(RQR + t_), tb(QR(3 - t_)), ["TB"], ["TB"])
                copy("dve", tb(RQI + t_), tb(QI(3 - t_)), ["TB"], ["TB"])
                copy("dve", tb(RNQI + t_), tb(NQI(3 - t_)), ["TB"], ["TB"])
            copy("dve", PW[:, 0, 0, :], tb(COS1), ["TB"], ["PW"])
            copy("dve", PW[:, 1, 0, :], tb(SIN1), ["TB"], ["PW"])
            for lv in range(10):
                pr, pi = PW[:, 0, lv, :], PW[:, 1, lv, :]
                tt_op("dve", tb(T0_), pr, pr, ALU.mult, ["PW", "TB"], ["TB"])
                tt_op("dve", tb(T1_), pi, pi, ALU.mult, ["PW", "TB"], ["TB"])
                tt_op("dve", PW[:, 0, lv + 1, :], tb(T0_), tb(T1_), ALU.subtract, ["TB", "PW"], ["PW"])
                tt_op("dve", tb(T0_), pr, pi, ALU.mult, ["PW", "TB"], ["TB"])
                ts_op("dve", PW[:, 1, lv + 1, :], tb(T0_), 2.0, ALU.mult, ["TB", "PW"], ["PW"])


        P.capture = []
        s5_tables(0)
        tblq = P.capture
        P.capture = None
        for l in range(nl):
            vb = 0
            VEC = VEC2[:, l % 2, :]
            VR = f"VEC{l % 2}"
            dma_io(VEC2[:, l % 2, :], vec_d[:, l * NVL:(l + 1) * NVL], [], [VR])

            rmsnorm(lambda kt: VEC[:, kt:kt + 1], VR, False)

            dma_io(S2[:, 0:1920], zbuf_d[l, :, :], [], [f"S2_{t}" for t in range(4)])
            for j in range(4):
                src = S2[:, j * 480:(j + 1) * 480].rearrange("p (s c) -> p s c", c=30)
                copy("pool", Zs(j)[:, :, 0:30], src, [f"S2_{t}" for t in range(4)], [f"Z{j}_4"])

            for m in range(12):
                ws_emit_pending()
                P.flush(tblq, 30)
                if m % 2 == 0:
                    wb = ws_get()
                wvw = wv(wb, 256)
                mc = slice((m % 2) * 128, (m % 2) * 128 + 128)
                bias = VEC[:, vb + 16 + m:vb + 17 + m]
                for tt in range(5):
                    b, ps = dense_group(lambda kt: wvw[:, kt, mc], lambda kt, tt: A[:, kt, tsl(tt)], 8, tt,
                                        lambda kt, tt: [f"wbf{wb}", f"A{kt}_{tt}"])
                    n = TT[tt][1]
                    if m < 4:
                        ti = T0C if tt < 4 else 4
                        ts_op("dve", U[:, m, tsl(tt)].rearrange("p (s k) -> p s k", s=ti),
                              ps.rearrange("p (k s) -> p s k", s=ti), bias, ALU.add, [f"ps{b}", VR], [f"U{m}_{tt}"])
                    elif m % 2 == 0:
                        act(S1[:, tsl(tt)], ps, AF.Sigmoid, [f"ps{b}", VR], [f"S1_{tt}"], bias=bias)
                    else:
                        j = (m - 5) // 2
                        if tt < 4:
                            stt("dve", Z[:, j, zsl(tt)], ps, bias, S1[:, tsl(tt)], ALU.add, ALU.mult,
                                [f"ps{b}", f"S1_{tt}", VR], [f"Z{j}_{tt}"])
                            if tt == 3:
                                stt("dve", CVP[:, j, :], ps[:, 482:512], bias, S1[:, 2018:2048], ALU.add, ALU.mult,
                                    [f"ps{b}", f"S1_{tt}", VR], ["CVP"])
                        else:
                            stt("dve", Zs(j)[:, :, 30:34], v3(ps, 4), bias, v3(S1[:, tsl(4)], 4), ALU.add, ALU.mult,
                                [f"ps{b}", f"S1_{tt}", VR], [f"Z{j}_4"])
                            stt("dve", CVS[:, j, :], ps, bias, S1[:, tsl(4)], ALU.add, ALU.mult,
                                [f"ps{b}", f"S1_{tt}", VR], ["CVS"])
            P.flush(tblq, 10 ** 6)
            dma_io(cvp_d[l, :, :], CVP[:, :, :].rearrange("p a b -> p (a b)"), ["CVP"], [])
            dma_io(cvs_d[l, :, :], CVS[:, :, :].rearrange("p a b -> p (a b)"), ["CVS"], [])

            NCH = NP_TOK // T0C
            CPT = 512 // T0C
            LG = T0C.bit_length() - 1
            SRE, SIM_ = SACC[:, 0, :], SACC[:, 1, :]
            DRE, DIM_, TMP, GRE, GIM = (S1[:, i * NCH:(i + 1) * NCH] for i in range(5))
            rS1 = ["S1_0", "S1_1", "S1_2"]
            T8A = S1[:, 1280:1536].rearrange("p (n w) -> p n w", w=32)
            T8B = S1[:, 1536:1792].rearrange("p (n w) -> p n w", w=32)
            rT8 = ["S1_2", "S1_3"]

            def coef8(dst, dres, cre_w, cim_w, wres, i_re, i_im, i_nim, pi_):
                def tab(i):
                    return TB[:, i:i + T0C, pi_].unsqueeze(2).broadcast_to([128, T0C, 32])
                crb = cre_w.unsqueeze(1).broadcast_to([128, T0C, 32])
                cib = cim_w.unsqueeze(1).broadcast_to([128, T0C, 32])
                rd = [wres, "TB"]
                tt_op("dve", T8A, crb, tab(i_re), ALU.mult, rd, rT8)
                tt_op("dve", T8B, cib, tab(i_im), ALU.mult, rd, rT8)
                tt_op("dve", dst[:, 0, :, :], T8A, T8B, ALU.subtract, rT8, [dres])
                tt_op("dve", T8A, crb, tab(i_nim), ALU.mult, rd, rT8)
                tt_op("dve", T8B, cib, tab(i_re), ALU.mult, rd, rT8)
                tt_op("dve", dst[:, 1, :, :], T8A, T8B, ALU.subtract, rT8, [dres])

            for j in range(4):
                wa = ws_get()
                for k_ in range(2):
                    dma_io(H0[:, k_, :, :], h0_d[l, :, k_, 4 * j:4 * j + 4, :], [], ["H0"])
                rwa = f"wbf{wa}"

                def blk(pp, i):
                    return WBF[:, wa, pp * 512 + i * 128:pp * 512 + (i + 1) * 128]

                def blkT(pp, i):
                    return WBF[:, wb2, pp * 256 + i * 128:pp * 256 + (i + 1) * 128]

                WCv = S2[:, 0:2048].rearrange("p (a b k) -> p a b k", a=4, b=2)
                rWC = ["S2_0", "S2_1", "S2_2", "S2_3"]
                def pair_stage(pp, stage):
                    pi_ = 4 * j + pp
                    sc = lambda i: TB[:, i, pi_:pi_ + 1]
                    bre, bim, cre, cim = blk(pp, 0), blk(pp, 1), blk(pp, 2), blk(pp, 3)
                    last = (pp == 3)
                    WCR, WCI = WCv[:, pp, 0, :], WCv[:, pp, 1, :]
                    h0r, h0i = H0[:, 0, pp, :], H0[:, 1, pp, :]
                    if stage in ("B0", "B1"):
                        ubv = U[:, j, 0:NP_TOK].rearrange("p (t s k) -> p t s k", t=4, s=T0C)
                        usv = U[:, j, tsl(4)].rearrange("p (t q) -> p t q", t=4)
                        nbu = [0]
                        BW = TMPF[:, :, :].rearrange("p a b -> p (a b)").bitcast(BF16).rearrange("p (a b c) -> p a b c", a=2, b=2)

                        def bu_mm(s_):
                            b = 5 + s_ % 2
                            rhs = ubv[:, :, s_, :]
                            ures = [f"U{j}_{t}" for t in range(4)]
                            mm(PS[b][:, 0:NCH], bre, rhs, True, True, [rwa] + ures, b)
                            mm(PS[b][:, 256:256 + NCH], bim, rhs, True, True, [rwa] + ures, b)

                        def bu_evac(s_):
                            b = 5 + s_ % 2
                            f = s_ % 2
                            n_e = T0C - 1 - s_
                            o1, o2 = BW[:, f, 0, :], BW[:, f, 1, :]
                            act(o1, PS[b][:, :], AF.Copy, [f"ps{b}", "TB"], [f"TMPF{f}"], scale=sc(QR(n_e)))
                            act(o2, PS[b][:, :], AF.Copy, [f"ps{b}", "TB"], [f"TMPF{f}"], scale=sc(QI(n_e)))

                        def bu_sacc(s_):
                            f = s_ % 2
                            o1, o2 = BW[:, f, 0, :], BW[:, f, 1, :]
                            rd = [f"TMPF{f}", "IDB", "NIDB"]
                            mm(PS[7][:, 0:NCH], IDB[:, :], o1[:, 0:NCH], s_ == 0, False, rd, 7)
                            mm(PS[7][:, 0:NCH], NIDB[:, :], o2[:, 256:256 + NCH], False, False, rd, 7)
                            mm(PS[7][:, 256:256 + NCH], IDB[:, :], o2[:, 0:NCH], False, False, rd, 7)
                            mm(PS[7][:, 256:256 + NCH], IDB[:, :], o1[:, 256:256 + NCH], False, s_ == T0C - 1, rd, 7)

                        def bu_step(rhs, ncol, dst_re, dst_im, n_e, first, ures):
                            b = 6
                            p_re, p_im = PS[b][:, 0:ncol], PS[b][:, 256:256 + ncol]
                            mm(p_re, bre, rhs, True, True, [rwa] + ures, b)
                            mm(p_im, bim, rhs, True, True, [rwa] + ures, b)
                            rb = [f"ps{b}", "TB", "SACC"]
                            if first:
                                ts_op("dve", dst_re, p_re, sc(QR(n_e)), ALU.mult, rb, ["SACC"])
                                ts_op("dve", dst_im, p_re, sc(QI(n_e)), ALU.mult, rb, ["SACC"])
                            else:
                                stt("dve", dst_re, p_re, sc(QR(n_e)), dst_re, ALU.mult, ALU.add, rb, ["SACC"])
                                stt("dve", dst_im, p_re, sc(QI(n_e)), dst_im, ALU.mult, ALU.add, rb, ["SACC"])
                            stt("dve", dst_re, p_im, sc(NQI(n_e)), dst_re, ALU.mult, ALU.add, rb, ["SACC"])
                            stt("dve", dst_im, p_im, sc(QR(n_e)), dst_im, ALU.mult, ALU.add, rb, ["SACC"])

                        if stage == "B0":
                            bu_mm(0)
                            bu_mm(1)
                            bu_evac(0)
                            bu_evac(1)
                        else:
                            for s_ in range(T0C):
                                bu_sacc(s_)
                                if s_ + 2 < T0C:
                                    bu_mm(s_ + 2)
                                    bu_evac(s_ + 2)
                    elif stage == "SMP":
                        mm(PS[6][:, 0:NS_TOK], bre, U[:, j, tsl(4)], True, True, [rwa, f"U{j}_4"], 6)
                        mm(PS[6][:, 256:256 + NS_TOK], bim, U[:, j, tsl(4)], True, True, [rwa, f"U{j}_4"], 6)
                        copy("dve", SBS[:, :, :], PS[6][:, :].rearrange("p (a k) -> p a k", a=2)[:, :, 0:NS_TOK], ["ps6"], ["SBS"])
                        ss_re, ss_im = SSO[:, 0, pp, 1:17], SSO[:, 1, pp, 1:17]
                        b_re = SBS[:, 0, :].rearrange("p (t q) -> p t q", t=4)
                        b_im = SBS[:, 1, :].rearrange("p (t q) -> p t q", t=4)
                        P1, P2 = S1[:, 1792:1856], S1[:, 1856:1920]
                        p1v = P1.rearrange("p (t q) -> p t q", t=4)
                        p2v = P2.rearrange("p (t q) -> p t q", t=4)

                        def tabr(i0_):
                            return TB[:, i0_:i0_ + 4, pi_].unsqueeze(2).broadcast_to([128, 4, NSEQ])

                        rb = ["SBS", "TB", "S1_3"]
                        for dst, ta, tb2 in ((ss_re, RQR, RNQI), (ss_im, RQI, RQR)):
                            tt_op("dve", p1v, b_re, tabr(ta), ALU.mult, rb, ["S1_3"])
                            tt_op("dve", p2v, b_im, tabr(tb2), ALU.mult, rb, ["S1_3"])
                            tt_op("dve", P1, P1, P2, ALU.add, ["S1_3"], ["S1_3"])
                            P.op("dve", lambda e, dst=dst: e.tensor_reduce(
                                out=dst, in_=P1.rearrange("p (t q) -> p q t", t=4), axis=mybir.AxisListType.X, op=ALU.add),
                                reads=["S1_3"], writes=["SSO"])
                        stt("dve", ss_re, h0r, sc(ER(4)), ss_re, ALU.mult, ALU.add, ["H0", "TB", "SSO"], ["SSO"])
                        stt("dve", ss_re, h0i, sc(NEI(4)), ss_re, ALU.mult, ALU.add, ["H0", "TB", "SSO"], ["SSO"])
                        stt("dve", ss_im, h0r, sc(EI(4)), ss_im, ALU.mult, ALU.add, ["H0", "TB", "SSO"], ["SSO"])
                        stt("dve", ss_im, h0i, sc(ER(4)), ss_im, ALU.mult, ALU.add, ["H0", "TB", "SSO"], ["SSO"])
                    elif stage == "ROTC":
                        copy("dve", SACC[:, :, 0:NCH], PS[7][:, :].rearrange("p (a k) -> p a k", a=2)[:, :, 0:NCH], ["ps7"], ["SACC"])
                    elif stage == "ROT":
                        s_re, s_im = SACC[:, 0, 0:NCH], SACC[:, 1, 0:NCH]
                        tt_op("dve", DRE, WCR, s_re, ALU.mult, rWC + ["SACC"], ["S1_0"])
                        tt_op("dve", TMP, WCI, s_im, ALU.mult, rWC + ["SACC"], ["S1_1"])
                        tt_op("dve", DRE, DRE, TMP, ALU.add, ["S1_0", "S1_1"], ["S1_0"])
                        tt_op("dve", DIM_, WCR, s_im, ALU.mult, rWC + ["SACC"], ["S1_0"])
                        tt_op("dve", TMP, WCI, s_re, ALU.mult, rWC + ["SACC"], ["S1_1"])
                        tt_op("dve", DIM_, DIM_, TMP, ALU.subtract, ["S1_0", "S1_1"], ["S1_0"])
                    elif stage == "SCAN":
                        act(RT[:, :], WCR, AF.Identity, rWC + ["TB"], ["RT"], bias=sc(RT0), scale=0.0)
                        P.op("dve", lambda e: e.tensor_tensor_scan(out=GRE, data0=RT[:, :], data1=DRE, initial=0.0,
                                                                   op0=ALU.mult, op1=ALU.add),
                             reads=["RT", "S1_0"], writes=["S1_1"])
                        P.op("dve", lambda e: e.tensor_tensor_scan(out=GIM, data0=RT[:, :], data1=DIM_, initial=0.0,
                                                                   op0=ALU.mult, op1=ALU.add),
                             reads=["RT", "S1_0"], writes=["S1_2"])
                        e_ = NCH - 1
                        tt_op("dve", CAR[:, 2:3], WCI[:, e_:e_ + 1], GIM[:, e_:e_ + 1], ALU.mult, rWC + ["S1_2", "CAR"], ["CAR"])
                        stt("dve", SSO[:, 0, pp, 0:1], WCR[:, e_:e_ + 1], GRE[:, e_:e_ + 1], CAR[:, 2:3], ALU.mult, ALU.subtract,
                            rWC + ["S1_1", "CAR"], ["SSO"])
                        tt_op("dve", CAR[:, 3:4], WCR[:, e_:e_ + 1], GIM[:, e_:e_ + 1], ALU.mult, rWC + ["S1_2", "CAR"], ["CAR"])
                        stt("dve", SSO[:, 1, pp, 0:1], WCI[:, e_:e_ + 1], GRE[:, e_:e_ + 1], CAR[:, 3:4], ALU.mult, ALU.add,
                            rWC + ["S1_1", "CAR"], ["SSO"])
                        tt_op("dve", DRE, WCR, GRE, ALU.mult, rWC + ["S1_1"], ["S1_0"])
                        tt_op("dve", TMP, WCI, GIM, ALU.mult, rWC + ["S1_2"], ["S1_1"])
                        tt_op("dve", HP[:, 0, 1:NCH + 1], DRE, TMP, ALU.subtract, ["S1_0", "S1_1"], ["HP"])
                        tt_op("dve", DIM_, WCI, GRE, ALU.mult, rWC + ["S1_1"], ["S1_0"])
                        tt_op("dve", TMP, WCR, GIM, ALU.mult, rWC + ["S1_2"], ["S1_1"])
                        tt_op("dve", HP[:, 1, 1:NCH + 1], DIM_, TMP, ALU.add, ["S1_0", "S1_1"], ["HP"])
                    elif stage == "C":
                        copy("dve", H0B[:, 0, :], h0r, ["H0"], ["H0B"])
                        copy("dve", H0B[:, 1, :], h0i, ["H0"], ["H0B"])
                        if 'c' in DBG:
                            return
                        w0 = 32 * pp
                        wz = 32 * ((pp - 1) % 4)
                        P.op("pool", lambda e, wz=wz: e.memset(LC8[:, :, :, wz:wz + 32], 0.0), writes=["LC8"])
                        coef8(LC8[:, :, :, w0:w0 + 32], "LC8", cre[:, w0:w0 + 32], cim[:, w0:w0 + 32], rwa,
                              ER(1), EI(1), NEI(1), pi_)
                        for jj in range(T0C):
                            fin = last and jj == T0C - 1
                            for tt in range(4):
                                pv = PS[tt][:, jj * CPT:(jj + 1) * CPT]
                                mm(pv, LC8[:, 0, jj, :], HP[:, 0, tt * CPT:(tt + 1) * CPT], False, False, ["LC8", "HP"], tt)
                                mm(pv, LC8[:, 1, jj, :], HP[:, 1, tt * CPT:(tt + 1) * CPT], False, fin, ["LC8", "HP"], tt)
                            if jj < 4:
                                pv = PS[4][:, jj * NSEQ:(jj + 1) * NSEQ]
                                mm(pv, LC8[:, 0, jj, :], H0B[:, 0, :], False, False, ["LC8", "H0B"], 4)
                                mm(pv, LC8[:, 1, jj, :], H0B[:, 1, :], False, last and jj == 3, ["LC8", "H0B"], 4)

                pair_stage(0, "B0")
                pair_stage(0, "B1")
                pair_stage(1, "B0")
                wb2 = ws_get(keep=1)
                rwb = f"wbf{wb2}"
                SCR = S1[:, 1280:1792].rearrange("p (a k) -> p a k", a=4)
                copy("dve", WCv[:, :, 0, 0:1], PW[:, 0, LG, 4 * j:4 * j + 4].unsqueeze(2), ["PW"], rWC)
                copy("dve", WCv[:, :, 1, 0:1], PW[:, 1, LG, 4 * j:4 * j + 4].unsqueeze(2), ["PW"], rWC)
                lv = 0
                while (1 << lv) < NCH and 'd' not in DBG:
                    m_ = 1 << lv
                    prb = PW[:, 0, LG + lv, 4 * j:4 * j + 4].unsqueeze(2).broadcast_to([128, 4, m_])
                    pib = PW[:, 1, LG + lv, 4 * j:4 * j + 4].unsqueeze(2).broadcast_to([128, 4, m_])
                    sR, sI = WCv[:, :, 0, 0:m_], WCv[:, :, 1, 0:m_]
                    dR, dI = WCv[:, :, 0, m_:2 * m_], WCv[:, :, 1, m_:2 * m_]
                    tt_op("dve", SCR[:, :, 0:m_], sI, pib, ALU.mult, rWC + ["PW"], rT8)
                    tt_op("dve", dR, sR, prb, ALU.mult, rWC + ["PW"], rWC)
                    tt_op("dve", dR, dR, SCR[:, :, 0:m_], ALU.subtract, rWC + rT8, rWC)
                    tt_op("dve", SCR[:, :, 0:m_], sR, pib, ALU.mult, rWC + ["PW"], rT8)
                    tt_op("dve", dI, sI, prb, ALU.mult, rWC + ["PW"], rWC)
                    tt_op("dve", dI, dI, SCR[:, :, 0:m_], ALU.add, rWC + rT8, rWC)
                    lv += 1
                for pp in range(4 if 't' not in DBG else 0):
                    pi_ = 4 * j + pp
                    w0 = 32 * pp
                    coef8(LC8[:, :, :, 0:32], "LC8", blk(pp, 2)[:, w0:w0 + 32], blk(pp, 3)[:, w0:w0 + 32], rwa, QR(0), QI(0), NQI(0), pi_)
                    for tau in range(T0C):
                        pst = PS[tau // 4][:, (tau % 4) * 128 + w0:(tau % 4) * 128 + w0 + 32]
                        mm(pst, blkT(pp, 0), LC8[:, 0, tau, 0:32], True, False, [rwb, "LC8"], tau // 4)
                        mm(pst, blkT(pp, 1), LC8[:, 1, tau, 0:32], False, True, [rwb, "LC8"], tau // 4)
                for tau in range(T0C if 't' not in DBG else 0):
                    pst = PS[tau // 4][:, (tau % 4) * 128:(tau % 4 + 1) * 128]
                    act(TAP[:, tau, :], pst, AF.Copy, [f"ps{tau // 4}"], [f"TAP{tau}"])
                    if tau == 0:
                        stt("dve", TAP[:, 0, :], IDB[:, :], VEC[:, vb + 28 + j:vb + 29 + j], TAP[:, 0, :], ALU.mult, ALU.add,
                            ["IDB", VR, "TAP0"], ["TAP0"])
                for tt in range(5):
                    t0, n = TT[tt]
                    ti = T0C if tt < 4 else 4
                    cw = n // ti
                    for tau in range(ti):
                        mm(PS[tt][:, tau * cw:n], TAP[:, tau, :], U[:, j, t0:t0 + (ti - tau) * cw], tau == 0, False,
                           [f"TAP{tau}", f"U{j}_{tt}"], tt)

                pair_stage(0, "ROTC")
                pair_stage(0, "SMP")
                pair_stage(0, "ROT")
                for pp in range(4):
                    if pp < 3:
                        pair_stage(pp + 1, "B1")
                    pair_stage(pp, "SCAN")
                    if pp < 2:
                        pair_stage(pp + 2, "B0")
                    pair_stage(pp, "C")
                    if pp == 1:
                        ws_emit_pending()
                    if pp < 3:
                        pair_stage(pp + 1, "ROTC")
                        pair_stage(pp + 1, "SMP")
                        pair_stage(pp + 1, "ROT")
                for k_ in range(2):
                    dma_io(sso_d[l, :, k_, 4 * j:4 * j + 4, :], SSO[:, k_, :, :], ["SSO"], [])
                for tt in range(5):
                    n = TT[tt][1]
                    ti = T0C if tt < 4 else 4
                    act(A[:, j, tsl(tt)].rearrange("p (k s) -> p k s", s=ti),
                        PS[tt][:, 0:n].rearrange("p (s k) -> p k s", s=ti), AF.Gelu_apprx_tanh,
                        [f"ps{tt}"], [f"A{j}_{tt}"])
            bank_ctr[0] = 0

            wb = ws_get()
            wvw = wv(wb, 512)
            for m in range(4):
                bias = VEC[:, vb + 32 + m:vb + 33 + m]
                for tt in range(5):
                    b, ps = dense_group(lambda kt: wvw[:, kt, m * 128:(m + 1) * 128],
                                        lambda kt, tt: A[:, kt, tsl(tt)], 4, tt,
                                        lambda kt, tt: [f"wbf{wb}", f"A{kt}_{tt}"])
                    act(S1[:, tsl(tt)], ps, AF.Sigmoid, [f"ps{b}", VR], [f"S1_{tt}"], bias=bias)
                    tt_op("dve", U[:, m, tsl(tt)], A[:, m, tsl(tt)], S1[:, tsl(tt)], ALU.mult,
                          [f"A{m}_{tt}", f"S1_{tt}"], [f"U{m}_{tt}"])

            for j in range(4):
                banks = [next_bank() for _ in range(5)]
                for k in range(31):
                    d = k % 4
                    ts_op("dve", DG[:, d, :], IDB[:, :], VEC[:, vb + 48 + j * 31 + k:vb + 49 + j * 31 + k], ALU.mult,
                          ["IDB", VR], [f"DG{d}"])
                    for tt in range(4):
                        t0 = TT[tt][0]
                        rd = [f"DG{d}", f"Z{j}_{tt}", f"Zpad{j}"] + ([f"Z{j}_{tt - 1}"] if tt > 0 else [])
                        mm(PS[banks[tt]][:, :], DG[:, d, :], Z[:, j, t0 + k:t0 + k + 512], k == 0, k == 30, rd, banks[tt])
                    mm(v3(PS[banks[4]][:, 0:64], 4), DG[:, d, :], Zs(j)[:, :, k:k + 4], k == 0, k == 30,
                       [f"DG{d}", f"Z{j}_4"], banks[4])
                bias = VEC[:, vb + 36 + j:vb + 37 + j]
                for tt in range(5):
                    n = TT[tt][1]
                    act(A[:, 4 + j, tsl(tt)], PS[banks[tt]][:, 0:n], AF.Identity, [f"ps{banks[tt]}", VR],
                        [f"A{4 + j}_{tt}"], bias=bias)
            bank_ctr[0] = 0
            for tt in range(5):
                n = TT[tt][1]
                b, ps = dense_group(lambda kt: ONB[:, :], lambda kt, tt: A[:, 4 + kt, tsl(tt)], 4, tt,
                                    lambda kt, tt: ["ONB", f"A{4 + kt}_{tt}"])
                act(S1[:, tsl(tt)], ps, AF.Copy, [f"ps{b}"], [f"S1_{tt}"], scale=1.0 / 512)
            for j in range(4):
                for tt in range(5):
                    act(Z[:, j, tsl(tt)], A[:, 4 + j, tsl(tt)], AF.Square, [f"A{4 + j}_{tt}"], Zrow(j) + [f"Zpad{j}"])
            for tt in range(5):
                b, ps = dense_group(lambda kt: ONB[:, :], lambda kt, tt: Z[:, kt, tsl(tt)], 4, tt,
                                    lambda kt, tt: ["ONB"] + Zrow(kt))
                tt_op("dve", S2[:, tsl(tt)], S1[:, tsl(tt)], S1[:, tsl(tt)], ALU.mult, [f"S1_{tt}"], [f"S2_{tt}"])
                stt("dve", S2[:, tsl(tt)], ps, 1.0 / 512, S2[:, tsl(tt)], ALU.mult, ALU.subtract,
                    [f"ps{b}", f"S2_{tt}"], [f"S2_{tt}"])
                act(S2[:, tsl(tt)], S2[:, tsl(tt)], AF.Ln, [f"S2_{tt}"], [f"S2_{tt}"], bias=EPS)
                act(S2[:, tsl(tt)], S2[:, tsl(tt)], AF.Exp, [f"S2_{tt}"], [f"S2_{tt}"], scale=-0.5)
            tf = [0]
            for j in range(4):
                lg, lb = VEC[:, vb + 40 + j:vb + 41 + j], VEC[:, vb + 44 + j:vb + 45 + j]
                for tt in range(5):
                    n = TT[tt][1]
                    f = tf[0] % 2
                    tf[0] += 1
                    tmp = TMPF[:, f, 0:n]
                    tt_op("dve", tmp, A[:, 4 + j, tsl(tt)], S1[:, tsl(tt)], ALU.subtract,
                          [f"A{4 + j}_{tt}", f"S1_{tt}"], [f"TMPF{f}"])
                    tt_op("dve", tmp, tmp, S2[:, tsl(tt)], ALU.mult, [f"TMPF{f}", f"S2_{tt}"], [f"TMPF{f}"])
                    act(A[:, 4 + j, tsl(tt)], tmp, AF.Silu, [f"TMPF{f}", VR], [f"A{4 + j}_{tt}"], bias=lb, scale=lg)

            for m in range(8):
                if m % 2 == 0:
                    wb = ws_get()
                wvw = wv(wb, 256)
                mc = slice((m % 2) * 128, (m % 2) * 128 + 128)
                for tt in range(5):
                    b, ps = dense_group(lambda kt: wvw[:, kt, mc],
                                        lambda kt, tt: (U[:, kt, tsl(tt)] if kt < 4 else A[:, kt, tsl(tt)]), 8, tt,
                                        lambda kt, tt: [f"wbf{wb}", (f"U{kt}_{tt}" if kt < 4 else f"A{kt}_{tt}")])
                    tt_op("dve", X[:, m, tsl(tt)], X[:, m, tsl(tt)], ps, ALU.add, [f"X{m}_{tt}", f"ps{b}"], [f"X{m}_{tt}"])

            rmsnorm(lambda kt: VEC[:, 8 + kt:9 + kt], VR, False)
            bank_ctr[0] = 0

            def Hh(m, tt):
                return U[:, m, tsl(tt)] if m < 4 else Z[:, m - 4, tsl(tt)]

            def Hr(m, tt):
                return [f"U{m}_{tt}"] if m < 4 else Zrow(m - 4) + [f"Zpad{m - 4}"]

            for fb in range(4):
                if fb == 0 and l + 1 < nl:
                    P.capture = []
                    s5_tables(l + 1)
                    tblq = P.capture
                    P.capture = None
                for m in range(8):
                    P.flush(tblq, 6)
                    if m % 2 == 0:
                        wb = ws_get()
                    wvw = wv(wb, 256)
                    mc = slice((m % 2) * 128, (m % 2) * 128 + 128)
                    for tt in range(5):
                        n = TT[tt][1]
                        b, ps = dense_group(lambda kt: wvw[:, kt, mc], lambda kt, tt: A[:, kt, tsl(tt)], 8, tt,
                                            lambda kt, tt: [f"wbf{wb}", f"A{kt}_{tt}"])
                        f = tf[0] % 2
                        tf[0] += 1
                        tmp = TMPF[:, f, 0:n]
                        act(tmp, ps, AF.Relu, [f"ps{b}"], [f"TMPF{f}"])
                        act(Hh(m, tt), tmp, AF.Square, [f"TMPF{f}"], Hr(m, tt))
                for m in range(8):
                    P.flush(tblq, 6)
                    if m % 2 == 0:
                        wb = ws_get()
                    wvw = wv(wb, 256)
                    mc = slice((m % 2) * 128, (m % 2) * 128 + 128)
                    for tt in range(5):
                        b, ps = dense_group(lambda kt: wvw[:, kt, mc], lambda kt, tt: Hh(kt, tt), 8, tt,
                                            lambda kt, tt: [f"wbf{wb}"] + Hr(kt, tt))
                        tt_op("dve", X[:, m, tsl(tt)], X[:, m, tsl(tt)], ps, ALU.add,
                              [f"X{m}_{tt}", f"ps{b}"], [f"X{m}_{tt}"])
            P.flush(tblq, 10 ** 6)
            for j in range(4):
                P.op("pool", lambda e, j=j: e.memset(Z[:, j, 0:30], 0.0), writes=Zrow(j) + [f"Zpad{j}"])
            bank_ctr[0] = 0

        rmsnorm(lambda kt: GF[:, kt:kt + 1], "GF", True)
        for kt in range(8):
            dma_io(yT[kt, :, :], X[:, kt, :], [f"X{kt}_{t}" for t in range(5)], [])

        P.emit(block, sems)
    return nc


def _prep_shared(inp):
    f = np.float32
    nl = DEPTH
    order = list(range(0, 512))
    for j in range(4):
        order += list(range(1024 + 128 * j, 1024 + 128 * (j + 1)))
        order += list(range(512 + 128 * j, 512 + 128 * (j + 1)))
    order = np.array(order)
    w_in = np.ascontiguousarray(inp["w_in"][:, :, order])
    b_in = inp["b_in"][:, order]
    vec = np.zeros((128, NV), f)

    def cols(v, n):
        return np.asarray(v, f).reshape(n, 128).T

    for l in range(nl):
        b = l * NVL
        vec[:, b + 0:b + 8] = cols(inp["norm_mix_g"][l], 8)
        vec[:, b + 8:b + 16] = cols(inp["norm_mlp_g"][l], 8)
        vec[:, b + 16:b + 28] = cols(b_in[l], 12)
        vec[:, b + 28:b + 32] = cols(inp["ssm_d"][l], 4)
        vec[:, b + 32:b + 36] = cols(inp["b_glu"][l], 4)
        vec[:, b + 36:b + 40] = cols(inp["conv_b"][l], 4)
        vec[:, b + 40:b + 44] = cols(inp["conv_ln_g"][l], 4)
        vec[:, b + 44:b + 48] = cols(inp["conv_ln_b"][l], 4)
        cw = np.asarray(inp["conv_w"][l], f)
        vec[:, b + 48:b + 172] = cw.reshape(31, 4, 128).transpose(2, 1, 0).reshape(128, 124)
    vec[:, NVL * nl:NVL * nl + 8] = cols(inp["norm_f_g"], 8)

    s5p = np.zeros((128, nl * 48), f)
    for l in range(nl):
        for nm, off in (("ssm_a_re", 0), ("ssm_a_im", 16)):
            a = np.asarray(inp[nm][l], f).reshape(16, 2, 64)
            s5p[:, l * 48 + off:l * 48 + off + 16] = a.transpose(1, 2, 0).reshape(128, 16)
        ld = np.asarray(inp["ssm_log_dt"][l], f).reshape(16, 2)
        s5p[:, l * 48 + 32:l * 48 + 48] = np.repeat(ld.T[:, None, :], 64, axis=1).reshape(128, 16)

    s5w = np.zeros((nl, 4, 128, 2048), f)
    for l in range(nl):
        for pi in range(16):
            j, pp = pi // 4, pi % 4
            for x in range(2):
                g = 2 * pi + x
                gl = 2 * pp + x
                rows_gc = slice(gl * 16, gl * 16 + 16)
                cols_xp = slice(x * 64, x * 64 + 64)
                base = pp * 512
                s5w[l, j, rows_gc, base + 0 + x * 64:base + 0 + x * 64 + 64] = inp["ssm_b_re"][l, g].T
                s5w[l, j, rows_gc, base + 128 + x * 64:base + 128 + x * 64 + 64] = inp["ssm_b_im"][l, g].T
                s5w[l, j, cols_xp, base + 256 + gl * 16:base + 256 + gl * 16 + 16] = inp["ssm_c_re"][l, g].T
                s5w[l, j, cols_xp, base + 384 + gl * 16:base + 384 + gl * 16 + 16] = inp["ssm_c_im"][l, g].T
    s5wT = np.zeros((nl, 4, 128, 1024), f)
    for l in range(nl):
        for pi in range(16):
            j, pp = pi // 4, pi % 4
            for x in range(2):
                g = 2 * pi + x
                gl = 2 * pp + x
                s5wT[l, j, x * 64:x * 64 + 64, pp * 256 + gl * 16:pp * 256 + gl * 16 + 16] = inp["ssm_b_re"][l, g]
                s5wT[l, j, x * 64:x * 64 + 64, pp * 256 + 128 + gl * 16:pp * 256 + 128 + gl * 16 + 16] = inp["ssm_b_im"][l, g]
    return dict(s5wT=s5wT, w_in=w_in, w_glu=np.ascontiguousarray(inp["w_glu"], f), w_out=np.ascontiguousarray(inp["w_out"], f),
                w_up=np.ascontiguousarray(inp["w_up"], f), w_down=np.ascontiguousarray(inp["w_down"], f),
                s5w=s5w, vec=vec, s5p=s5p, ident=np.eye(128, dtype=f))


def _prep_core(inp, c):
    f = np.float32
    xa = np.concatenate([inp["x_prompt"][c], inp["x_sample"][NSEQ * c:NSEQ * (c + 1)].reshape(NS_TOK, D)], axis=0)
    xT = np.ascontiguousarray(xa.T.reshape(8, 128, NTOK), f)
    sl = slice(NSEQ * c, NSEQ * (c + 1))
    h0 = np.zeros((DEPTH, 128, 2, 16, 16), f)
    for k, nm in enumerate(("state_ssm_re", "state_ssm_im")):
        s = np.asarray(inp[nm][:, sl], f).reshape(DEPTH, NSEQ, 16, 2, 64)
        h0[:, :, k] = s.transpose(0, 3, 4, 2, 1).reshape(DEPTH, 128, 16, NSEQ)
    sc = np.asarray(inp["state_conv"][:, sl], f)
    zbuf = sc.reshape(DEPTH, NSEQ, 30, 4, 128).transpose(0, 4, 3, 1, 2)
    return dict(xT=xT, h0=np.ascontiguousarray(h0),
                zbuf=np.ascontiguousarray(zbuf.reshape(DEPTH, 128, 1920)), sconv=np.ascontiguousarray(sc))


_NC_CACHE = {}


def run_device(inp, nl=DEPTH):
    if nl not in _NC_CACHE:
        _NC_CACHE[nl] = build(nl)
    nc = _NC_CACHE[nl]
    shared = _prep_shared(inp)
    in_maps = []
    for c in range(NCORES):
        m = dict(shared)
        m.update(_prep_core(inp, c))
        in_maps.append(m)
    res = run_bass_kernel_spmd(nc, in_maps, core_ids=list(range(NCORES)))
    return res.results


def assemble(results, nl=DEPTH):
    f = np.float32
    y_p = np.zeros((NCORES, NP_TOK, D), f)
    y_s = np.zeros((NCORES * NSEQ, 4, D), f)
    re_p = np.zeros((nl, NCORES, 32, 64), f)
    im_p = np.zeros((nl, NCORES, 32, 64), f)
    cv_p = np.zeros((nl, NCORES, 30, 512), f)
    re_s = np.zeros((nl, NCORES * NSEQ, 32, 64), f)
    im_s = np.zeros((nl, NCORES * NSEQ, 32, 64), f)
    cv_s = np.zeros((nl, NCORES * NSEQ, 30, 512), f)
    for c, r in enumerate(results):
        y = np.asarray(r["yT"]).reshape(D, NTOK).T
        y_p[c] = y[:NP_TOK]
        y_s[NSEQ * c:NSEQ * (c + 1)] = y[NP_TOK:].reshape(NSEQ, 4, D)
        sso = np.asarray(r["sso"])[:nl].reshape(nl, 2, 64, 2, 16, 17)
        st = sso.transpose(0, 3, 5, 4, 1, 2).reshape(nl, 2, 17, 32, 64)
        re_p[:, c], im_p[:, c] = st[:, 0, 0], st[:, 1, 0]
        re_s[:, NSEQ * c:NSEQ * (c + 1)] = st[:, 0, 1:]
        im_s[:, NSEQ * c:NSEQ * (c + 1)] = st[:, 1, 1:]
        cvp = np.asarray(r["cvp"])[:nl].reshape(nl, 128, 4, 30)
        cv_p[:, c] = cvp.transpose(0, 3, 2, 1).reshape(nl, 30, 512)
        cvs = np.asarray(r["cvs"])[:nl].reshape(nl, 128, 4, NSEQ, 4)
        zs = cvs.transpose(0, 3, 4, 2, 1).reshape(nl, NSEQ, 4, 512)
        cvc = np.asarray(r["cvc"])[:nl]
        cv_s[:, NSEQ * c:NSEQ * (c + 1)] = np.concatenate([cvc, zs], axis=2)
    return (y_p, y_s, re_p, im_p, cv_p, re_s, im_s, cv_s)


def kernel(**inputs):
    inp = {k: np.asarray(v) for k, v in inputs.items()}
    return assemble(run_device(inp, DEPTH), DEPTH)
```

```python
import math
from contextlib import ExitStack
import numpy as np
import concourse.bass as bass
import concourse.mybir as mybir
from concourse.bass_utils import run_bass_kernel_spmd

F32 = mybir.dt.float32
BF16 = mybir.dt.bfloat16
AF = mybir.ActivationFunctionType
ALU = mybir.AluOpType

NCORES = 8
D = 1024
DEPTH = 4
NP_TOK = 2048
NSEQ = 16
NS_TOK = 64
NTOK = NP_TOK + NS_TOK
TT = [(0, 512), (512, 512), (1024, 512), (1536, 512), (2048, 64)]
ZW = 30 + NP_TOK + NSEQ * 34
ZS0 = 30 + NP_TOK
EPS = 1e-6
NVL = 172
NV = NVL * DEPTH + 8
MAGIC = 12582912.0
TWO_PI = 2.0 * math.pi
T0C = 8
import os
DBG = os.environ.get('DBG_SKIP', '')


class _Op:
    __slots__ = ("eng", "fn", "deps", "semkey", "inc", "signaled", "count", "idx")

    def __init__(self, eng, fn, semkey, inc, always):
        self.eng = eng
        self.fn = fn
        self.deps = set()
        self.semkey = semkey
        self.inc = inc
        self.signaled = always
        self.count = 0


class Prog:
    ENGS = ("pe", "act", "dve", "pool", "sp")

    def __init__(self):
        self.ops = []
        self.last_write = {}
        self.readers = {}
        self.dma_hist = {}
        self.capture = None

    def _add(self, op, reads, writes, after):
        idx = len(self.ops)
        op.idx = idx
        deps = set(after)
        for r in reads:
            lw = self.last_write.get(r)
            if lw is not None:
                deps.add(lw)
        for w in writes:
            lw = self.last_write.get(w)
            if lw is not None:
                deps.add(lw)
            for rd in self.readers.get(w, ()):
                deps.add(rd)
        deps.discard(idx)
        op.deps = deps
        self.ops.append(op)
        for r in reads:
            self.readers.setdefault(r, []).append(idx)
        for w in writes:
            self.last_write[w] = idx
            self.readers[w] = []
        return idx

    def op(self, eng, fn, reads=(), writes=(), after=()):
        if self.capture is not None:
            self.capture.append((eng, fn, tuple(reads), tuple(writes), tuple(after)))
            return None
        return self._add(_Op(eng, fn, eng, 1, False), reads, writes, after)

    def flush(self, queue, n):
        for _ in range(min(n, len(queue))):
            eng, fn, reads, writes, after = queue.pop(0)
            self._add(_Op(eng, fn, eng, 1, False), reads, writes, after)

    DMA_SLOTS = {"w": 2, "io": 8}

    def dma(self, stream, fn, reads=(), writes=(), after=()):
        hist = self.dma_hist.setdefault(stream, [])
        k = self.DMA_SLOTS[stream]
        n = len(hist)
        after = list(after)
        if n >= k:
            after.append(hist[n - k])
        idx = self._add(_Op("sp", fn, "dma:%s%d" % (stream, n % k), 16, True), reads, writes, after)
        hist.append(idx)
        return idx

    def _skip(self, p, eng):
        return p.eng == eng and (not p.semkey.startswith("dma:")) and eng == "pe"

    def emit(self, block, sems):
        ops = self.ops
        for o in ops:
            for d in o.deps:
                p = ops[d]
                if self._skip(p, o.eng):
                    continue
                p.signaled = True
        counts = {}
        for o in ops:
            if o.signaled:
                counts[o.semkey] = counts.get(o.semkey, 0) + o.inc
                o.count = counts[o.semkey]
        per_eng = {e: [] for e in self.ENGS}
        for o in ops:
            per_eng[o.eng].append(o)

        def run_engine(ename, eng):
            waited = {}
            for o in per_eng[ename]:
                need = {}
                for d in o.deps:
                    p = ops[d]
                    if not p.signaled or self._skip(p, ename):
                        continue
                    if p.count > need.get(p.semkey, 0):
                        need[p.semkey] = p.count
                for k, v in need.items():
                    if waited.get(k, 0) < v:
                        eng.wait_ge(sems[k], v)
                        waited[k] = v
                ins = o.fn(eng)
                if o.signaled:
                    ins.then_inc(sems[o.semkey], o.inc)
            return waited

        @block.tensor
        def _(e):
            run_engine("pe", e)

        @block.scalar
        def _(e):
            run_engine("act", e)

        @block.vector
        def _(e):
            run_engine("dve", e)

        @block.gpsimd
        def _(e):
            run_engine("pool", e)

        @block.sync
        def _(e):
            w = run_engine("sp", e)
            for k, v in counts.items():
                if k.startswith("dma:") and w.get(k, 0) < v:
                    e.wait_ge(sems[k], v)


def build(nl=DEPTH):
    nc = bass.Bass("TRN2", target_bir_lowering=False)

    def din(name, shape):
        return nc.dram_tensor(name, shape, F32, kind="ExternalInput").ap()

    def dout(name, shape):
        return nc.dram_tensor(name, shape, F32, kind="ExternalOutput").ap()

    xT = din("xT", [8, 128, NTOK])
    w_in = din("w_in", [DEPTH, D, 1536])
    w_glu = din("w_glu", [DEPTH, 512, 512])
    w_out = din("w_out", [DEPTH, D, D])
    w_up = din("w_up", [DEPTH, D, 4096])
    w_down = din("w_down", [DEPTH, 4096, D])
    s5w = din("s5w", [DEPTH, 4, 128, 2048])
    s5wT = din("s5wT", [DEPTH, 4, 128, 1024])
    vec_d = din("vec", [128, NV])
    s5p_d = din("s5p", [128, DEPTH * 48])
    h0_d = din("h0", [DEPTH, 128, 2, 16, 16])
    zbuf_d = din("zbuf", [DEPTH, 128, 4 * 16 * 30])
    sconv_d = din("sconv", [DEPTH, NSEQ, 30, 512])
    ident_d = din("ident", [128, 128])

    yT = dout("yT", [8, 128, NTOK])
    sso_d = dout("sso", [DEPTH, 128, 2, 16, 17])
    cvp_d = dout("cvp", [DEPTH, 128, 4 * 30])
    cvs_d = dout("cvs", [DEPTH, 128, 4 * 64])
    cvc_d = dout("cvc", [DEPTH, NSEQ, 26, 512])

    P = Prog()
    with ExitStack() as es:
        def sb(name, shape, dt):
            return es.enter_context(nc.sbuf_tensor(name, shape, dt))

        X = sb("X", [128, 8, NTOK], F32)
        A = sb("A", [128, 8, NTOK], BF16)
        U = sb("U", [128, 4, NTOK], BF16)
        Z = sb("Z", [128, 4, ZW], BF16)
        S1 = sb("S1", [128, NTOK], F32)
        S2 = sb("S2", [128, NTOK], F32)
        STG = sb("STG", [128, 2, 2048], F32)
        WBF = sb("WBF", [128, 3, 2048], BF16)
        TMPF = sb("TMPF", [128, 2, 512], F32)
        VEC2 = sb("VEC2", [128, 2, NVL], F32)
        GF = sb("GF", [128, 8], F32)
        S5P = sb("S5P", [128, DEPTH * 48], F32)
        TB = sb("TB", [128, 77, 16], F32)
        PW = sb("PW", [128, 2, 11, 16], F32)
        H0 = sb("H0", [128, 2, 4, 16], F32)
        SSO = sb("SSO", [128, 2, 4, 17], F32)
        CVP = sb("CVP", [128, 4, 30], F32)
        CVS = sb("CVS", [128, 4, 64], F32)
        SACC = sb("SACC", [128, 2, NP_TOK // T0C], F32)
        SBS = sb("SBS", [128, 2, NS_TOK], F32)
        TAP = sb("TAP", [128, T0C, 128], BF16)
        RT = sb("RT", [128, NP_TOK // T0C], F32)
        HP = sb("HP", [128, 2, NP_TOK // T0C + 1], BF16)
        H0B = sb("H0B", [128, 2, NSEQ], BF16)
        LC8 = sb("LC8", [128, 2, T0C, 128], BF16)
        CAR = sb("CAR", [128, 4], F32)
        IDB = sb("IDB", [128, 128], BF16)
        NIDB = sb("NIDB", [128, 128], BF16)
        ONB = sb("ONB", [128, 128], BF16)
        DG = sb("DG", [128, 4, 128], BF16)
        PS = [es.enter_context(nc.psum_tensor(f"ps{i}", [128, 512], F32)) for i in range(8)]

        sem_names = ["pe", "act", "dve", "pool"] + ["dma:w%d" % i for i in range(2)] + ["dma:io%d" % i for i in range(8)]
        sems = {k: es.enter_context(nc.semaphore("s_" + k.replace(":", "_"))) for k in sem_names}
        block = es.enter_context(nc.Block())

        def tsl(tt):
            t0, n = TT[tt]
            return slice(t0, t0 + n)

        def zsl(tt):
            t0, n = TT[tt]
            return slice(30 + t0, 30 + t0 + n)

        def v3(ap, inner):
            return ap.rearrange("p (s t) -> p s t", t=inner)

        def Zs(j):
            return Z[:, j, ZS0:ZW].rearrange("p (s c) -> p s c", c=34)

        def Zrow(j):
            return [f"Z{j}_{t}" for t in range(5)]

        bank_ctr = [0]

        def next_bank():
            b = bank_ctr[0] % 8
            bank_ctr[0] += 1
            return b

        def mm(ps_ap, lhsT, rhs, start, stop, reads, bank):
            wres = bank if isinstance(bank, str) else f"ps{bank}"
            P.op("pe", lambda e: e.matmul(ps_ap, lhsT=lhsT, rhs=rhs, start=start, stop=stop),
                 reads=reads, writes=[wres])

        def mmt(ps_ap, lhsT, rhs, start, stop, reads, bank, col0):
            P.op("pe", lambda e: e.matmul(ps_ap, lhsT=lhsT, rhs=rhs, start=start, stop=stop, tile_position=(0, col0)),
                 reads=reads, writes=[f"ps{bank}"])

        def act(out, in_, func, reads, writes, bias=None, scale=None):
            kw = {}
            if bias is not None:
                kw["bias"] = bias
            if scale is not None:
                kw["scale"] = scale
            P.op("act", lambda e: e.activation(out=out, in_=in_, func=func, **kw), reads=reads, writes=writes)

        def tt_op(eng, out, in0, in1, op, reads, writes):
            P.op(eng, lambda e: e.tensor_tensor(out=out, in0=in0, in1=in1, op=op), reads=reads, writes=writes)

        def ts_op(eng, out, in0, s1, op0, reads, writes, s2=None, op1=None):
            if op1 is None:
                P.op(eng, lambda e: e.tensor_scalar(out=out, in0=in0, scalar1=s1, scalar2=None, op0=op0),
                     reads=reads, writes=writes)
            else:
                P.op(eng, lambda e: e.tensor_scalar(out=out, in0=in0, scalar1=s1, scalar2=s2, op0=op0, op1=op1),
                     reads=reads, writes=writes)

        def stt(eng, out, in0, scalar, in1, op0, op1, reads, writes):
            P.op(eng, lambda e: e.scalar_tensor_tensor(out=out, in0=in0, scalar=scalar, in1=in1, op0=op0, op1=op1),
                 reads=reads, writes=writes)

        def copy(eng, out, in_, reads, writes):
            P.op(eng, lambda e: e.tensor_copy(out=out, in_=in_), reads=reads, writes=writes)

        def dma_io(out, in_, reads, writes):
            P.dma("io", lambda e: e.dma_start(out=out, in_=in_), reads=reads, writes=writes)

        slabs = []

        def plan_slabs():
            for l in range(nl):
                for s in range(6):
                    slabs.append(w_in[l, :, s * 256:(s + 1) * 256].rearrange("(k p) c -> p k c", p=128))
                for j in range(4):
                    s5_idx.add(len(slabs))
                    slabs.append(s5w[l, j, :, :])
                    s5_idx.add(len(slabs))
                    slabs.append(s5wT[l, j, :, :])
                act_idx.add(len(slabs))
                slabs.append(w_glu[l, :, :].rearrange("(k p) c -> p k c", p=128))
                for s in range(4):
                    act_idx.add(len(slabs))
                    slabs.append(w_out[l, :, s * 256:(s + 1) * 256].rearrange("(k p) c -> p k c", p=128))
                for fb in range(4):
                    for s in range(4):
                        c0 = fb * 1024 + s * 256
                        slabs.append(w_up[l, :, c0:c0 + 256].rearrange("(k p) c -> p k c", p=128))
                    for s in range(4):
                        slabs.append(w_down[l, fb * 1024:(fb + 1) * 1024, s * 256:(s + 1) * 256]
                                     .rearrange("(k p) c -> p k c", p=128))

        act_idx = set()
        s5_idx = set()
        plan_slabs()
        ws = {"issued": 0, "got": 0}
        pend = {}

        def ws_emit_pending(upto=None):
            for i in sorted(pend):
                if upto is None or i <= upto:
                    out_ap, in_ap, rd, wr = pend.pop(i)
                    P.op("act", lambda e, out_ap=out_ap, in_ap=in_ap: e.activation(out=out_ap, in_=in_ap, func=AF.Copy),
                         reads=rd, writes=wr)

        def ws_issue():
            i = ws["issued"]
            s, b = i % 2, i % 3
            src = slabs[i]
            if len(src.shape) == 3:
                nel = src.shape[1] * src.shape[2]
                dst = STG[:, s, 0:nel].rearrange("p (k c) -> p k c", c=src.shape[2])
            else:
                nel = src.shape[1]
                dst = STG[:, s, 0:nel]
            ws_emit_pending(i - 2)
            P.dma("w", lambda e: e.dma_start(out=dst, in_=src), writes=[f"stg{s}"])
            if i in s5_idx or i in act_idx:
                pend[i] = (WBF[:, b, 0:nel], STG[:, s, 0:nel], [f"stg{s}"], [f"wbf{b}"])
            else:
                P.op("pool", lambda e: e.tensor_copy(out=WBF[:, b, 0:nel], in_=STG[:, s, 0:nel]),
                     reads=[f"stg{s}"], writes=[f"wbf{b}"])
            ws["issued"] += 1

        def ws_get(keep=0):
            while ws["issued"] < min(len(slabs), ws["got"] - keep + 3):
                ws_issue()
            i = ws["got"]
            ws["got"] += 1
            ws_emit_pending(i)
            if (i + 1) in act_idx and (i + 1) in pend:
                ws_emit_pending(i + 1)
            return i % 3

        def wv(b, ncols):
            return WBF[:, b, :].rearrange("p (k c) -> p k c", c=ncols)

        for kt in range(8):
            dma_io(X[:, kt, :], xT[kt, :, :], [], [f"X{kt}_{t}" for t in range(5)])
        dma_io(GF[:, :], vec_d[:, NVL * DEPTH:NVL * DEPTH + 8], [], ["GF"])
        dma_io(S5P[:, :], s5p_d[:, :], [], ["S5P"])
        dma_io(TMPF[:, 0, 0:128], ident_d[:, :], [], ["TMPF0"])
        P.dma("io", lambda e: e.dma_start(out=cvc_d[0:nl, :, :, :], in_=sconv_d[0:nl, :, 4:30, :]))
        copy("pool", IDB[:, :], TMPF[:, 0, 0:128], ["TMPF0"], ["IDB"])
        ts_op("pool", NIDB[:, :], TMPF[:, 0, 0:128], -1.0, ALU.mult, ["TMPF0"], ["NIDB"])
        P.op("pool", lambda e: e.memset(ONB[:, :], 1.0), writes=["ONB"])
        P.op("pool", lambda e: e.memset(HP[:, :, 0:1], 0.0), writes=["HP"])
        P.op("pool", lambda e: e.memset(LC8[:, :, :, :], 0.0), writes=["LC8"])
        for j in range(4):
            P.op("pool", lambda e, j=j: e.memset(Z[:, j, 0:30], 0.0), writes=[f"Zpad{j}"])

        def rmsnorm(gfn, gres, to_x):
            for kt in range(8):
                for tt in range(5):
                    sq = U[:, kt % 2, tsl(tt)]
                    act(sq, X[:, kt, tsl(tt)], AF.Square, [f"X{kt}_{tt}"], [f"U{kt % 2}_{tt}"])
                    n = TT[tt][1]
                    mm(PS[tt][:, 0:n], ONB[:, :], sq, kt == 0, kt == 7, ["ONB", f"U{kt % 2}_{tt}"], tt)
            for tt in range(5):
                n = TT[tt][1]
                act(S1[:, tsl(tt)], PS[tt][:, 0:n], AF.Ln, [f"ps{tt}"], [f"S1_{tt}"], bias=EPS, scale=1.0 / D)
                act(S1[:, tsl(tt)], S1[:, tsl(tt)], AF.Exp, [f"S1_{tt}"], [f"S1_{tt}"], scale=-0.5)
            for kt in range(8):
                for tt in range(5):
                    g = gfn(kt)
                    if to_x:
                        stt("dve", X[:, kt, tsl(tt)], X[:, kt, tsl(tt)], g, S1[:, tsl(tt)], ALU.mult, ALU.mult,
                            [f"X{kt}_{tt}", f"S1_{tt}", gres], [f"X{kt}_{tt}"])
                    else:
                        stt("dve", A[:, kt, tsl(tt)], X[:, kt, tsl(tt)], g, S1[:, tsl(tt)], ALU.mult, ALU.mult,
                            [f"X{kt}_{tt}", f"S1_{tt}", gres], [f"A{kt}_{tt}"])

        def dense_group(lhs_fn, rhs_fn, nk, tt, reads):
            b = next_bank()
            n = TT[tt][1]
            for kt in range(nk):
                mm(PS[b][:, 0:n], lhs_fn(kt), rhs_fn(kt, tt), kt == 0, kt == nk - 1, reads(kt, tt), b)
            return b, PS[b][:, 0:n]

        def tb(i):
            return TB[:, i, :]

        (DT, RHO, TH, R1, Y, K, FR, SIN1, COS1, QRE, QIM, T0_, T1_, RT0) = range(14)
        YC, KC, FRC = Y, K, FR
        NRE, DEN, INV = Y, K, FR
        ER = lambda n: 14 + n
        EI = lambda n: 23 + n
        NEI = lambda n: 32 + n
        QR = lambda n: 41 + n
        QI = lambda n: 49 + n
        NQI = lambda n: 57 + n
        RQR, RQI, RNQI = 65, 69, 73
        E1R, E1I = ER(1), EI(1)

        def tbo(eng, kind, out_i, *a):
            r = ["S5P", "TB"]
            w = ["TB"]
            if kind == "tt":
                tt_op(eng, tb(out_i), a[0], a[1], a[2], r, w)
            elif kind == "ts":
                ts_op(eng, tb(out_i), a[0], a[1], a[2], r, w)

        def cmul(out_r, out_i, ar, ai, br, bi):
            tbo("dve", "tt", T0_, tb(ar), tb(br), ALU.mult)
            tbo("dve", "tt", T1_, tb(ai), tb(bi), ALU.mult)
            tbo("dve", "tt", out_r, tb(T0_), tb(T1_), ALU.subtract)
            tbo("dve", "tt", T0_, tb(ar), tb(bi), ALU.mult)
            tbo("dve", "tt", T1_, tb(ai), tb(br), ALU.mult)
            tbo("dve", "tt", out_i, tb(T0_), tb(T1_), ALU.add)

        def s5_tables(l):
            sp0 = l * 48
            ARE, AIM, LDT = S5P[:, sp0:sp0 + 16], S5P[:, sp0 + 16:sp0 + 32], S5P[:, sp0 + 32:sp0 + 48]
            act(tb(DT), LDT, AF.Exp, ["S5P"], ["TB"])
            tbo("dve", "tt", RHO, tb(DT), ARE, ALU.mult)
            tbo("dve", "tt", TH, tb(DT), AIM, ALU.mult)
            act(tb(R1), tb(RHO), AF.Exp, ["TB"], ["TB"])
            act(tb(RT0), tb(RHO), AF.Exp, ["TB"], ["TB"], scale=float(T0C))
            tbo("dve", "ts", Y, tb(TH), 1.0 / TWO_PI, ALU.mult)
            tbo("dve", "ts", K, tb(Y), MAGIC, ALU.add)
            tbo("dve", "ts", K, tb(K), MAGIC, ALU.subtract)
            tbo("dve", "tt", FR, tb(Y), tb(K), ALU.subtract)
            act(tb(SIN1), tb(FR), AF.Sin, ["TB"], ["TB"], scale=TWO_PI * (1.0 - 1e-6))
            tbo("dve", "ts", YC, tb(Y), 0.25, ALU.add)
            tbo("dve", "ts", KC, tb(YC), MAGIC, ALU.add)
            tbo("dve", "ts", KC, tb(KC), MAGIC, ALU.subtract)
            tbo("dve", "tt", FRC, tb(YC), tb(KC), ALU.subtract)
            act(tb(COS1), tb(FRC), AF.Sin, ["TB"], ["TB"], scale=TWO_PI * (1.0 - 1e-6))
            tbo("dve", "tt", E1R, tb(R1), tb(COS1), ALU.mult)
            tbo("dve", "tt", E1I, tb(R1), tb(SIN1), ALU.mult)
            tbo("dve", "ts", NRE, tb(E1R), -1.0, ALU.add)
            tbo("dve", "tt", T0_, ARE, ARE, ALU.mult)
            tbo("dve", "tt", T1_, AIM, AIM, ALU.mult)
            tbo("dve", "tt", DEN, tb(T0_), tb(T1_), ALU.add)
            P.op("dve", lambda e: e.reciprocal(out=tb(INV), in_=tb(DEN)), reads=["TB"], writes=["TB"])
            tbo("dve", "tt", T0_, tb(NRE), ARE, ALU.mult)
            tbo("dve", "tt", T1_, tb(E1I), AIM, ALU.mult)
            tbo("dve", "tt", T0_, tb(T0_), tb(T1_), ALU.add)
            tbo("dve", "tt", QRE, tb(T0_), tb(INV), ALU.mult)
            tbo("dve", "tt", T0_, tb(E1I), ARE, ALU.mult)
            tbo("dve", "tt", T1_, tb(NRE), AIM, ALU.mult)
            tbo("dve", "tt", T0_, tb(T0_), tb(T1_), ALU.subtract)
            tbo("dve", "tt", QIM, tb(T0_), tb(INV), ALU.mult)
            P.op("dve", lambda e: e.memset(tb(ER(0)), 1.0), reads=["TB"], writes=["TB"])
            P.op("dve", lambda e: e.memset(tb(EI(0)), 0.0), reads=["TB"], writes=["TB"])
            for n in range(1, T0C):
                cmul(ER(n + 1), EI(n + 1), ER(n), EI(n), E1R, E1I)
            for n in range(T0C):
                cmul(QR(n), QI(n), ER(n), EI(n), QRE, QIM)
                tbo("dve", "ts", NQI(n), tb(QI(n)), -1.0, ALU.mult)
            for n in range(T0C + 1):
                tbo("dve", "ts", NEI(n), tb(EI(n)), -1.0, ALU.mult)
            for t_ in range(4):
                copy("dve", tb(RQR + t_), tb(QR(3 - t_)), ["TB"], ["TB"])
                copy("dve", tb(RQI + t_), tb(QI(3 - t_)), ["TB"], ["TB"])
                copy("dve", tb(RNQI + t_), tb(NQI(3 - t_)), ["TB"], ["TB"])
            copy("dve", PW[:, 0, 0, :], tb(COS1), ["TB"], ["PW"])
            copy("dve", PW[:, 1, 0, :], tb(SIN1), ["TB"], ["PW"])
            for lv in range(10):
                pr, pi = PW[:, 0, lv, :], PW[:, 1, lv, :]
                tt_op("dve", tb(T0_), pr, pr, ALU.mult, ["PW", "TB"], ["TB"])
                tt_op("dve", tb(T1_), pi, pi, ALU.mult, ["PW", "TB"], ["TB"])
                tt_op("dve", PW[:, 0, lv + 1, :], tb(T0_), tb(T1_), ALU.subtract, ["TB", "PW"], ["PW"])
                tt_op("dve", tb(T0_), pr, pi, ALU.mult, ["PW", "TB"], ["TB"])
                ts_op("dve", PW[:, 1, lv + 1, :], tb(T0_), 2.0, ALU.mult, ["TB", "PW"], ["PW"])


        P.capture = []
        s5_tables(0)
        tblq = P.capture
        P.capture = None
        for l in range(nl):
            vb = 0
            VEC = VEC2[:, l % 2, :]
            VR = f"VEC{l % 2}"
            dma_io(VEC2[:, l % 2, :], vec_d[:, l * NVL:(l + 1) * NVL], [], [VR])

            rmsnorm(lambda kt: VEC[:, kt:kt + 1], VR, False)

            dma_io(S2[:, 0:1920], zbuf_d[l, :, :], [], [f"S2_{t}" for t in range(4)])
            for j in range(4):
                src = S2[:, j * 480:(j + 1) * 480].rearrange("p (s c) -> p s c", c=30)
                copy("pool", Zs(j)[:, :, 0:30], src, [f"S2_{t}" for t in range(4)], [f"Z{j}_4"])

            for m in range(12):
                ws_emit_pending()
                P.flush(tblq, 30)
                if m % 2 == 0:
                    wb = ws_get()
                wvw = wv(wb, 256)
                mc = slice((m % 2) * 128, (m % 2) * 128 + 128)
                bias = VEC[:, vb + 16 + m:vb + 17 + m]
                for tt in range(5):
                    b, ps = dense_group(lambda kt: wvw[:, kt, mc], lambda kt, tt: A[:, kt, tsl(tt)], 8, tt,
                                        lambda kt, tt: [f"wbf{wb}", f"A{kt}_{tt}"])
                    n = TT[tt][1]
                    if m < 4:
                        ti = T0C if tt < 4 else 4
                        ts_op("dve", U[:, m, tsl(tt)].rearrange("p (s k) -> p s k", s=ti),
                              ps.rearrange("p (k s) -> p s k", s=ti), bias, ALU.add, [f"ps{b}", VR], [f"U{m}_{tt}"])
                    elif m % 2 == 0:
                        act(S1[:, tsl(tt)], ps, AF.Sigmoid, [f"ps{b}", VR], [f"S1_{tt}"], bias=bias)
                    else:
                        j = (m - 5) // 2
                        if tt < 4:
                            stt("dve", Z[:, j, zsl(tt)], ps, bias, S1[:, tsl(tt)], ALU.add, ALU.mult,
                                [f"ps{b}", f"S1_{tt}", VR], [f"Z{j}_{tt}"])
                            if tt == 3:
                                stt("dve", CVP[:, j, :], ps[:, 482:512], bias, S1[:, 2018:2048], ALU.add, ALU.mult,
                                    [f"ps{b}", f"S1_{tt}", VR], ["CVP"])
                        else:
                            stt("dve", Zs(j)[:, :, 30:34], v3(ps, 4), bias, v3(S1[:, tsl(4)], 4), ALU.add, ALU.mult,
                                [f"ps{b}", f"S1_{tt}", VR], [f"Z{j}_4"])
                            stt("dve", CVS[:, j, :], ps, bias, S1[:, tsl(4)], ALU.add, ALU.mult,
                                [f"ps{b}", f"S1_{tt}", VR], ["CVS"])
            P.flush(tblq, 10 ** 6)
            dma_io(cvp_d[l, :, :], CVP[:, :, :].rearrange("p a b -> p (a b)"), ["CVP"], [])
            dma_io(cvs_d[l, :, :], CVS[:, :, :].rearrange("p a b -> p (a b)"), ["CVS"], [])

            NCH = NP_TOK // T0C
            CPT = 512 // T0C
            LG = T0C.bit_length() - 1
            SRE, SIM_ = SACC[:, 0, :], SACC[:, 1, :]
            DRE, DIM_, TMP, GRE, GIM = (S1[:, i * NCH:(i + 1) * NCH] for i in range(5))
            rS1 = ["S1_0", "S1_1", "S1_2"]
            T8A = S1[:, 1280:1536].rearrange("p (n w) -> p n w", w=32)
            T8B = S1[:, 1536:1792].rearrange("p (n w) -> p n w", w=32)
            rT8 = ["S1_2", "S1_3"]

            def coef8(dst, dres, cre_w, cim_w, wres, i_re, i_im, i_nim, pi_):
                def tab(i):
                    return TB[:, i:i + T0C, pi_].unsqueeze(2).broadcast_to([128, T0C, 32])
                crb = cre_w.unsqueeze(1).broadcast_to([128, T0C, 32])
                cib = cim_w.unsqueeze(1).broadcast_to([128, T0C, 32])
                rd = [wres, "TB"]
                tt_op("dve", T8A, crb, tab(i_re), ALU.mult, rd, rT8)
                tt_op("dve", T8B, cib, tab(i_im), ALU.mult, rd, rT8)
                tt_op("dve", dst[:, 0, :, :], T8A, T8B, ALU.subtract, rT8, [dres])
                tt_op("dve", T8A, crb, tab(i_nim), ALU.mult, rd, rT8)
                tt_op("dve", T8B, cib, tab(i_re), ALU.mult, rd, rT8)
                tt_op("dve", dst[:, 1, :, :], T8A, T8B, ALU.subtract, rT8, [dres])

            for j in range(4):
                wa = ws_get()
                for k_ in range(2):
                    dma_io(H0[:, k_, :, :], h0_d[l, :, k_, 4 * j:4 * j + 4, :], [], ["H0"])
                rwa = f"wbf{wa}"

                def blk(pp, i):
                    return WBF[:, wa, pp * 512 + i * 128:pp * 512 + (i + 1) * 128]

                def blkT(pp, i):
                    return WBF[:, wb2, pp * 256 + i * 128:pp * 256 + (i + 1) * 128]

                WCv = S2[:, 0:2048].rearrange("p (a b k) -> p a b k", a=4, b=2)
                rWC = ["S2_0", "S2_1", "S2_2", "S2_3"]
                def pair_stage(pp, stage):
                    pi_ = 4 * j + pp
                    sc = lambda i: TB[:, i, pi_:pi_ + 1]
                    bre, bim, cre, cim = blk(pp, 0), blk(pp, 1), blk(pp, 2), blk(pp, 3)
                    last = (pp == 3)
                    WCR, WCI = WCv[:, pp, 0, :], WCv[:, pp, 1, :]
                    h0r, h0i = H0[:, 0, pp, :], H0[:, 1, pp, :]
                    if stage in ("B0", "B1"):
                        ubv = U[:, j, 0:NP_TOK].rearrange("p (t s k) -> p t s k", t=4, s=T0C)
                        usv = U[:, j, tsl(4)].rearrange("p (t q) -> p t q", t=4)
                        nbu = [0]
                        BW = TMPF[:, :, :].rearrange("p a b -> p (a b)").bitcast(BF16).rearrange("p (a b c) -> p a b c", a=2, b=2)

                        def bu_mm(s_):
                            b = 5 + s_ % 2
                            rhs = ubv[:, :, s_, :]
                            ures = [f"U{j}_{t}" for t in range(4)]
                            mm(PS[b][:, 0:NCH], bre, rhs, True, True, [rwa] + ures, b)
                            mm(PS[b][:, 256:256 + NCH], bim, rhs, True, True, [rwa] + ures, b)

                        def bu_evac(s_):
                            b = 5 + s_ % 2
                            f = s_ % 2
                            n_e = T0C - 1 - s_
                            o1, o2 = BW[:, f, 0, :], BW[:, f, 1, :]
                            act(o1, PS[b][:, :], AF.Copy, [f"ps{b}", "TB"], [f"TMPF{f}"], scale=sc(QR(n_e)))
                            act(o2, PS[b][:, :], AF.Copy, [f"ps{b}", "TB"], [f"TMPF{f}"], scale=sc(QI(n_e)))

                        def bu_sacc(s_):
                            f = s_ % 2
                            o1, o2 = BW[:, f, 0, :], BW[:, f, 1, :]
                            rd = [f"TMPF{f}", "IDB", "NIDB"]
                            mm(PS[7][:, 0:NCH], IDB[:, :], o1[:, 0:NCH], s_ == 0, False, rd, 7)
                            mm(PS[7][:, 0:NCH], NIDB[:, :], o2[:, 256:256 + NCH], False, False, rd, 7)
                            mm(PS[7][:, 256:256 + NCH], IDB[:, :], o2[:, 0:NCH], False, False, rd, 7)
                            mm(PS[7][:, 256:256 + NCH], IDB[:, :], o1[:, 256:256 + NCH], False, s_ == T0C - 1, rd, 7)

                        def bu_step(rhs, ncol, dst_re, dst_im, n_e, first, ures):
                            b = 6
                            p_re, p_im = PS[b][:, 0:ncol], PS[b][:, 256:256 + ncol]
                            mm(p_re, bre, rhs, True, True, [rwa] + ures, b)
                            mm(p_im, bim, rhs, True, True, [rwa] + ures, b)
                            rb = [f"ps{b}", "TB", "SACC"]
                            if first:
                                ts_op("dve", dst_re, p_re, sc(QR(n_e)), ALU.mult, rb, ["SACC"])
                                ts_op("dve", dst_im, p_re, sc(QI(n_e)), ALU.mult, rb, ["SACC"])
                            else:
                                stt("dve", dst_re, p_re, sc(QR(n_e)), dst_re, ALU.mult, ALU.add, rb, ["SACC"])
                                stt("dve", dst_im, p_re, sc(QI(n_e)), dst_im, ALU.mult, ALU.add, rb, ["SACC"])
                            stt("dve", dst_re, p_im, sc(NQI(n_e)), dst_re, ALU.mult, ALU.add, rb, ["SACC"])
                            stt("dve", dst_im, p_im, sc(QR(n_e)), dst_im, ALU.mult, ALU.add, rb, ["SACC"])

                        if stage == "B0":
                            bu_mm(0)
                            bu_mm(1)
                            bu_evac(0)
                            bu_evac(1)
                        else:
                            for s_ in range(T0C):
                                bu_sacc(s_)
                                if s_ + 2 < T0C:
                                    bu_mm(s_ + 2)
                                    bu_evac(s_ + 2)
                    elif stage == "SMP":
                        mm(PS[6][:, 0:NS_TOK], bre, U[:, j, tsl(4)], True, True, [rwa, f"U{j}_4"], 6)
                        mm(PS[6][:, 256:256 + NS_TOK], bim, U[:, j, tsl(4)], True, True, [rwa, f"U{j}_4"], 6)
                        copy("dve", SBS[:, :, :], PS[6][:, :].rearrange("p (a k) -> p a k", a=2)[:, :, 0:NS_TOK], ["ps6"], ["SBS"])
                        ss_re, ss_im = SSO[:, 0, pp, 1:17], SSO[:, 1, pp, 1:17]
                        b_re = SBS[:, 0, :].rearrange("p (t q) -> p t q", t=4)
                        b_im = SBS[:, 1, :].rearrange("p (t q) -> p t q", t=4)
                        P1, P2 = S1[:, 1792:1856], S1[:, 1856:1920]
                        p1v = P1.rearrange("p (t q) -> p t q", t=4)
                        p2v = P2.rearrange("p (t q) -> p t q", t=4)

                        def tabr(i0_):
                            return TB[:, i0_:i0_ + 4, pi_].unsqueeze(2).broadcast_to([128, 4, NSEQ])

                        rb = ["SBS", "TB", "S1_3"]
                        for dst, ta, tb2 in ((ss_re, RQR, RNQI), (ss_im, RQI, RQR)):
                            tt_op("dve", p1v, b_re, tabr(ta), ALU.mult, rb, ["S1_3"])
                            tt_op("dve", p2v, b_im, tabr(tb2), ALU.mult, rb, ["S1_3"])
                            tt_op("dve", P1, P1, P2, ALU.add, ["S1_3"], ["S1_3"])
                            P.op("dve", lambda e, dst=dst: e.tensor_reduce(
                                out=dst, in_=P1.rearrange("p (t q) -> p q t", t=4), axis=mybir.AxisListType.X, op=ALU.add),
                                reads=["S1_3"], writes=["SSO"])
                        stt("dve", ss_re, h0r, sc(ER(4)), ss_re, ALU.mult, ALU.add, ["H0", "TB", "SSO"], ["SSO"])
                        stt("dve", ss_re, h0i, sc(NEI(4)), ss_re, ALU.mult, ALU.add, ["H0", "TB", "SSO"], ["SSO"])
                        stt("dve", ss_im, h0r, sc(EI(4)), ss_im, ALU.mult, ALU.add, ["H0", "TB", "SSO"], ["SSO"])
                        stt("dve", ss_im, h0i, sc(ER(4)), ss_im, ALU.mult, ALU.add, ["H0", "TB", "SSO"], ["SSO"])
                    elif stage == "ROTC":
                        copy("dve", SACC[:, :, 0:NCH], PS[7][:, :].rearrange("p (a k) -> p a k", a=2)[:, :, 0:NCH], ["ps7"], ["SACC"])
                    elif stage == "ROT":
                        s_re, s_im = SACC[:, 0, 0:NCH], SACC[:, 1, 0:NCH]
                        tt_op("dve", DRE, WCR, s_re, ALU.mult, rWC + ["SACC"], ["S1_0"])
                        tt_op("dve", TMP, WCI, s_im, ALU.mult, rWC + ["SACC"], ["S1_1"])
                        tt_op("dve", DRE, DRE, TMP, ALU.add, ["S1_0", "S1_1"], ["S1_0"])
                        tt_op("dve", DIM_, WCR, s_im, ALU.mult, rWC + ["SACC"], ["S1_0"])
                        tt_op("dve", TMP, WCI, s_re, ALU.mult, rWC + ["SACC"], ["S1_1"])
                        tt_op("dve", DIM_, DIM_, TMP, ALU.subtract, ["S1_0", "S1_1"], ["S1_0"])
                    elif stage == "SCAN":
                        act(RT[:, :], WCR, AF.Identity, rWC + ["TB"], ["RT"], bias=sc(RT0), scale=0.0)
                        P.op("dve", lambda e: e.tensor_tensor_scan(out=GRE, data0=RT[:, :], data1=DRE, initial=0.0,
                                                                   op0=ALU.mult, op1=ALU.add),
                             reads=["RT", "S1_0"], writes=["S1_1"])
                        P.op("dve", lambda e: e.tensor_tensor_scan(out=GIM, data0=RT[:, :], data1=DIM_, initial=0.0,
                                                                   op0=ALU.mult, op1=ALU.add),
                             reads=["RT", "S1_0"], writes=["S1_2"])
                        e_ = NCH - 1
                        tt_op("dve", CAR[:, 2:3], WCI[:, e_:e_ + 1], GIM[:, e_:e_ + 1], ALU.mult, rWC + ["S1_2", "CAR"], ["CAR"])
                        stt("dve", SSO[:, 0, pp, 0:1], WCR[:, e_:e_ + 1], GRE[:, e_:e_ + 1], CAR[:, 2:3], ALU.mult, ALU.subtract,
                            rWC + ["S1_1", "CAR"], ["SSO"])
                        tt_op("dve", CAR[:, 3:4], WCR[:, e_:e_ + 1], GIM[:, e_:e_ + 1], ALU.mult, rWC + ["S1_2", "CAR"], ["CAR"])
                        stt("dve", SSO[:, 1, pp, 0:1], WCI[:, e_:e_ + 1], GRE[:, e_:e_ + 1], CAR[:, 3:4], ALU.mult, ALU.add,
                            rWC + ["S1_1", "CAR"], ["SSO"])
                        tt_op("dve", DRE, WCR, GRE, ALU.mult, rWC + ["S1_1"], ["S1_0"])
                        tt_op("dve", TMP, WCI, GIM, ALU.mult, rWC + ["S1_2"], ["S1_1"])
                        tt_op("dve", HP[:, 0, 1:NCH + 1], DRE, TMP, ALU.subtract, ["S1_0", "S1_1"], ["HP"])
                        tt_op("dve", DIM_, WCI, GRE, ALU.mult, rWC + ["S1_1"], ["S1_0"])
                        tt_op("dve", TMP, WCR, GIM, ALU.mult, rWC + ["S1_2"], ["S1_1"])
                        tt_op("dve", HP[:, 1, 1:NCH + 1], DIM_, TMP, ALU.add, ["S1_0", "S1_1"], ["HP"])
                    elif stage == "C":
                        copy("dve", H0B[:, 0, :], h0r, ["H0"], ["H0B"])
                        copy("dve", H0B[:, 1, :], h0i, ["H0"], ["H0B"])
                        if 'c' in DBG:
                            return
                        w0 = 32 * pp
                        wz = 32 * ((pp - 1) % 4)
                        P.op("pool", lambda e, wz=wz: e.memset(LC8[:, :, :, wz:wz + 32], 0.0), writes=["LC8"])
                        coef8(LC8[:, :, :, w0:w0 + 32], "LC8", cre[:, w0:w0 + 32], cim[:, w0:w0 + 32], rwa,
                              ER(1), EI(1), NEI(1), pi_)
                        for jj in range(T0C):
                            fin = last and jj == T0C - 1
                            for tt in range(4):
                                pv = PS[tt][:, jj * CPT:(jj + 1) * CPT]
                                mm(pv, LC8[:, 0, jj, :], HP[:, 0, tt * CPT:(tt + 1) * CPT], False, False, ["LC8", "HP"], tt)
                                mm(pv, LC8[:, 1, jj, :], HP[:, 1, tt * CPT:(tt + 1) * CPT], False, fin, ["LC8", "HP"], tt)
                            if jj < 4:
                                pv = PS[4][:, jj * NSEQ:(jj + 1) * NSEQ]
                                mm(pv, LC8[:, 0, jj, :], H0B[:, 0, :], False, False, ["LC8", "H0B"], 4)
                                mm(pv, LC8[:, 1, jj, :], H0B[:, 1, :], False, last and jj == 3, ["LC8", "H0B"], 4)

                pair_stage(0, "B0")
                pair_stage(0, "B1")
                pair_stage(1, "B0")
                wb2 = ws_get(keep=1)
                rwb = f"wbf{wb2}"
                SCR = S1[:, 1280:1792].rearrange("p (a k) -> p a k", a=4)
                copy("dve", WCv[:, :, 0, 0:1], PW[:, 0, LG, 4 * j:4 * j + 4].unsqueeze(2), ["PW"], rWC)
                copy("dve", WCv[:, :, 1, 0:1], PW[:, 1, LG, 4 * j:4 * j + 4].unsqueeze(2), ["PW"], rWC)
                lv = 0
                while (1 << lv) < NCH and 'd' not in DBG:
                    m_ = 1 << lv
                    prb = PW[:, 0, LG + lv, 4 * j:4 * j + 4].unsqueeze(2).broadcast_to([128, 4, m_])
                    pib = PW[:, 1, LG + lv, 4 * j:4 * j + 4].unsqueeze(2).broadcast_to([128, 4, m_])
                    sR, sI = WCv[:, :, 0, 0:m_], WCv[:, :, 1, 0:m_]
                    dR, dI = WCv[:, :, 0, m_:2 * m_], WCv[:, :, 1, m_:2 * m_]
                    tt_op("dve", SCR[:, :, 0:m_], sI, pib, ALU.mult, rWC + ["PW"], rT8)
                    tt_op("dve", dR, sR, prb, ALU.mult, rWC + ["PW"], rWC)
                    tt_op("dve", dR, dR, SCR[:, :, 0:m_], ALU.subtract, rWC + rT8, rWC)
                    tt_op("dve", SCR[:, :, 0:m_], sR, pib, ALU.mult, rWC + ["PW"], rT8)
                    tt_op("dve", dI, sI, prb, ALU.mult, rWC + ["PW"], rWC)
                    tt_op("dve", dI, dI, SCR[:, :, 0:m_], ALU.add, rWC + rT8, rWC)
                    lv += 1
                for pp in range(4 if 't' not in DBG else 0):
                    pi_ = 4 * j + pp
                    w0 = 32 * pp
                    coef8(LC8[:, :, :, 0:32], "LC8", blk(pp, 2)[:, w0:w0 + 32], blk(pp, 3)[:, w0:w0 + 32], rwa, QR(0), QI(0), NQI(0), pi_)
                    for tau in range(T0C):
                        pst = PS[tau // 4][:, (tau % 4) * 128 + w0:(tau % 4) * 128 + w0 + 32]
                        mm(pst, blkT(pp, 0), LC8[:, 0, tau, 0:32], True, False, [rwb, "LC8"], tau // 4)
                        mm(pst, blkT(pp, 1), LC8[:, 1, tau, 0:32], False, True, [rwb, "LC8"], tau // 4)
                for tau in range(T0C if 't' not in DBG else 0):
                    pst = PS[tau // 4][:, (tau % 4) * 128:(tau % 4 + 1) * 128]
                    act(TAP[:, tau, :], pst, AF.Copy, [f"ps{tau // 4}"], [f"TAP{tau}"])
                    if tau == 0:
                        stt("dve", TAP[:, 0, :], IDB[:, :], VEC[:, vb + 28 + j:vb + 29 + j], TAP[:, 0, :], ALU.mult, ALU.add,
                            ["IDB", VR, "TAP0"], ["TAP0"])
                for tt in range(5):
                    t0, n = TT[tt]
                    ti = T0C if tt < 4 else 4
                    cw = n // ti
                    for tau in range(ti):
                        mm(PS[tt][:, tau * cw:n], TAP[:, tau, :], U[:, j, t0:t0 + (ti - tau) * cw], tau == 0, False,
                           [f"TAP{tau}", f"U{j}_{tt}"], tt)

                pair_stage(0, "ROTC")
                pair_stage(0, "SMP")
                pair_stage(0, "ROT")
                for pp in range(4):
                    if pp < 3:
                        pair_stage(pp + 1, "B1")
                    pair_stage(pp, "SCAN")
                    if pp < 2:
                        pair_stage(pp + 2, "B0")
                    pair_stage(pp, "C")
                    if pp == 1:
                        ws_emit_pending()
                    if pp < 3:
                        pair_stage(pp + 1, "ROTC")
                        pair_stage(pp + 1, "SMP")
                        pair_stage(pp + 1, "ROT")
                for k_ in range(2):
                    dma_io(sso_d[l, :, k_, 4 * j:4 * j + 4, :], SSO[:, k_, :, :], ["SSO"], [])
                for tt in range(5):
                    n = TT[tt][1]
                    ti = T0C if tt < 4 else 4
                    act(A[:, j, tsl(tt)].rearrange("p (k s) -> p k s", s=ti),
                        PS[tt][:, 0:n].rearrange("p (s k) -> p k s", s=ti), AF.Gelu_apprx_tanh,
                        [f"ps{tt}"], [f"A{j}_{tt}"])
            bank_ctr[0] = 0

            wb = ws_get()
            wvw = wv(wb, 512)
            for m in range(4):
                bias = VEC[:, vb + 32 + m:vb + 33 + m]
                for tt in range(5):
                    b, ps = dense_group(lambda kt: wvw[:, kt, m * 128:(m + 1) * 128],
                                        lambda kt, tt: A[:, kt, tsl(tt)], 4, tt,
                                        lambda kt, tt: [f"wbf{wb}", f"A{kt}_{tt}"])
                    act(S1[:, tsl(tt)], ps, AF.Sigmoid, [f"ps{b}", VR], [f"S1_{tt}"], bias=bias)
                    tt_op("dve", U[:, m, tsl(tt)], A[:, m, tsl(tt)], S1[:, tsl(tt)], ALU.mult,
                          [f"A{m}_{tt}", f"S1_{tt}"], [f"U{m}_{tt}"])

            for j in range(4):
                banks = [next_bank() for _ in range(5)]
                for k in range(31):
                    d = k % 4
                    ts_op("dve", DG[:, d, :], IDB[:, :], VEC[:, vb + 48 + j * 31 + k:vb + 49 + j * 31 + k], ALU.mult,
                          ["IDB", VR], [f"DG{d}"])
                    for tt in range(4):
                        t0 = TT[tt][0]
                        rd = [f"DG{d}", f"Z{j}_{tt}", f"Zpad{j}"] + ([f"Z{j}_{tt - 1}"] if tt > 0 else [])
                        mm(PS[banks[tt]][:, :], DG[:, d, :], Z[:, j, t0 + k:t0 + k + 512], k == 0, k == 30, rd, banks[tt])
                    mm(v3(PS[banks[4]][:, 0:64], 4), DG[:, d, :], Zs(j)[:, :, k:k + 4], k == 0, k == 30,
                       [f"DG{d}", f"Z{j}_4"], banks[4])
                bias = VEC[:, vb + 36 + j:vb + 37 + j]
                for tt in range(5):
                    n = TT[tt][1]
                    act(A[:, 4 + j, tsl(tt)], PS[banks[tt]][:, 0:n], AF.Identity, [f"ps{banks[tt]}", VR],
                        [f"A{4 + j}_{tt}"], bias=bias)
            bank_ctr[0] = 0
            for tt in range(5):
                n = TT[tt][1]
                b, ps = dense_group(lambda kt: ONB[:, :], lambda kt, tt: A[:, 4 + kt, tsl(tt)], 4, tt,
                                    lambda kt, tt: ["ONB", f"A{4 + kt}_{tt}"])
                act(S1[:, tsl(tt)], ps, AF.Copy, [f"ps{b}"], [f"S1_{tt}"], scale=1.0 / 512)
            for j in range(4):
                for tt in range(5):
                    act(Z[:, j, tsl(tt)], A[:, 4 + j, tsl(tt)], AF.Square, [f"A{4 + j}_{tt}"], Zrow(j) + [f"Zpad{j}"])
            for tt in range(5):
                b, ps = dense_group(lambda kt: ONB[:, :], lambda kt, tt: Z[:, kt, tsl(tt)], 4, tt,
                                    lambda kt, tt: ["ONB"] + Zrow(kt))
                tt_op("dve", S2[:, tsl(tt)], S1[:, tsl(tt)], S1[:, tsl(tt)], ALU.mult, [f"S1_{tt}"], [f"S2_{tt}"])
                stt("dve", S2[:, tsl(tt)], ps, 1.0 / 512, S2[:, tsl(tt)], ALU.mult, ALU.subtract,
                    [f"ps{b}", f"S2_{tt}"], [f"S2_{tt}"])
                act(S2[:, tsl(tt)], S2[:, tsl(tt)], AF.Ln, [f"S2_{tt}"], [f"S2_{tt}"], bias=EPS)
                act(S2[:, tsl(tt)], S2[:, tsl(tt)], AF.Exp, [f"S2_{tt}"], [f"S2_{tt}"], scale=-0.5)
            tf = [0]
            for j in range(4):
                lg, lb = VEC[:, vb + 40 + j:vb + 41 + j], VEC[:, vb + 44 + j:vb + 45 + j]
                for tt in range(5):
                    n = TT[tt][1]
                    f = tf[0] % 2
                    tf[0] += 1
                    tmp = TMPF[:, f, 0:n]
                    tt_op("dve", tmp, A[:, 4 + j, tsl(tt)], S1[:, tsl(tt)], ALU.subtract,
                          [f"A{4 + j}_{tt}", f"S1_{tt}"], [f"TMPF{f}"])
                    tt_op("dve", tmp, tmp, S2[:, tsl(tt)], ALU.mult, [f"TMPF{f}", f"S2_{tt}"], [f"TMPF{f}"])
                    act(A[:, 4 + j, tsl(tt)], tmp, AF.Silu, [f"TMPF{f}", VR], [f"A{4 + j}_{tt}"], bias=lb, scale=lg)

            for m in range(8):
                if m % 2 == 0:
                    wb = ws_get()
                wvw = wv(wb, 256)
                mc = slice((m % 2) * 128, (m % 2) * 128 + 128)
                for tt in range(5):
                    b, ps = dense_group(lambda kt: wvw[:, kt, mc],
                                        lambda kt, tt: (U[:, kt, tsl(tt)] if kt < 4 else A[:, kt, tsl(tt)]), 8, tt,
                                        lambda kt, tt: [f"wbf{wb}", (f"U{kt}_{tt}" if kt < 4 else f"A{kt}_{tt}")])
                    tt_op("dve", X[:, m, tsl(tt)], X[:, m, tsl(tt)], ps, ALU.add, [f"X{m}_{tt}", f"ps{b}"], [f"X{m}_{tt}"])

            rmsnorm(lambda kt: VEC[:, 8 + kt:9 + kt], VR, False)
            bank_ctr[0] = 0

            def Hh(m, tt):
                return U[:, m, tsl(tt)] if m < 4 else Z[:, m - 4, tsl(tt)]

            def Hr(m, tt):
                return [f"U{m}_{tt}"] if m < 4 else Zrow(m - 4) + [f"Zpad{m - 4}"]

            for fb in range(4):
                if fb == 0 and l + 1 < nl:
                    P.capture = []
                    s5_tables(l + 1)
                    tblq = P.capture
                    P.capture = None
                for m in range(8):
                    P.flush(tblq, 6)
                    if m % 2 == 0:
                        wb = ws_get()
                    wvw = wv(wb, 256)
                    mc = slice((m % 2) * 128, (m % 2) * 128 + 128)
                    for tt in range(5):
                        n = TT[tt][1]
                        b, ps = dense_group(lambda kt: wvw[:, kt, mc], lambda kt, tt: A[:, kt, tsl(tt)], 8, tt,
                                            lambda kt, tt: [f"wbf{wb}", f"A{kt}_{tt}"])
                        f = tf[0] % 2
                        tf[0] += 1
                        tmp = TMPF[:, f, 0:n]
                        act(tmp, ps, AF.Relu, [f"ps{b}"], [f"TMPF{f}"])
                        act(Hh(m, tt), tmp, AF.Square, [f"TMPF{f}"], Hr(m, tt))
                for m in range(8):
                    P.flush(tblq, 6)
                    if m % 2 == 0:
                        wb = ws_get()
                    wvw = wv(wb, 256)
                    mc = slice((m % 2) * 128, (m % 2) * 128 + 128)
                    for tt in range(5):
                        b, ps = dense_group(lambda kt: wvw[:, kt, mc], lambda kt, tt: Hh(kt, tt), 8, tt,
                                            lambda kt, tt: [f"wbf{wb}"] + Hr(kt, tt))
                        tt_op("dve", X[:, m, tsl(tt)], X[:, m, tsl(tt)], ps, ALU.add,
                              [f"X{m}_{tt}", f"ps{b}"], [f"X{m}_{tt}"])
            P.flush(tblq, 10 ** 6)
            for j in range(4):
                P.op("pool", lambda e, j=j: e.memset(Z[:, j, 0:30], 0.0), writes=Zrow(j) + [f"Zpad{j}"])
            bank_ctr[0] = 0

        rmsnorm(lambda kt: GF[:, kt:kt + 1], "GF", True)
        for kt in range(8):
            dma_io(yT[kt, :, :], X[:, kt, :], [f"X{kt}_{t}" for t in range(5)], [])

        P.emit(block, sems)
    return nc


def _prep_shared(inp):
    f = np.float32
    nl = DEPTH
    order = list(range(0, 512))
    for j in range(4):
        order += list(range(1024 + 128 * j, 1024 + 128 * (j + 1)))
        order += list(range(512 + 128 * j, 512 + 128 * (j + 1)))
    order = np.array(order)
    w_in = np.ascontiguousarray(inp["w_in"][:, :, order])
    b_in = inp["b_in"][:, order]
    vec = np.zeros((128, NV), f)

    def cols(v, n):
        return np.asarray(v, f).reshape(n, 128).T

    for l in range(nl):
        b = l * NVL
        vec[:, b + 0:b + 8] = cols(inp["norm_mix_g"][l], 8)
        vec[:, b + 8:b + 16] = cols(inp["norm_mlp_g"][l], 8)
        vec[:, b + 16:b + 28] = cols(b_in[l], 12)
        vec[:, b + 28:b + 32] = cols(inp["ssm_d"][l], 4)
        vec[:, b + 32:b + 36] = cols(inp["b_glu"][l], 4)
        vec[:, b + 36:b + 40] = cols(inp["conv_b"][l], 4)
        vec[:, b + 40:b + 44] = cols(inp["conv_ln_g"][l], 4)
        vec[:, b + 44:b + 48] = cols(inp["conv_ln_b"][l], 4)
        cw = np.asarray(inp["conv_w"][l], f)
        vec[:, b + 48:b + 172] = cw.reshape(31, 4, 128).transpose(2, 1, 0).reshape(128, 124)
    vec[:, NVL * nl:NVL * nl + 8] = cols(inp["norm_f_g"], 8)

    s5p = np.zeros((128, nl * 48), f)
    for l in range(nl):
        for nm, off in (("ssm_a_re", 0), ("ssm_a_im", 16)):
            a = np.asarray(inp[nm][l], f).reshape(16, 2, 64)
            s5p[:, l * 48 + off:l * 48 + off + 16] = a.transpose(1, 2, 0).reshape(128, 16)
        ld = np.asarray(inp["ssm_log_dt"][l], f).reshape(16, 2)
        s5p[:, l * 48 + 32:l * 48 + 48] = np.repeat(ld.T[:, None, :], 64, axis=1).reshape(128, 16)

    s5w = np.zeros((nl, 4, 128, 2048), f)
    for l in range(nl):
        for pi in range(16):
            j, pp = pi // 4, pi % 4
            for x in range(2):
                g = 2 * pi + x
                gl = 2 * pp + x
                rows_gc = slice(gl * 16, gl * 16 + 16)
                cols_xp = slice(x * 64, x * 64 + 64)
                base = pp * 512
                s5w[l, j, rows_gc, base + 0 + x * 64:base + 0 + x * 64 + 64] = inp["ssm_b_re"][l, g].T
                s5w[l, j, rows_gc, base + 128 + x * 64:base + 128 + x * 64 + 64] = inp["ssm_b_im"][l, g].T
                s5w[l, j, cols_xp, base + 256 + gl * 16:base + 256 + gl * 16 + 16] = inp["ssm_c_re"][l, g].T
                s5w[l, j, cols_xp, base + 384 + gl * 16:base + 384 + gl * 16 + 16] = inp["ssm_c_im"][l, g].T
    s5wT = np.zeros((nl, 4, 128, 1024), f)
    for l in range(nl):
        for pi in range(16):
            j, pp = pi // 4, pi % 4
            for x in range(2):
                g = 2 * pi + x
                gl = 2 * pp + x
                s5wT[l, j, x * 64:x * 64 + 64, pp * 256 + gl * 16:pp * 256 + gl * 16 + 16] = inp["ssm_b_re"][l, g]
                s5wT[l, j, x * 64:x * 64 + 64, pp * 256 + 128 + gl * 16:pp * 256 + 128 + gl * 16 + 16] = inp["ssm_b_im"][l, g]
    return dict(s5wT=s5wT, w_in=w_in, w_glu=np.ascontiguousarray(inp["w_glu"], f), w_out=np.ascontiguousarray(inp["w_out"], f),
                w_up=np.ascontiguousarray(inp["w_up"], f), w_down=np.ascontiguousarray(inp["w_down"], f),
                s5w=s5w, vec=vec, s5p=s5p, ident=np.eye(128, dtype=f))


def _prep_core(inp, c):
    f = np.float32
    xa = np.concatenate([inp["x_prompt"][c], inp["x_sample"][NSEQ * c:NSEQ * (c + 1)].reshape(NS_TOK, D)], axis=0)
    xT = np.ascontiguousarray(xa.T.reshape(8, 128, NTOK), f)
    sl = slice(NSEQ * c, NSEQ * (c + 1))
    h0 = np.zeros((DEPTH, 128, 2, 16, 16), f)
    for k, nm in enumerate(("state_ssm_re", "state_ssm_im")):
        s = np.asarray(inp[nm][:, sl], f).reshape(DEPTH, NSEQ, 16, 2, 64)
        h0[:, :, k] = s.transpose(0, 3, 4, 2, 1).reshape(DEPTH, 128, 16, NSEQ)
    sc = np.asarray(inp["state_conv"][:, sl], f)
    zbuf = sc.reshape(DEPTH, NSEQ, 30, 4, 128).transpose(0, 4, 3, 1, 2)
    return dict(xT=xT, h0=np.ascontiguousarray(h0),
                zbuf=np.ascontiguousarray(zbuf.reshape(DEPTH, 128, 1920)), sconv=np.ascontiguousarray(sc))


_NC_CACHE = {}


def run_device(inp, nl=DEPTH):
    if nl not in _NC_CACHE:
        _NC_CACHE[nl] = build(nl)
    nc = _NC_CACHE[nl]
    shared = _prep_shared(inp)
    in_maps = []
    for c in range(NCORES):
        m = dict(shared)
        m.update(_prep_core(inp, c))
        in_maps.append(m)
    res = run_bass_kernel_spmd(nc, in_maps, core_ids=list(range(NCORES)))
    return res.results


def assemble(results, nl=DEPTH):
    f = np.float32
    y_p = np.zeros((NCORES, NP_TOK, D), f)
    y_s = np.zeros((NCORES * NSEQ, 4, D), f)
    re_p = np.zeros((nl, NCORES, 32, 64), f)
    im_p = np.zeros((nl, NCORES, 32, 64), f)
    cv_p = np.zeros((nl, NCORES, 30, 512), f)
    re_s = np.zeros((nl, NCORES * NSEQ, 32, 64), f)
    im_s = np.zeros((nl, NCORES * NSEQ, 32, 64), f)
    cv_s = np.zeros((nl, NCORES * NSEQ, 30, 512), f)
    for c, r in enumerate(results):
        y = np.asarray(r["yT"]).reshape(D, NTOK).T
        y_p[c] = y[:NP_TOK]
        y_s[NSEQ * c:NSEQ * (c + 1)] = y[NP_TOK:].reshape(NSEQ, 4, D)
        sso = np.asarray(r["sso"])[:nl].reshape(nl, 2, 64, 2, 16, 17)
        st = sso.transpose(0, 3, 5, 4, 1, 2).reshape(nl, 2, 17, 32, 64)
        re_p[:, c], im_p[:, c] = st[:, 0, 0], st[:, 1, 0]
        re_s[:, NSEQ * c:NSEQ * (c + 1)] = st[:, 0, 1:]
        im_s[:, NSEQ * c:NSEQ * (c + 1)] = st[:, 1, 1:]
        cvp = np.asarray(r["cvp"])[:nl].reshape(nl, 128, 4, 30)
        cv_p[:, c] = cvp.transpose(0, 3, 2, 1).reshape(nl, 30, 512)
        cvs = np.asarray(r["cvs"])[:nl].reshape(nl, 128, 4, NSEQ, 4)
        zs = cvs.transpose(0, 3, 4, 2, 1).reshape(nl, NSEQ, 4, 512)
        cvc = np.asarray(r["cvc"])[:nl]
        cv_s[:, NSEQ * c:NSEQ * (c + 1)] = np.concatenate([cvc, zs], axis=2)
    return (y_p, y_s, re_p, im_p, cv_p, re_s, im_s, cv_s)


def kernel(**inputs):
    inp = {k: np.asarray(v) for k, v in inputs.items()}
    return assemble(run_device(inp, DEPTH), DEPTH)
```

```python
import math
from contextlib import ExitStack
import numpy as np
import concourse.bass as bass
import concourse.mybir as mybir
from concourse.bass_utils import run_bass_kernel_spmd

F32 = mybir.dt.float32
BF16 = mybir.dt.bfloat16
AF = mybir.ActivationFunctionType
ALU = mybir.AluOpType

NCORES = 8
D = 1024
DEPTH = 4
NP_TOK = 2048
NSEQ = 16
NS_TOK = 64
NTOK = NP_TOK + NS_TOK
TT = [(0, 512), (512, 512), (1024, 512), (1536, 512), (2048, 64)]
ZW = 30 + NP_TOK + NSEQ * 34
ZS0 = 30 + NP_TOK
EPS = 1e-6
NVL = 172
NV = NVL * DEPTH + 8
MAGIC = 12582912.0
TWO_PI = 2.0 * math.pi
T0C = 8
import os
DBG = os.environ.get('DBG_SKIP', '')


class _Op:
    __slots__ = ("eng", "fn", "deps", "semkey", "inc", "signaled", "count", "idx")

    def __init__(self, eng, fn, semkey, inc, always):
        self.eng = eng
        self.fn = fn
        self.deps = set()
        self.semkey = semkey
        self.inc = inc
        self.signaled = always
        self.count = 0


class Prog:
    ENGS = ("pe", "act", "dve", "pool", "sp")

    def __init__(self):
        self.ops = []
        self.last_write = {}
        self.readers = {}
        self.dma_hist = {}
        self.capture = None

    def _add(self, op, reads, writes, after):
        idx = len(self.ops)
        op.idx = idx
        deps = set(after)
        for r in reads:
            lw = self.last_write.get(r)
            if lw is not None:
                deps.add(lw)
        for w in writes:
            lw = self.last_write.get(w)
            if lw is not None:
                deps.add(lw)
            for rd in self.readers.get(w, ()):
                deps.add(rd)
        deps.discard(idx)
        op.deps = deps
        self.ops.append(op)
        for r in reads:
            self.readers.setdefault(r, []).append(idx)
        for w in writes:
            self.last_write[w] = idx
            self.readers[w] = []
        return idx

    def op(self, eng, fn, reads=(), writes=(), after=()):
        if self.capture is not None:
            self.capture.append((eng, fn, tuple(reads), tuple(writes), tuple(after)))
            return None
        return self._add(_Op(eng, fn, eng, 1, False), reads, writes, after)

    def flush(self, queue, n):
        for _ in range(min(n, len(queue))):
            eng, fn, reads, writes, after = queue.pop(0)
            self._add(_Op(eng, fn, eng, 1, False), reads, writes, after)

    DMA_SLOTS = {"w": 2, "io": 8}

    def dma(self, stream, fn, reads=(), writes=(), after=()):
        hist = self.dma_hist.setdefault(stream, [])
        k = self.DMA_SLOTS[stream]
        n = len(hist)
        after = list(after)
        if n >= k:
            after.append(hist[n - k])
        idx = self._add(_Op("sp", fn, "dma:%s%d" % (stream, n % k), 16, True), reads, writes, after)
        hist.append(idx)
        return idx

    def _skip(self, p, eng):
        return p.eng == eng and (not p.semkey.startswith("dma:")) and eng == "pe"

    def emit(self, block, sems):
        ops = self.ops
        for o in ops:
            for d in o.deps:
                p = ops[d]
                if self._skip(p, o.eng):
                    continue
                p.signaled = True
        counts = {}
        for o in ops:
            if o.signaled:
                counts[o.semkey] = counts.get(o.semkey, 0) + o.inc
                o.count = counts[o.semkey]
        per_eng = {e: [] for e in self.ENGS}
        for o in ops:
            per_eng[o.eng].append(o)

        def run_engine(ename, eng):
            waited = {}
            for o in per_eng[ename]:
                need = {}
                for d in o.deps:
                    p = ops[d]
                    if not p.signaled or self._skip(p, ename):
                        continue
                    if p.count > need.get(p.semkey, 0):
                        need[p.semkey] = p.count
                for k, v in need.items():
                    if waited.get(k, 0) < v:
                        eng.wait_ge(sems[k], v)
                        waited[k] = v
                ins = o.fn(eng)
                if o.signaled:
                    ins.then_inc(sems[o.semkey], o.inc)
            return waited

        @block.tensor
        def _(e):
            run_engine("pe", e)

        @block.scalar
        def _(e):
            run_engine("act", e)

        @block.vector
        def _(e):
            run_engine("dve", e)

        @block.gpsimd
        def _(e):
            run_engine("pool", e)

        @block.sync
        def _(e):
            w = run_engine("sp", e)
            for k, v in counts.items():
                if k.startswith("dma:") and w.get(k, 0) < v:
                    e.wait_ge(sems[k], v)


def build(nl=DEPTH):
    nc = bass.Bass("TRN2", target_bir_lowering=False)

    def din(name, shape):
        return nc.dram_tensor(name, shape, F32, kind="ExternalInput").ap()

    def dout(name, shape):
        return nc.dram_tensor(name, shape, F32, kind="ExternalOutput").ap()

    xT = din("xT", [8, 128, NTOK])
    w_in = din("w_in", [DEPTH, D, 1536])
    w_glu = din("w_glu", [DEPTH, 512, 512])
    w_out = din("w_out", [DEPTH, D, D])
    w_up = din("w_up", [DEPTH, D, 4096])
    w_down = din("w_down", [DEPTH, 4096, D])
    s5w = din("s5w", [DEPTH, 4, 128, 2048])
    s5wT = din("s5wT", [DEPTH, 4, 128, 1024])
    vec_d = din("vec", [128, NV])
    s5p_d = din("s5p", [128, DEPTH * 48])
    h0_d = din("h0", [DEPTH, 128, 2, 16, 16])
    zbuf_d = din("zbuf", [DEPTH, 128, 4 * 16 * 30])
    sconv_d = din("sconv", [DEPTH, NSEQ, 30, 512])
    ident_d = din("ident", [128, 128])

    yT = dout("yT", [8, 128, NTOK])
    sso_d = dout("sso", [DEPTH, 128, 2, 16, 17])
    cvp_d = dout("cvp", [DEPTH, 128, 4 * 30])
    cvs_d = dout("cvs", [DEPTH, 128, 4 * 64])
    cvc_d = dout("cvc", [DEPTH, NSEQ, 26, 512])

    P = Prog()
    with ExitStack() as es:
        def sb(name, shape, dt):
            return es.enter_context(nc.sbuf_tensor(name, shape, dt))

        X = sb("X", [128, 8, NTOK], F32)
        A = sb("A", [128, 8, NTOK], BF16)
        U = sb("U", [128, 4, NTOK], BF16)
        Z = sb("Z", [128, 4, ZW], BF16)
        S1 = sb("S1", [128, NTOK], F32)
        S2 = sb("S2", [128, NTOK], F32)
        STG = sb("STG", [128, 2, 2048], F32)
        WBF = sb("WBF", [128, 3, 2048], BF16)
        TMPF = sb("TMPF", [128, 2, 512], F32)
        VEC2 = sb("VEC2", [128, 2, NVL], F32)
        GF = sb("GF", [128, 8], F32)
        S5P = sb("S5P", [128, DEPTH * 48], F32)
        TB = sb("TB", [128, 77, 16], F32)
        PW = sb("PW", [128, 2, 11, 16], F32)
        H0 = sb("H0", [128, 2, 4, 16], F32)
        SSO = sb("SSO", [128, 2, 4, 17], F32)
        CVP = sb("CVP", [128, 4, 30], F32)
        CVS = sb("CVS", [128, 4, 64], F32)
        SACC = sb("SACC", [128, 2, NP_TOK // T0C], F32)
        SBS = sb("SBS", [128, 2, NS_TOK], F32)
        TAP = sb("TAP", [128, T0C, 128], BF16)
        RT = sb("RT", [128, NP_TOK // T0C], F32)
        HP = sb("HP", [128, 2, NP_TOK // T0C + 1], BF16)
        H0B = sb("H0B", [128, 2, NSEQ], BF16)
        LC8 = sb("LC8", [128, 2, T0C, 128], BF16)
        CAR = sb("CAR", [128, 4], F32)
        IDB = sb("IDB", [128, 128], BF16)
        NIDB = sb("NIDB", [128, 128], BF16)
        ONB = sb("ONB", [128, 128], BF16)
        DG = sb("DG", [128, 4, 128], BF16)
        PS = [es.enter_context(nc.psum_tensor(f"ps{i}", [128, 512], F32)) for i in range(8)]

        sem_names = ["pe", "act", "dve", "pool"] + ["dma:w%d" % i for i in range(2)] + ["dma:io%d" % i for i in range(8)]
        sems = {k: es.enter_context(nc.semaphore("s_" + k.replace(":", "_"))) for k in sem_names}
        block = es.enter_context(nc.Block())

        def tsl(tt):
            t0, n = TT[tt]
            return slice(t0, t0 + n)

        def zsl(tt):
            t0, n = TT[tt]
            return slice(30 + t0, 30 + t0 + n)

        def v3(ap, inner):
            return ap.rearrange("p (s t) -> p s t", t=inner)

        def Zs(j):
            return Z[:, j, ZS0:ZW].rearrange("p (s c) -> p s c", c=34)

        def Zrow(j):
            return [f"Z{j}_{t}" for t in range(5)]

        bank_ctr = [0]

        def next_bank():
            b = bank_ctr[0] % 8
            bank_ctr[0] += 1
            return b

        def mm(ps_ap, lhsT, rhs, start, stop, reads, bank):
            wres = bank if isinstance(bank, str) else f"ps{bank}"
            P.op("pe", lambda e: e.matmul(ps_ap, lhsT=lhsT, rhs=rhs, start=start, stop=stop),
                 reads=reads, writes=[wres])

        def mmt(ps_ap, lhsT, rhs, start, stop, reads, bank, col0):
            P.op("pe", lambda e: e.matmul(ps_ap, lhsT=lhsT, rhs=rhs, start=start, stop=stop, tile_position=(0, col0)),
                 reads=reads, writes=[f"ps{bank}"])

        def act(out, in_, func, reads, writes, bias=None, scale=None):
            kw = {}
            if bias is not None:
                kw["bias"] = bias
            if scale is not None:
                kw["scale"] = scale
            P.op("act", lambda e: e.activation(out=out, in_=in_, func=func, **kw), reads=reads, writes=writes)

        def tt_op(eng, out, in0, in1, op, reads, writes):
            P.op(eng, lambda e: e.tensor_tensor(out=out, in0=in0, in1=in1, op=op), reads=reads, writes=writes)

        def ts_op(eng, out, in0, s1, op0, reads, writes, s2=None, op1=None):
            if op1 is None:
                P.op(eng, lambda e: e.tensor_scalar(out=out, in0=in0, scalar1=s1, scalar2=None, op0=op0),
                     reads=reads, writes=writes)
            else:
                P.op(eng, lambda e: e.tensor_scalar(out=out, in0=in0, scalar1=s1, scalar2=s2, op0=op0, op1=op1),
                     reads=reads, writes=writes)

        def stt(eng, out, in0, scalar, in1, op0, op1, reads, writes):
            P.op(eng, lambda e: e.scalar_tensor_tensor(out=out, in0=in0, scalar=scalar, in1=in1, op0=op0, op1=op1),
                 reads=reads, writes=writes)

        def copy(eng, out, in_, reads, writes):
            P.op(eng, lambda e: e.tensor_copy(out=out, in_=in_), reads=reads, writes=writes)

        def dma_io(out, in_, reads, writes):
            P.dma("io", lambda e: e.dma_start(out=out, in_=in_), reads=reads, writes=writes)

        slabs = []

        def plan_slabs():
            for l in range(nl):
                for s in range(6):
                    act_idx.add(len(slabs))
                    slabs.append(w_in[l, :, s * 256:(s + 1) * 256].rearrange("(k p) c -> p k c", p=128))
                for j in range(4):
                    s5_idx.add(len(slabs))
                    slabs.append(s5w[l, j, :, :])
                    s5_idx.add(len(slabs))
                    slabs.append(s5wT[l, j, :, :])
                act_idx.add(len(slabs))
                slabs.append(w_glu[l, :, :].rearrange("(k p) c -> p k c", p=128))
                for s in range(4):
                    act_idx.add(len(slabs))
                    slabs.append(w_out[l, :, s * 256:(s + 1) * 256].rearrange("(k p) c -> p k c", p=128))
                for fb in range(4):
                    for s in range(4):
                        c0 = fb * 1024 + s * 256
                        slabs.append(w_up[l, :, c0:c0 + 256].rearrange("(k p) c -> p k c", p=128))
                    for s in range(4):
                        slabs.append(w_down[l, fb * 1024:(fb + 1) * 1024, s * 256:(s + 1) * 256]
                                     .rearrange("(k p) c -> p k c", p=128))

        act_idx = set()
        s5_idx = set()
        plan_slabs()
        ws = {"issued": 0, "got": 0}
        pend = {}

        def ws_emit_pending(upto=None):
            for i in sorted(pend):
                if upto is None or i <= upto:
                    out_ap, in_ap, rd, wr = pend.pop(i)
                    P.op("act", lambda e, out_ap=out_ap, in_ap=in_ap: e.activation(out=out_ap, in_=in_ap, func=AF.Copy),
                         reads=rd, writes=wr)

        def ws_issue():
            i = ws["issued"]
            s, b = i % 2, i % 3
            src = slabs[i]
            if len(src.shape) == 3:
                nel = src.shape[1] * src.shape[2]
                dst = STG[:, s, 0:nel].rearrange("p (k c) -> p k c", c=src.shape[2])
            else:
                nel = src.shape[1]
                dst = STG[:, s, 0:nel]
            ws_emit_pending(i - 2)
            P.dma("w", lambda e: e.dma_start(out=dst, in_=src), writes=[f"stg{s}"])
            if i in s5_idx or i in act_idx:
                pend[i] = (WBF[:, b, 0:nel], STG[:, s, 0:nel], [f"stg{s}"], [f"wbf{b}"])
            else:
                P.op("pool", lambda e: e.tensor_copy(out=WBF[:, b, 0:nel], in_=STG[:, s, 0:nel]),
                     reads=[f"stg{s}"], writes=[f"wbf{b}"])
            ws["issued"] += 1

        def ws_get(keep=0):
            while ws["issued"] < min(len(slabs), ws["got"] - keep + 3):
                ws_issue()
            i = ws["got"]
            ws["got"] += 1
            ws_emit_pending(i)
            if (i + 1) in act_idx and (i + 1) in pend:
                ws_emit_pending(i + 1)
            return i % 3

        def wv(b, ncols):
            return WBF[:, b, :].rearrange("p (k c) -> p k c", c=ncols)

        for kt in range(8):
            dma_io(X[:, kt, :], xT[kt, :, :], [], [f"X{kt}_{t}" for t in range(5)])
        dma_io(GF[:, :], vec_d[:, NVL * DEPTH:NVL * DEPTH + 8], [], ["GF"])
        dma_io(S5P[:, :], s5p_d[:, :], [], ["S5P"])
        dma_io(TMPF[:, 0, 0:128], ident_d[:, :], [], ["TMPF0"])
        P.dma("io", lambda e: e.dma_start(out=cvc_d[0:nl, :, :, :], in_=sconv_d[0:nl, :, 4:30, :]))
        copy("pool", IDB[:, :], TMPF[:, 0, 0:128], ["TMPF0"], ["IDB"])
        ts_op("pool", NIDB[:, :], TMPF[:, 0, 0:128], -1.0, ALU.mult, ["TMPF0"], ["NIDB"])
        P.op("pool", lambda e: e.memset(ONB[:, :], 1.0), writes=["ONB"])
        P.op("pool", lambda e: e.memset(HP[:, :, 0:1], 0.0), writes=["HP"])
        P.op("pool", lambda e: e.memset(LC8[:, :, :, :], 0.0), writes=["LC8"])
        for j in range(4):
            P.op("pool", lambda e, j=j: e.memset(Z[:, j, 0:30], 0.0), writes=[f"Zpad{j}"])

        def rmsnorm(gfn, gres, to_x):
            for kt in range(8):
                for tt in range(5):
                    sq = U[:, kt % 2, tsl(tt)]
                    act(sq, X[:, kt, tsl(tt)], AF.Square, [f"X{kt}_{tt}"], [f"U{kt % 2}_{tt}"])
                    n = TT[tt][1]
                    mm(PS[tt][:, 0:n], ONB[:, :], sq, kt == 0, kt == 7, ["ONB", f"U{kt % 2}_{tt}"], tt)
            for tt in range(5):
                n = TT[tt][1]
                act(S1[:, tsl(tt)], PS[tt][:, 0:n], AF.Ln, [f"ps{tt}"], [f"S1_{tt}"], bias=EPS, scale=1.0 / D)
                act(S1[:, tsl(tt)], S1[:, tsl(tt)], AF.Exp, [f"S1_{tt}"], [f"S1_{tt}"], scale=-0.5)
            for kt in range(8):
                for tt in range(5):
                    g = gfn(kt)
                    if to_x:
                        stt("dve", X[:, kt, tsl(tt)], X[:, kt, tsl(tt)], g, S1[:, tsl(tt)], ALU.mult, ALU.mult,
                            [f"X{kt}_{tt}", f"S1_{tt}", gres], [f"X{kt}_{tt}"])
                    else:
                        stt("dve", A[:, kt, tsl(tt)], X[:, kt, tsl(tt)], g, S1[:, tsl(tt)], ALU.mult, ALU.mult,
                            [f"X{kt}_{tt}", f"S1_{tt}", gres], [f"A{kt}_{tt}"])

        def dense_group(lhs_fn, rhs_fn, nk, tt, reads):
            b = next_bank()
            n = TT[tt][1]
            for kt in range(nk):
                mm(PS[b][:, 0:n], lhs_fn(kt), rhs_fn(kt, tt), kt == 0, kt == nk - 1, reads(kt, tt), b)
            return b, PS[b][:, 0:n]

        def tb(i):
            return TB[:, i, :]

        (DT, RHO, TH, R1, Y, K, FR, SIN1, COS1, QRE, QIM, T0_, T1_, RT0) = range(14)
        YC, KC, FRC = Y, K, FR
        NRE, DEN, INV = Y, K, FR
        ER = lambda n: 14 + n
        EI = lambda n: 23 + n
        NEI = lambda n: 32 + n
        QR = lambda n: 41 + n
        QI = lambda n: 49 + n
        NQI = lambda n: 57 + n
        RQR, RQI, RNQI = 65, 69, 73
        E1R, E1I = ER(1), EI(1)

        def tbo(eng, kind, out_i, *a):
            r = ["S5P", "TB"]
            w = ["TB"]
            if kind == "tt":
                tt_op(eng, tb(out_i), a[0], a[1], a[2], r, w)
            elif kind == "ts":
                ts_op(eng, tb(out_i), a[0], a[1], a[2], r, w)

        def cmul(out_r, out_i, ar, ai, br, bi):
            tbo("dve", "tt", T0_, tb(ar), tb(br), ALU.mult)
            tbo("dve", "tt", T1_, tb(ai), tb(bi), ALU.mult)
            tbo("dve", "tt", out_r, tb(T0_), tb(T1_), ALU.subtract)
            tbo("dve", "tt", T0_, tb(ar), tb(bi), ALU.mult)
            tbo("dve", "tt", T1_, tb(ai), tb(br), ALU.mult)
            tbo("dve", "tt", out_i, tb(T0_), tb(T1_), ALU.add)

        def s5_tables(l):
            sp0 = l * 48
            ARE, AIM, LDT = S5P[:, sp0:sp0 + 16], S5P[:, sp0 + 16:sp0 + 32], S5P[:, sp0 + 32:sp0 + 48]
            act(tb(DT), LDT, AF.Exp, ["S5P"], ["TB"])
            tbo("dve", "tt", RHO, tb(DT), ARE, ALU.mult)
            tbo("dve", "tt", TH, tb(DT), AIM, ALU.mult)
            act(tb(R1), tb(RHO), AF.Exp, ["TB"], ["TB"])
            act(tb(RT0), tb(RHO), AF.Exp, ["TB"], ["TB"], scale=float(T0C))
            tbo("dve", "ts", Y, tb(TH), 1.0 / TWO_PI, ALU.mult)
            tbo("dve", "ts", K, tb(Y), MAGIC, ALU.add)
            tbo("dve", "ts", K, tb(K), MAGIC, ALU.subtract)
            tbo("dve", "tt", FR, tb(Y), tb(K), ALU.subtract)
            act(tb(SIN1), tb(FR), AF.Sin, ["TB"], ["TB"], scale=TWO_PI * (1.0 - 1e-6))
            tbo("dve", "ts", YC, tb(Y), 0.25, ALU.add)
            tbo("dve", "ts", KC, tb(YC), MAGIC, ALU.add)
            tbo("dve", "ts", KC, tb(KC), MAGIC, ALU.subtract)
            tbo("dve", "tt", FRC, tb(YC), tb(KC), ALU.subtract)
            act(tb(COS1), tb(FRC), AF.Sin, ["TB"], ["TB"], scale=TWO_PI * (1.0 - 1e-6))
            tbo("dve", "tt", E1R, tb(R1), tb(COS1), ALU.mult)
            tbo("dve", "tt", E1I, tb(R1), tb(SIN1), ALU.mult)
            tbo("dve", "ts", NRE, tb(E1R), -1.0, ALU.add)
            tbo("dve", "tt", T0_, ARE, ARE, ALU.mult)
            tbo("dve", "tt", T1_, AIM, AIM, ALU.mult)
            tbo("dve", "tt", DEN, tb(T0_), tb(T1_), ALU.add)
            P.op("dve", lambda e: e.reciprocal(out=tb(INV), in_=tb(DEN)), reads=["TB"], writes=["TB"])
            tbo("dve", "tt", T0_, tb(NRE), ARE, ALU.mult)
            tbo("dve", "tt", T1_, tb(E1I), AIM, ALU.mult)
            tbo("dve", "tt", T0_, tb(T0_), tb(T1_), ALU.add)
            tbo("dve", "tt", QRE, tb(T0_), tb(INV), ALU.mult)
            tbo("dve", "tt", T0_, tb(E1I), ARE, ALU.mult)
            tbo("dve", "tt", T1_, tb(NRE), AIM, ALU.mult)
            tbo("dve", "tt", T0_, tb(T0_), tb(T1_), ALU.subtract)
            tbo("dve", "tt", QIM, tb(T0_), tb(INV), ALU.mult)
            P.op("dve", lambda e: e.memset(tb(ER(0)), 1.0), reads=["TB"], writes=["TB"])
            P.op("dve", lambda e: e.memset(tb(EI(0)), 0.0), reads=["TB"], writes=["TB"])
            for n in range(1, T0C):
                cmul(ER(n + 1), EI(n + 1), ER(n), EI(n), E1R, E1I)
            for n in range(T0C):
                cmul(QR(n), QI(n), ER(n), EI(n), QRE, QIM)
                tbo("dve", "ts", NQI(n), tb(QI(n)), -1.0, ALU.mult)
            for n in range(T0C + 1):
                tbo("dve", "ts", NEI(n), tb(EI(n)), -1.0, ALU.mult)
            for t_ in range(4):
                copy("dve", tb(RQR + t_), tb(QR(3 - t_)), ["TB"], ["TB"])
                copy("dve", tb(RQI + t_), tb(QI(3 - t_)), ["TB"], ["TB"])
                copy("dve", tb(RNQI + t_), tb(NQI(3 - t_)), ["TB"], ["TB"])
            copy("dve", PW[:, 0, 0, :], tb(COS1), ["TB"], ["PW"])
            copy("dve", PW[:, 1, 0, :], tb(SIN1), ["TB"], ["PW"])
            for lv in range(10):
                pr, pi = PW[:, 0, lv, :], PW[:, 1, lv, :]
                tt_op("dve", tb(T0_), pr, pr, ALU.mult, ["PW", "TB"], ["TB"])
                tt_op("dve", tb(T1_), pi, pi, ALU.mult, ["PW", "TB"], ["TB"])
                tt_op("dve", PW[:, 0, lv + 1, :], tb(T0_), tb(T1_), ALU.subtract, ["TB", "PW"], ["PW"])
                tt_op("dve", tb(T0_), pr, pi, ALU.mult, ["PW", "TB"], ["TB"])
                ts_op("dve", PW[:, 1, lv + 1, :], tb(T0_), 2.0, ALU.mult, ["TB", "PW"], ["PW"])


        P.capture = []
        s5_tables(0)
        tblq = P.capture
        P.capture = None
        for l in range(nl):
            vb = 0
            VEC = VEC2[:, l % 2, :]
            VR = f"VEC{l % 2}"
            dma_io(VEC2[:, l % 2, :], vec_d[:, l * NVL:(l + 1) * NVL], [], [VR])

            rmsnorm(lambda kt: VEC[:, kt:kt + 1], VR, False)

            dma_io(S2[:, 0:1920], zbuf_d[l, :, :], [], [f"S2_{t}" for t in range(4)])
            for j in range(4):
                src = S2[:, j * 480:(j + 1) * 480].rearrange("p (s c) -> p s c", c=30)
                copy("pool", Zs(j)[:, :, 0:30], src, [f"S2_{t}" for t in range(4)], [f"Z{j}_4"])

            for m in range(12):
                ws_emit_pending()
                P.flush(tblq, 30)
                if m % 2 == 0:
                    wb = ws_get()
                wvw = wv(wb, 256)
                mc = slice((m % 2) * 128, (m % 2) * 128 + 128)
                bias = VEC[:, vb + 16 + m:vb + 17 + m]
                for tt in range(5):
                    b, ps = dense_group(lambda kt: wvw[:, kt, mc], lambda kt, tt: A[:, kt, tsl(tt)], 8, tt,
                                        lambda kt, tt: [f"wbf{wb}", f"A{kt}_{tt}"])
                    n = TT[tt][1]
                    if m < 4:
                        ti = T0C if tt < 4 else 4
                        ts_op("dve", U[:, m, tsl(tt)].rearrange("p (s k) -> p s k", s=ti),
                              ps.rearrange("p (k s) -> p s k", s=ti), bias, ALU.add, [f"ps{b}", VR], [f"U{m}_{tt}"])
                    elif m % 2 == 0:
                        act(S1[:, tsl(tt)], ps, AF.Sigmoid, [f"ps{b}", VR], [f"S1_{tt}"], bias=bias)
                    else:
                        j = (m - 5) // 2
                        if tt < 4:
                            stt("dve", Z[:, j, zsl(tt)], ps, bias, S1[:, tsl(tt)], ALU.add, ALU.mult,
                                [f"ps{b}", f"S1_{tt}", VR], [f"Z{j}_{tt}"])
                            if tt == 3:
                                stt("dve", CVP[:, j, :], ps[:, 482:512], bias, S1[:, 2018:2048], ALU.add, ALU.mult,
                                    [f"ps{b}", f"S1_{tt}", VR], ["CVP"])
                        else:
                            stt("dve", Zs(j)[:, :, 30:34], v3(ps, 4), bias, v3(S1[:, tsl(4)], 4), ALU.add, ALU.mult,
                                [f"ps{b}", f"S1_{tt}", VR], [f"Z{j}_4"])
                            stt("dve", CVS[:, j, :], ps, bias, S1[:, tsl(4)], ALU.add, ALU.mult,
                                [f"ps{b}", f"S1_{tt}", VR], ["CVS"])
            P.flush(tblq, 10 ** 6)
            dma_io(cvp_d[l, :, :], CVP[:, :, :].rearrange("p a b -> p (a b)"), ["CVP"], [])
            dma_io(cvs_d[l, :, :], CVS[:, :, :].rearrange("p a b -> p (a b)"), ["CVS"], [])

            NCH = NP_TOK // T0C
            CPT = 512 // T0C
            LG = T0C.bit_length() - 1
            SRE, SIM_ = SACC[:, 0, :], SACC[:, 1, :]
            DRE, DIM_, TMP, GRE, GIM = (S1[:, i * NCH:(i + 1) * NCH] for i in range(5))
            rS1 = ["S1_0", "S1_1", "S1_2"]
            T8A = S1[:, 1280:1536].rearrange("p (n w) -> p n w", w=32)
            T8B = S1[:, 1536:1792].rearrange("p (n w) -> p n w", w=32)
            rT8 = ["S1_2", "S1_3"]

            def coef8(dst, dres, cre_w, cim_w, wres, i_re, i_im, i_nim, pi_):
                def tab(i):
                    return TB[:, i:i + T0C, pi_].unsqueeze(2).broadcast_to([128, T0C, 32])
                crb = cre_w.unsqueeze(1).broadcast_to([128, T0C, 32])
                cib = cim_w.unsqueeze(1).broadcast_to([128, T0C, 32])
                rd = [wres, "TB"]
                tt_op("dve", T8A, crb, tab(i_re), ALU.mult, rd, rT8)
                tt_op("dve", T8B, cib, tab(i_im), ALU.mult, rd, rT8)
                tt_op("dve", dst[:, 0, :, :], T8A, T8B, ALU.subtract, rT8, [dres])
                tt_op("dve", T8A, crb, tab(i_nim), ALU.mult, rd, rT8)
                tt_op("dve", T8B, cib, tab(i_re), ALU.mult, rd, rT8)
                tt_op("dve", dst[:, 1, :, :], T8A, T8B, ALU.subtract, rT8, [dres])

            for j in range(4):
                wa = ws_get()
                for k_ in range(2):
                    dma_io(H0[:, k_, :, :], h0_d[l, :, k_, 4 * j:4 * j + 4, :], [], ["H0"])
                rwa = f"wbf{wa}"

                def blk(pp, i):
                    return WBF[:, wa, pp * 512 + i * 128:pp * 512 + (i + 1) * 128]

                def blkT(pp, i):
                    return WBF[:, wb2, pp * 256 + i * 128:pp * 256 + (i + 1) * 128]

                WCv = S2[:, 0:2048].rearrange("p (a b k) -> p a b k", a=4, b=2)
                rWC = ["S2_0", "S2_1", "S2_2", "S2_3"]
                def pair_stage(pp, stage):
                    pi_ = 4 * j + pp
                    sc = lambda i: TB[:, i, pi_:pi_ + 1]
                    bre, bim, cre, cim = blk(pp, 0), blk(pp, 1), blk(pp, 2), blk(pp, 3)
                    last = (pp == 3)
                    WCR, WCI = WCv[:, pp, 0, :], WCv[:, pp, 1, :]
                    h0r, h0i = H0[:, 0, pp, :], H0[:, 1, pp, :]
                    if stage in ("B0", "B1"):
                        ubv = U[:, j, 0:NP_TOK].rearrange("p (t s k) -> p t s k", t=4, s=T0C)
                        usv = U[:, j, tsl(4)].rearrange("p (t q) -> p t q", t=4)
                        nbu = [0]
                        BW = TMPF[:, :, :].rearrange("p a b -> p (a b)").bitcast(BF16).rearrange("p (a b c) -> p a b c", a=2, b=2)

                        def bu_mm(s_):
                            b = 5 + s_ % 2
                            rhs = ubv[:, :, s_, :]
                            ures = [f"U{j}_{t}" for t in range(4)]
                            mm(PS[b][:, 0:NCH], bre, rhs, True, True, [rwa] + ures, b)
                            mm(PS[b][:, 256:256 + NCH], bim, rhs, True, True, [rwa] + ures, b)

                        def bu_evac(s_):
                            b = 5 + s_ % 2
                            f = s_ % 2
                            n_e = T0C - 1 - s_
                            o1, o2 = BW[:, f, 0, :], BW[:, f, 1, :]
                            act(o1, PS[b][:, :], AF.Copy, [f"ps{b}", "TB"], [f"TMPF{f}"], scale=sc(QR(n_e)))
                            act(o2, PS[b][:, :], AF.Copy, [f"ps{b}", "TB"], [f"TMPF{f}"], scale=sc(QI(n_e)))

                        def bu_sacc(s_):
                            f = s_ % 2
                            o1, o2 = BW[:, f, 0, :], BW[:, f, 1, :]
                            rd = [f"TMPF{f}", "IDB", "NIDB"]
                            mm(PS[7][:, 0:NCH], IDB[:, :], o1[:, 0:NCH], s_ == 0, False, rd, 7)
                            mm(PS[7][:, 0:NCH], NIDB[:, :], o2[:, 256:256 + NCH], False, False, rd, 7)
                            mm(PS[7][:, 256:256 + NCH], IDB[:, :], o2[:, 0:NCH], False, False, rd, 7)
                            mm(PS[7][:, 256:256 + NCH], IDB[:, :], o1[:, 256:256 + NCH], False, s_ == T0C - 1, rd, 7)

                        def bu_step(rhs, ncol, dst_re, dst_im, n_e, first, ures):
                            b = 6
                            p_re, p_im = PS[b][:, 0:ncol], PS[b][:, 256:256 + ncol]
                            mm(p_re, bre, rhs, True, True, [rwa] + ures, b)
                            mm(p_im, bim, rhs, True, True, [rwa] + ures, b)
                            rb = [f"ps{b}", "TB", "SACC"]
                            if first:
                                ts_op("dve", dst_re, p_re, sc(QR(n_e)), ALU.mult, rb, ["SACC"])
                                ts_op("dve", dst_im, p_re, sc(QI(n_e)), ALU.mult, rb, ["SACC"])
                            else:
                                stt("dve", dst_re, p_re, sc(QR(n_e)), dst_re, ALU.mult, ALU.add, rb, ["SACC"])
                                stt("dve", dst_im, p_re, sc(QI(n_e)), dst_im, ALU.mult, ALU.add, rb, ["SACC"])
                            stt("dve", dst_re, p_im, sc(NQI(n_e)), dst_re, ALU.mult, ALU.add, rb, ["SACC"])
                            stt("dve", dst_im, p_im, sc(QR(n_e)), dst_im, ALU.mult, ALU.add, rb, ["SACC"])

                        if stage == "B0":
                            bu_mm(0)
                            bu_mm(1)
                            bu_evac(0)
                            bu_evac(1)
                        else:
                            for s_ in range(T0C):
                                bu_sacc(s_)
                                if s_ + 2 < T0C:
                                    bu_mm(s_ + 2)
                                    bu_evac(s_ + 2)
                    elif stage == "SMP":
                        mm(PS[6][:, 0:NS_TOK], bre, U[:, j, tsl(4)], True, True, [rwa, f"U{j}_4"], 6)
                        mm(PS[6][:, 256:256 + NS_TOK], bim, U[:, j, tsl(4)], True, True, [rwa, f"U{j}_4"], 6)
                        copy("dve", SBS[:, :, :], PS[6][:, :].rearrange("p (a k) -> p a k", a=2)[:, :, 0:NS_TOK], ["ps6"], ["SBS"])
                        ss_re, ss_im = SSO[:, 0, pp, 1:17], SSO[:, 1, pp, 1:17]
                        b_re = SBS[:, 0, :].rearrange("p (t q) -> p t q", t=4)
                        b_im = SBS[:, 1, :].rearrange("p (t q) -> p t q", t=4)
                        P1, P2 = S1[:, 1792:1856], S1[:, 1856:1920]
                        p1v = P1.rearrange("p (t q) -> p t q", t=4)
                        p2v = P2.rearrange("p (t q) -> p t q", t=4)

                        def tabr(i0_):
                            return TB[:, i0_:i0_ + 4, pi_].unsqueeze(2).broadcast_to([128, 4, NSEQ])

                        rb = ["SBS", "TB", "S1_3"]
                        for dst, ta, tb2 in ((ss_re, RQR, RNQI), (ss_im, RQI, RQR)):
                            tt_op("dve", p1v, b_re, tabr(ta), ALU.mult, rb, ["S1_3"])
                            tt_op("dve", p2v, b_im, tabr(tb2), ALU.mult, rb, ["S1_3"])
                            tt_op("dve", P1, P1, P2, ALU.add, ["S1_3"], ["S1_3"])
                            P.op("dve", lambda e, dst=dst: e.tensor_reduce(
                                out=dst, in_=P1.rearrange("p (t q) -> p q t", t=4), axis=mybir.AxisListType.X, op=ALU.add),
                                reads=["S1_3"], writes=["SSO"])
                        stt("dve", ss_re, h0r, sc(ER(4)), ss_re, ALU.mult, ALU.add, ["H0", "TB", "SSO"], ["SSO"])
                        stt("dve", ss_re, h0i, sc(NEI(4)), ss_re, ALU.mult, ALU.add, ["H0", "TB", "SSO"], ["SSO"])
                        stt("dve", ss_im, h0r, sc(EI(4)), ss_im, ALU.mult, ALU.add, ["H0", "TB", "SSO"], ["SSO"])
                        stt("dve", ss_im, h0i, sc(ER(4)), ss_im, ALU.mult, ALU.add, ["H0", "TB", "SSO"], ["SSO"])
                    elif stage == "ROTC":
                        copy("dve", SACC[:, :, 0:NCH], PS[7][:, :].rearrange("p (a k) -> p a k", a=2)[:, :, 0:NCH], ["ps7"], ["SACC"])
                    elif stage == "ROT":
                        s_re, s_im = SACC[:, 0, 0:NCH], SACC[:, 1, 0:NCH]
                        tt_op("dve", DRE, WCR, s_re, ALU.mult, rWC + ["SACC"], ["S1_0"])
                        tt_op("dve", TMP, WCI, s_im, ALU.mult, rWC + ["SACC"], ["S1_1"])
                        tt_op("dve", DRE, DRE, TMP, ALU.add, ["S1_0", "S1_1"], ["S1_0"])
                        tt_op("dve", DIM_, WCR, s_im, ALU.mult, rWC + ["SACC"], ["S1_0"])
                        tt_op("dve", TMP, WCI, s_re, ALU.mult, rWC + ["SACC"], ["S1_1"])
                        tt_op("dve", DIM_, DIM_, TMP, ALU.subtract, ["S1_0", "S1_1"], ["S1_0"])
                    elif stage == "SCAN":
                        act(RT[:, :], WCR, AF.Identity, rWC + ["TB"], ["RT"], bias=sc(RT0), scale=0.0)
                        P.op("dve", lambda e: e.tensor_tensor_scan(out=GRE, data0=RT[:, :], data1=DRE, initial=0.0,
                                                                   op0=ALU.mult, op1=ALU.add),
                             reads=["RT", "S1_0"], writes=["S1_1"])
                        P.op("dve", lambda e: e.tensor_tensor_scan(out=GIM, data0=RT[:, :], data1=DIM_, initial=0.0,
                                                                   op0=ALU.mult, op1=ALU.add),
                             reads=["RT", "S1_0"], writes=["S1_2"])
                        e_ = NCH - 1
                        tt_op("dve", CAR[:, 2:3], WCI[:, e_:e_ + 1], GIM[:, e_:e_ + 1], ALU.mult, rWC + ["S1_2", "CAR"], ["CAR"])
                        stt("dve", SSO[:, 0, pp, 0:1], WCR[:, e_:e_ + 1], GRE[:, e_:e_ + 1], CAR[:, 2:3], ALU.mult, ALU.subtract,
                            rWC + ["S1_1", "CAR"], ["SSO"])
                        tt_op("dve", CAR[:, 3:4], WCR[:, e_:e_ + 1], GIM[:, e_:e_ + 1], ALU.mult, rWC + ["S1_2", "CAR"], ["CAR"])
                        stt("dve", SSO[:, 1, pp, 0:1], WCI[:, e_:e_ + 1], GRE[:, e_:e_ + 1], CAR[:, 3:4], ALU.mult, ALU.add,
                            rWC + ["S1_1", "CAR"], ["SSO"])
                        tt_op("dve", DRE, WCR, GRE, ALU.mult, rWC + ["S1_1"], ["S1_0"])
                        tt_op("dve", TMP, WCI, GIM, ALU.mult, rWC + ["S1_2"], ["S1_1"])
                        tt_op("dve", HP[:, 0, 1:NCH + 1], DRE, TMP, ALU.subtract, ["S1_0", "S1_1"], ["HP"])
                        tt_op("dve", DIM_, WCI, GRE, ALU.mult, rWC + ["S1_1"], ["S1_0"])
                        tt_op("dve", TMP, WCR, GIM, ALU.mult, rWC + ["S1_2"], ["S1_1"])
                        tt_op("dve", HP[:, 1, 1:NCH + 1], DIM_, TMP, ALU.add, ["S1_0", "S1_1"], ["HP"])
                    elif stage == "C":
                        copy("dve", H0B[:, 0, :], h0r, ["H0"], ["H0B"])
                        copy("dve", H0B[:, 1, :], h0i, ["H0"], ["H0B"])
                        if 'c' in DBG:
                            return
                        w0 = 32 * pp
                        wz = 32 * ((pp - 1) % 4)
                        P.op("pool", lambda e, wz=wz: e.memset(LC8[:, :, :, wz:wz + 32], 0.0), writes=["LC8"])
                        coef8(LC8[:, :, :, w0:w0 + 32], "LC8", cre[:, w0:w0 + 32], cim[:, w0:w0 + 32], rwa,
                              ER(1), EI(1), NEI(1), pi_)
                        for jj in range(T0C):
                            fin = last and jj == T0C - 1
                            for tt in range(4):
                                pv = PS[tt][:, jj * CPT:(jj + 1) * CPT]
                                mm(pv, LC8[:, 0, jj, :], HP[:, 0, tt * CPT:(tt + 1) * CPT], False, False, ["LC8", "HP"], tt)
                                mm(pv, LC8[:, 1, jj, :], HP[:, 1, tt * CPT:(tt + 1) * CPT], False, fin, ["LC8", "HP"], tt)
                            if jj < 4:
                                pv = PS[4][:, jj * NSEQ:(jj + 1) * NSEQ]
                                mm(pv, LC8[:, 0, jj, :], H0B[:, 0, :], False, False, ["LC8", "H0B"], 4)
                                mm(pv, LC8[:, 1, jj, :], H0B[:, 1, :], False, last and jj == 3, ["LC8", "H0B"], 4)

                pair_stage(0, "B0")
                pair_stage(0, "B1")
                pair_stage(1, "B0")
                wb2 = ws_get(keep=1)
                rwb = f"wbf{wb2}"
                SCR = S1[:, 1280:1792].rearrange("p (a k) -> p a k", a=4)
                copy("dve", WCv[:, :, 0, 0:1], PW[:, 0, LG, 4 * j:4 * j + 4].unsqueeze(2), ["PW"], rWC)
                copy("dve", WCv[:, :, 1, 0:1], PW[:, 1, LG, 4 * j:4 * j + 4].unsqueeze(2), ["PW"], rWC)
                lv = 0
                while (1 << lv) < NCH and 'd' not in DBG:
                    m_ = 1 << lv
                    prb = PW[:, 0, LG + lv, 4 * j:4 * j + 4].unsqueeze(2).broadcast_to([128, 4, m_])
                    pib = PW[:, 1, LG + lv, 4 * j:4 * j + 4].unsqueeze(2).broadcast_to([128, 4, m_])
                    sR, sI = WCv[:, :, 0, 0:m_], WCv[:, :, 1, 0:m_]
                    dR, dI = WCv[:, :, 0, m_:2 * m_], WCv[:, :, 1, m_:2 * m_]
                    tt_op("dve", SCR[:, :, 0:m_], sI, pib, ALU.mult, rWC + ["PW"], rT8)
                    tt_op("dve", dR, sR, prb, ALU.mult, rWC + ["PW"], rWC)
                    tt_op("dve", dR, dR, SCR[:, :, 0:m_], ALU.subtract, rWC + rT8, rWC)
                    tt_op("dve", SCR[:, :, 0:m_], sR, pib, ALU.mult, rWC + ["PW"], rT8)
                    tt_op("dve", dI, sI, prb, ALU.mult, rWC + ["PW"], rWC)
                    tt_op("dve", dI, dI, SCR[:, :, 0:m_], ALU.add, rWC + rT8, rWC)
                    lv += 1
                for pp in range(4 if 't' not in DBG else 0):
                    pi_ = 4 * j + pp
                    w0 = 32 * pp
                    coef8(LC8[:, :, :, 0:32], "LC8", blk(pp, 2)[:, w0:w0 + 32], blk(pp, 3)[:, w0:w0 + 32], rwa, QR(0), QI(0), NQI(0), pi_)
                    for tau in range(T0C):
                        pst = PS[tau // 4][:, (tau % 4) * 128 + w0:(tau % 4) * 128 + w0 + 32]
                        mm(pst, blkT(pp, 0), LC8[:, 0, tau, 0:32], True, False, [rwb, "LC8"], tau // 4)
                        mm(pst, blkT(pp, 1), LC8[:, 1, tau, 0:32], False, True, [rwb, "LC8"], tau // 4)
                for tau in range(T0C if 't' not in DBG else 0):
                    pst = PS[tau // 4][:, (tau % 4) * 128:(tau % 4 + 1) * 128]
                    act(TAP[:, tau, :], pst, AF.Copy, [f"ps{tau // 4}"], [f"TAP{tau}"])
                    if tau == 0:
                        stt("dve", TAP[:, 0, :], IDB[:, :], VEC[:, vb + 28 + j:vb + 29 + j], TAP[:, 0, :], ALU.mult, ALU.add,
                            ["IDB", VR, "TAP0"], ["TAP0"])
                for tt in range(5):
                    t0, n = TT[tt]
                    ti = T0C if tt < 4 else 4
                    cw = n // ti
                    for tau in range(ti):
                        mm(PS[tt][:, tau * cw:n], TAP[:, tau, :], U[:, j, t0:t0 + (ti - tau) * cw], tau == 0, False,
                           [f"TAP{tau}", f"U{j}_{tt}"], tt)

                pair_stage(0, "ROTC")
                pair_stage(0, "SMP")
                pair_stage(0, "ROT")
                for pp in range(4):
                    if pp < 3:
                        pair_stage(pp + 1, "B1")
                    pair_stage(pp, "SCAN")
                    if pp < 2:
                        pair_stage(pp + 2, "B0")
                    pair_stage(pp, "C")
                    if pp == 1:
                        ws_emit_pending()
                    if pp < 3:
                        pair_stage(pp + 1, "ROTC")
                        pair_stage(pp + 1, "SMP")
                        pair_stage(pp + 1, "ROT")
                for k_ in range(2):
                    dma_io(sso_d[l, :, k_, 4 * j:4 * j + 4, :], SSO[:, k_, :, :], ["SSO"], [])
                for tt in range(5):
                    n = TT[tt][1]
                    ti = T0C if tt < 4 else 4
                    act(A[:, j, tsl(tt)].rearrange("p (k s) -> p k s", s=ti),
                        PS[tt][:, 0:n].rearrange("p (s k) -> p k s", s=ti), AF.Gelu_apprx_tanh,
                        [f"ps{tt}"], [f"A{j}_{tt}"])
            bank_ctr[0] = 0

            wb = ws_get()
            wvw = wv(wb, 512)
            for m in range(4):
                bias = VEC[:, vb + 32 + m:vb + 33 + m]
                for tt in range(5):
                    b, ps = dense_group(lambda kt: wvw[:, kt, m * 128:(m + 1) * 128],
                                        lambda kt, tt: A[:, kt, tsl(tt)], 4, tt,
                                        lambda kt, tt: [f"wbf{wb}", f"A{kt}_{tt}"])
                    act(S1[:, tsl(tt)], ps, AF.Sigmoid, [f"ps{b}", VR], [f"S1_{tt}"], bias=bias)
                    tt_op("dve", U[:, m, tsl(tt)], A[:, m, tsl(tt)], S1[:, tsl(tt)], ALU.mult,
                          [f"A{m}_{tt}", f"S1_{tt}"], [f"U{m}_{tt}"])

            for j in range(4):
                banks = [next_bank() for _ in range(5)]
                for k in range(31):
                    d = k % 4
                    ts_op("dve", DG[:, d, :], IDB[:, :], VEC[:, vb + 48 + j * 31 + k:vb + 49 + j * 31 + k], ALU.mult,
                          ["IDB", VR], [f"DG{d}"])
                    for tt in range(4):
                        t0 = TT[tt][0]
                        rd = [f"DG{d}", f"Z{j}_{tt}", f"Zpad{j}"] + ([f"Z{j}_{tt - 1}"] if tt > 0 else [])
                        mm(PS[banks[tt]][:, :], DG[:, d, :], Z[:, j, t0 + k:t0 + k + 512], k == 0, k == 30, rd, banks[tt])
                    mm(v3(PS[banks[4]][:, 0:64], 4), DG[:, d, :], Zs(j)[:, :, k:k + 4], k == 0, k == 30,
                       [f"DG{d}", f"Z{j}_4"], banks[4])
                bias = VEC[:, vb + 36 + j:vb + 37 + j]
                for tt in range(5):
                    n = TT[tt][1]
                    act(A[:, 4 + j, tsl(tt)], PS[banks[tt]][:, 0:n], AF.Identity, [f"ps{banks[tt]}", VR],
                        [f"A{4 + j}_{tt}"], bias=bias)
            bank_ctr[0] = 0
            for tt in range(5):
                n = TT[tt][1]
                b, ps = dense_group(lambda kt: ONB[:, :], lambda kt, tt: A[:, 4 + kt, tsl(tt)], 4, tt,
                                    lambda kt, tt: ["ONB", f"A{4 + kt}_{tt}"])
                act(S1[:, tsl(tt)], ps, AF.Copy, [f"ps{b}"], [f"S1_{tt}"], scale=1.0 / 512)
            for j in range(4):
                for tt in range(5):
                    act(Z[:, j, tsl(tt)], A[:, 4 + j, tsl(tt)], AF.Square, [f"A{4 + j}_{tt}"], Zrow(j) + [f"Zpad{j}"])
            for tt in range(5):
                b, ps = dense_group(lambda kt: ONB[:, :], lambda kt, tt: Z[:, kt, tsl(tt)], 4, tt,
                                    lambda kt, tt: ["ONB"] + Zrow(kt))
                tt_op("dve", S2[:, tsl(tt)], S1[:, tsl(tt)], S1[:, tsl(tt)], ALU.mult, [f"S1_{tt}"], [f"S2_{tt}"])
                stt("dve", S2[:, tsl(tt)], ps, 1.0 / 512, S2[:, tsl(tt)], ALU.mult, ALU.subtract,
                    [f"ps{b}", f"S2_{tt}"], [f"S2_{tt}"])
                act(S2[:, tsl(tt)], S2[:, tsl(tt)], AF.Ln, [f"S2_{tt}"], [f"S2_{tt}"], bias=EPS)
                act(S2[:, tsl(tt)], S2[:, tsl(tt)], AF.Exp, [f"S2_{tt}"], [f"S2_{tt}"], scale=-0.5)
            tf = [0]
            for j in range(4):
                lg, lb = VEC[:, vb + 40 + j:vb + 41 + j], VEC[:, vb + 44 + j:vb + 45 + j]
                for tt in range(5):
                    n = TT[tt][1]
                    f = tf[0] % 2
                    tf[0] += 1
                    tmp = TMPF[:, f, 0:n]
                    tt_op("dve", tmp, A[:, 4 + j, tsl(tt)], S1[:, tsl(tt)], ALU.subtract,
                          [f"A{4 + j}_{tt}", f"S1_{tt}"], [f"TMPF{f}"])
                    tt_op("dve", tmp, tmp, S2[:, tsl(tt)], ALU.mult, [f"TMPF{f}", f"S2_{tt}"], [f"TMPF{f}"])
                    act(A[:, 4 + j, tsl(tt)], tmp, AF.Silu, [f"TMPF{f}", VR], [f"A{4 + j}_{tt}"], bias=lb, scale=lg)

            for m in range(8):
                if m % 2 == 0:
                    wb = ws_get()
                wvw = wv(wb, 256)
                mc = slice((m % 2) * 128, (m % 2) * 128 + 128)
                for tt in range(5):
                    b, ps = dense_group(lambda kt: wvw[:, kt, mc],
                                        lambda kt, tt: (U[:, kt, tsl(tt)] if kt < 4 else A[:, kt, tsl(tt)]), 8, tt,
                                        lambda kt, tt: [f"wbf{wb}", (f"U{kt}_{tt}" if kt < 4 else f"A{kt}_{tt}")])
                    tt_op("dve", X[:, m, tsl(tt)], X[:, m, tsl(tt)], ps, ALU.add, [f"X{m}_{tt}", f"ps{b}"], [f"X{m}_{tt}"])

            rmsnorm(lambda kt: VEC[:, 8 + kt:9 + kt], VR, False)
            bank_ctr[0] = 0

            def Hh(m, tt):
                return U[:, m, tsl(tt)] if m < 4 else Z[:, m - 4, tsl(tt)]

            def Hr(m, tt):
                return [f"U{m}_{tt}"] if m < 4 else Zrow(m - 4) + [f"Zpad{m - 4}"]

            for fb in range(4):
                if fb == 0 and l + 1 < nl:
                    P.capture = []
                    s5_tables(l + 1)
                    tblq = P.capture
                    P.capture = None
                for m in range(8):
                    P.flush(tblq, 6)
                    if m % 2 == 0:
                        wb = ws_get()
                    wvw = wv(wb, 256)
                    mc = slice((m % 2) * 128, (m % 2) * 128 + 128)
                    for tt in range(5):
                        n = TT[tt][1]
                        b, ps = dense_group(lambda kt: wvw[:, kt, mc], lambda kt, tt: A[:, kt, tsl(tt)], 8, tt,
                                            lambda kt, tt: [f"wbf{wb}", f"A{kt}_{tt}"])
                        f = tf[0] % 2
                        tf[0] += 1
                        tmp = TMPF[:, f, 0:n]
                        act(tmp, ps, AF.Relu, [f"ps{b}"], [f"TMPF{f}"])
                        act(Hh(m, tt), tmp, AF.Square, [f"TMPF{f}"], Hr(m, tt))
                for m in range(8):
                    P.flush(tblq, 6)
                    if m % 2 == 0:
                        wb = ws_get()
                    wvw = wv(wb, 256)
                    mc = slice((m % 2) * 128, (m % 2) * 128 + 128)
                    for tt in range(5):
                        b, ps = dense_group(lambda kt: wvw[:, kt, mc], lambda kt, tt: Hh(kt, tt), 8, tt,
                                            lambda kt, tt: [f"wbf{wb}"] + Hr(kt, tt))
                        tt_op("dve", X[:, m, tsl(tt)], X[:, m, tsl(tt)], ps, ALU.add,
                              [f"X{m}_{tt}", f"ps{b}"], [f"X{m}_{tt}"])
            P.flush(tblq, 10 ** 6)
            for j in range(4):
                P.op("pool", lambda e, j=j: e.memset(Z[:, j, 0:30], 0.0), writes=Zrow(j) + [f"Zpad{j}"])
            bank_ctr[0] = 0

        rmsnorm(lambda kt: GF[:, kt:kt + 1], "GF", True)
        for kt in range(8):
            dma_io(yT[kt, :, :], X[:, kt, :], [f"X{kt}_{t}" for t in range(5)], [])

        P.emit(block, sems)
    return nc


def _prep_shared(inp):
    f = np.float32
    nl = DEPTH
    order = list(range(0, 512))
    for j in range(4):
        order += list(range(1024 + 128 * j, 1024 + 128 * (j + 1)))
        order += list(range(512 + 128 * j, 512 + 128 * (j + 1)))
    order = np.array(order)
    w_in = np.ascontiguousarray(inp["w_in"][:, :, order])
    b_in = inp["b_in"][:, order]
    vec = np.zeros((128, NV), f)

    def cols(v, n):
        return np.asarray(v, f).reshape(n, 128).T

    for l in range(nl):
        b = l * NVL
        vec[:, b + 0:b + 8] = cols(inp["norm_mix_g"][l], 8)
        vec[:, b + 8:b + 16] = cols(inp["norm_mlp_g"][l], 8)
        vec[:, b + 16:b + 28] = cols(b_in[l], 12)
        vec[:, b + 28:b + 32] = cols(inp["ssm_d"][l], 4)
        vec[:, b + 32:b + 36] = cols(inp["b_glu"][l], 4)
        vec[:, b + 36:b + 40] = cols(inp["conv_b"][l], 4)
        vec[:, b + 40:b + 44] = cols(inp["conv_ln_g"][l], 4)
        vec[:, b + 44:b + 48] = cols(inp["conv_ln_b"][l], 4)
        cw = np.asarray(inp["conv_w"][l], f)
        vec[:, b + 48:b + 172] = cw.reshape(31, 4, 128).transpose(2, 1, 0).reshape(128, 124)
    vec[:, NVL * nl:NVL * nl + 8] = cols(inp["norm_f_g"], 8)

    s5p = np.zeros((128, nl * 48), f)
    for l in range(nl):
        for nm, off in (("ssm_a_re", 0), ("ssm_a_im", 16)):
            a = np.asarray(inp[nm][l], f).reshape(16, 2, 64)
            s5p[:, l * 48 + off:l * 48 + off + 16] = a.transpose(1, 2, 0).reshape(128, 16)
        ld = np.asarray(inp["ssm_log_dt"][l], f).reshape(16, 2)
        s5p[:, l * 48 + 32:l * 48 + 48] = np.repeat(ld.T[:, None, :], 64, axis=1).reshape(128, 16)

    s5w = np.zeros((nl, 4, 128, 2048), f)
    for l in range(nl):
        for pi in range(16):
            j, pp = pi // 4, pi % 4
            for x in range(2):
                g = 2 * pi + x
                gl = 2 * pp + x
                rows_gc = slice(gl * 16, gl * 16 + 16)
                cols_xp = slice(x * 64, x * 64 + 64)
                base = pp * 512
                s5w[l, j, rows_gc, base + 0 + x * 64:base + 0 + x * 64 + 64] = inp["ssm_b_re"][l, g].T
                s5w[l, j, rows_gc, base + 128 + x * 64:base + 128 + x * 64 + 64] = inp["ssm_b_im"][l, g].T
                s5w[l, j, cols_xp, base + 256 + gl * 16:base + 256 + gl * 16 + 16] = inp["ssm_c_re"][l, g].T
                s5w[l, j, cols_xp, base + 384 + gl * 16:base + 384 + gl * 16 + 16] = inp["ssm_c_im"][l, g].T
    s5wT = np.zeros((nl, 4, 128, 1024), f)
    for l in range(nl):
        for pi in range(16):
            j, pp = pi // 4, pi % 4
            for x in range(2):
                g = 2 * pi + x
                gl = 2 * pp + x
                s5wT[l, j, x * 64:x * 64 + 64, pp * 256 + gl * 16:pp * 256 + gl * 16 + 16] = inp["ssm_b_re"][l, g]
                s5wT[l, j, x * 64:x * 64 + 64, pp * 256 + 128 + gl * 16:pp * 256 + 128 + gl * 16 + 16] = inp["ssm_b_im"][l, g]
    return dict(s5wT=s5wT, w_in=w_in, w_glu=np.ascontiguousarray(inp["w_glu"], f), w_out=np.ascontiguousarray(inp["w_out"], f),
                w_up=np.ascontiguousarray(inp["w_up"], f), w_down=np.ascontiguousarray(inp["w_down"], f),
                s5w=s5w, vec=vec, s5p=s5p, ident=np.eye(128, dtype=f))


def _prep_core(inp, c):
    f = np.float32
    xa = np.concatenate([inp["x_prompt"][c], inp["x_sample"][NSEQ * c:NSEQ * (c + 1)].reshape(NS_TOK, D)], axis=0)
    xT = np.ascontiguousarray(xa.T.reshape(8, 128, NTOK), f)
    sl = slice(NSEQ * c, NSEQ * (c + 1))
    h0 = np.zeros((DEPTH, 128, 2, 16, 16), f)
    for k, nm in enumerate(("state_ssm_re", "state_ssm_im")):
        s = np.asarray(inp[nm][:, sl], f).reshape(DEPTH, NSEQ, 16, 2, 64)
        h0[:, :, k] = s.transpose(0, 3, 4, 2, 1).reshape(DEPTH, 128, 16, NSEQ)
    sc = np.asarray(inp["state_conv"][:, sl], f)
    zbuf = sc.reshape(DEPTH, NSEQ, 30, 4, 128).transpose(0, 4, 3, 1, 2)
    return dict(xT=xT, h0=np.ascontiguousarray(h0),
                zbuf=np.ascontiguousarray(zbuf.reshape(DEPTH, 128, 1920)), sconv=np.ascontiguousarray(sc))


_NC_CACHE = {}


def run_device(inp, nl=DEPTH):
    if nl not in _NC_CACHE:
        _NC_CACHE[nl] = build(nl)
    nc = _NC_CACHE[nl]
    shared = _prep_shared(inp)
    in_maps = []
    for c in range(NCORES):
        m = dict(shared)
        m.update(_prep_core(inp, c))
        in_maps.append(m)
    res = run_bass_kernel_spmd(nc, in_maps, core_ids=list(range(NCORES)))
    return res.results


def assemble(results, nl=DEPTH):
    f = np.float32
    y_p = np.zeros((NCORES, NP_TOK, D), f)
    y_s = np.zeros((NCORES * NSEQ, 4, D), f)
    re_p = np.zeros((nl, NCORES, 32, 64), f)
    im_p = np.zeros((nl, NCORES, 32, 64), f)
    cv_p = np.zeros((nl, NCORES, 30, 512), f)
    re_s = np.zeros((nl, NCORES * NSEQ, 32, 64), f)
    im_s = np.zeros((nl, NCORES * NSEQ, 32, 64), f)
    cv_s = np.zeros((nl, NCORES * NSEQ, 30, 512), f)
    for c, r in enumerate(results):
        y = np.asarray(r["yT"]).reshape(D, NTOK).T
        y_p[c] = y[:NP_TOK]
        y_s[NSEQ * c:NSEQ * (c + 1)] = y[NP_TOK:].reshape(NSEQ, 4, D)
        sso = np.asarray(r["sso"])[:nl].reshape(nl, 2, 64, 2, 16, 17)
        st = sso.transpose(0, 3, 5, 4, 1, 2).reshape(nl, 2, 17, 32, 64)
        re_p[:, c], im_p[:, c] = st[:, 0, 0], st[:, 1, 0]
        re_s[:, NSEQ * c:NSEQ * (c + 1)] = st[:, 0, 1:]
        im_s[:, NSEQ * c:NSEQ * (c + 1)] = st[:, 1, 1:]
        cvp = np.asarray(r["cvp"])[:nl].reshape(nl, 128, 4, 30)
        cv_p[:, c] = cvp.transpose(0, 3, 2, 1).reshape(nl, 30, 512)
        cvs = np.asarray(r["cvs"])[:nl].reshape(nl, 128, 4, NSEQ, 4)
        zs = cvs.transpose(0, 3, 4, 2, 1).reshape(nl, NSEQ, 4, 512)
        cvc = np.asarray(r["cvc"])[:nl]
        cv_s[:, NSEQ * c:NSEQ * (c + 1)] = np.concatenate([cvc, zs], axis=2)
    return (y_p, y_s, re_p, im_p, cv_p, re_s, im_s, cv_s)


def kernel(**inputs):
    inp = {k: np.asarray(v) for k, v in inputs.items()}
    return assemble(run_device(inp, DEPTH), DEPTH)
```
